# Optimizing a Trainium2 kernel written in Bass

```python
import jax, jax.numpy as jnp
from jax import lax
import numpy as np

D_MODEL = 1024
BATCH = 16
SEQ = 2048
DEPTH = 2

CHUNK = 64
Q_BLOCK = 128
N_MIXERS = 2
N_RET_LAYERS = (DEPTH + 1) // 2
N_MLA_LAYERS = DEPTH // 2
RMS_EPS = 1e-6
ROPE_THETA = 10000.0

RET_HEADS = D_MODEL // 256
RET_QK_DIM = 256
RET_V_DIM = 2 * D_MODEL // RET_HEADS
RET_GAMMA_BASE = -5.0

MLA_HEADS = D_MODEL // 128
MLA_Q_RANK = 384
MLA_KV_RANK = D_MODEL // 4
MLA_NOPE_DIM = 128
MLA_ROPE_DIM = 64
MLA_V_DIM = 128
MLA_QK_DIM = MLA_NOPE_DIM + MLA_ROPE_DIM
MASK_VALUE = -1e30

FFN_DIM = 2816
CONV_WIDTH = 3

kernel_name = "hybrid_retention_mla_convffn_trunk"


def rms_norm(x, gain):
    xf = x.astype(jnp.float32)
    y = xf * lax.rsqrt(jnp.mean(xf * xf, axis=-1, keepdims=True) + RMS_EPS)
    return (y * gain.astype(jnp.float32)).astype(x.dtype)


def rope(x, pos):
    half = x.shape[-1] // 2
    inv_freq = ROPE_THETA ** (-jnp.arange(half, dtype=jnp.float32) / half)
    ang = pos.astype(jnp.float32)[:, None] * inv_freq[None, :]
    cos = jnp.cos(ang)[None, :, None, :]
    sin = jnp.sin(ang)[None, :, None, :]
    xf = x.astype(jnp.float32)
    x1, x2 = xf[..., :half], xf[..., half:]
    return jnp.concatenate([x1 * cos - x2 * sin, x2 * cos + x1 * sin], axis=-1).astype(x.dtype)


def retention_mixer(h, w_in, gn_gain, w_out):
    B, S, _ = h.shape
    H, dk, dv = RET_HEADS, RET_QK_DIM, RET_V_DIM
    proj = h @ w_in
    q, k, v, g = jnp.split(proj, [H * dk, 2 * H * dk, 2 * H * dk + H * dv], axis=-1)
    pos = jnp.arange(S)
    q = rope(q.reshape(B, S, H, dk), pos)
    k = rope(k.reshape(B, S, H, dk), pos) * (dk ** -0.5)
    v = v.reshape(B, S, H, dv)
    n_chunks = S // CHUNK

    def to_chunks(t):
        return t.reshape(B, n_chunks, CHUNK, H, t.shape[-1]).transpose(1, 0, 3, 2, 4)

    log_gamma = jnp.log1p(-jnp.exp2(RET_GAMMA_BASE - jnp.arange(H, dtype=jnp.float32)))
    idx = jnp.arange(CHUNK, dtype=jnp.float32)
    intra_decay = jnp.exp(log_gamma[:, None, None] * jnp.abs(idx[:, None] - idx[None, :]))
    q_decay = jnp.exp(log_gamma[:, None] * (idx + 1.0))[:, :, None]
    k_decay = jnp.exp(log_gamma[:, None] * (CHUNK - 1.0 - idx))[:, :, None]
    chunk_decay = jnp.exp(log_gamma * CHUNK)[:, None, None]

    def step(state, qkv):
        qc, kc, vc = qkv
        scores = jnp.einsum('bhid,bhjd->bhij', qc, kc) * intra_decay
        inner = jnp.einsum('bhij,bhjv->bhiv', scores, vc)
        cross = jnp.einsum('bhid,bhdv->bhiv', qc * q_decay, state)
        state = state * chunk_decay + jnp.einsum('bhjd,bhjv->bhdv', kc * k_decay, vc)
        return state, inner + cross

    state0 = jnp.zeros((B, H, dk, dv), jnp.float32)
    _, out = lax.scan(step, state0, (to_chunks(q), to_chunks(k), to_chunks(v)))
    out = out.transpose(1, 0, 3, 2, 4).reshape(B, S, H, dv)
    out = rms_norm(out, gn_gain).astype(h.dtype)
    out = out.reshape(B, S, H * dv) * jax.nn.silu(g)
    return out @ w_out


def mla_mixer(h, w_in, q_norm_g, w_qb, kv_norm_g, w_kvb, q_head_g, k_head_g, w_out):
    B, S, _ = h.shape
    H = MLA_HEADS
    proj = h @ w_in
    c_q, c_kv, k_rope = jnp.split(proj, [MLA_Q_RANK, MLA_Q_RANK + MLA_KV_RANK], axis=-1)
    q = (rms_norm(c_q, q_norm_g) @ w_qb).reshape(B, S, H, MLA_QK_DIM)
    kv = (rms_norm(c_kv, kv_norm_g) @ w_kvb).reshape(B, S, H, MLA_NOPE_DIM + MLA_V_DIM)
    k_nope, v = kv[..., :MLA_NOPE_DIM], kv[..., MLA_NOPE_DIM:]
    k_rope = jnp.broadcast_to(k_rope[:, :, None, :], (B, S, H, MLA_ROPE_DIM))
    k = jnp.concatenate([k_nope, k_rope], axis=-1)
    q = rms_norm(q, q_head_g)
    k = rms_norm(k, k_head_g)
    pos = jnp.arange(S)
    q = jnp.concatenate([q[..., :MLA_NOPE_DIM], rope(q[..., MLA_NOPE_DIM:], pos)], axis=-1)
    k = jnp.concatenate([k[..., :MLA_NOPE_DIM], rope(k[..., MLA_NOPE_DIM:], pos)], axis=-1)
    q = q * (MLA_QK_DIM ** -0.5)
    chunk_id = jnp.arange(S) // CHUNK
    outs = []
    for blk in range(S // Q_BLOCK):
        start, stop = blk * Q_BLOCK, (blk + 1) * Q_BLOCK
        logits = jnp.einsum('bqhd,bkhd->bhqk', q[:, start:stop], k[:, :stop]).astype(jnp.float32)
        mask = chunk_id[None, :stop] <= chunk_id[start:stop, None]
        logits = jnp.where(mask, logits, MASK_VALUE)
        p = jax.nn.softmax(logits, axis=-1).astype(v.dtype)
        outs.append(jnp.einsum('bhqk,bkhd->bqhd', p, v[:, :stop]))
    o = jnp.concatenate(outs, axis=1).reshape(B, S, H * MLA_V_DIM)
    return o @ w_out


def conv_ffn(h, w_in, conv_w, conv_b, w_out):
    a, g = jnp.split(h @ w_in, 2, axis=-1)
    g = lax.conv_general_dilated(
        g, conv_w[:, None, :], window_strides=(1,), padding=[(CONV_WIDTH - 1, 0)],
        dimension_numbers=('NWC', 'WIO', 'NWC'), feature_group_count=FFN_DIM) + conv_b
    return (jax.nn.silu(g) * a) @ w_out


def setup_inputs(seed: int = 0) -> dict:
    key = jax.random.key(seed)
    ks = jax.random.split(key, 20)

    def dense(k, lead, fan_in, fan_out):
        return jax.random.normal(k, (lead, fan_in, fan_out), jnp.float32) * (fan_in ** -0.5)

    def gain(k, shape):
        return 1.0 + 0.01 * jax.random.normal(k, shape, jnp.float32)

    R, M, L = N_RET_LAYERS, N_MLA_LAYERS, DEPTH
    ret_in_width = 2 * RET_HEADS * RET_QK_DIM + 2 * RET_HEADS * RET_V_DIM
    mla_in_width = MLA_Q_RANK + MLA_KV_RANK + MLA_ROPE_DIM
    return {
        "x": jax.random.normal(ks[0], (BATCH, SEQ, D_MODEL), jnp.float32),
        "ret_norm": gain(ks[1], (R, D_MODEL)),
        "ret_w_in": dense(ks[2], R, D_MODEL, ret_in_width),
        "ret_gn": gain(ks[3], (R, RET_HEADS, RET_V_DIM)),
        "ret_w_out": dense(ks[4], R, RET_HEADS * RET_V_DIM, D_MODEL),
        "mla_norm": gain(ks[5], (M, D_MODEL)),
        "mla_w_in": dense(ks[6], M, D_MODEL, mla_in_width),
        "mla_q_norm": gain(ks[7], (M, MLA_Q_RANK)),
        "mla_w_qb": dense(ks[8], M, MLA_Q_RANK, MLA_HEADS * MLA_QK_DIM),
        "mla_kv_norm": gain(ks[9], (M, MLA_KV_RANK)),
        "mla_w_kvb": dense(ks[10], M, MLA_KV_RANK, MLA_HEADS * (MLA_NOPE_DIM + MLA_V_DIM)),
        "mla_q_head_norm": gain(ks[11], (M, MLA_QK_DIM)),
        "mla_k_head_norm": gain(ks[12], (M, MLA_QK_DIM)),
        "mla_w_out": dense(ks[13], M, MLA_HEADS * MLA_V_DIM, D_MODEL),
        "ffn_norm": gain(ks[14], (L, D_MODEL)),
        "ffn_w_in": dense(ks[15], L, D_MODEL, 2 * FFN_DIM),
        "ffn_conv_w": jax.random.normal(ks[16], (L, CONV_WIDTH, FFN_DIM), jnp.float32) * (CONV_WIDTH ** -0.5),
        "ffn_conv_b": 0.01 * jax.random.normal(ks[17], (L, FFN_DIM), jnp.float32),
        "ffn_w_out": dense(ks[18], L, FFN_DIM, D_MODEL),
    }


def reference(x, ret_norm, ret_w_in, ret_gn, ret_w_out, mla_norm, mla_w_in, mla_q_norm, mla_w_qb,
              mla_kv_norm, mla_w_kvb, mla_q_head_norm, mla_k_head_norm, mla_w_out,
              ffn_norm, ffn_w_in, ffn_conv_w, ffn_conv_b, ffn_w_out):
    for i in range(DEPTH):
        j = i // N_MIXERS
        if i % N_MIXERS == 0:
            x = x + retention_mixer(rms_norm(x, ret_norm[j]), ret_w_in[j], ret_gn[j], ret_w_out[j])
        else:
            x = x + mla_mixer(rms_norm(x, mla_norm[j]), mla_w_in[j], mla_q_norm[j], mla_w_qb[j],
                              mla_kv_norm[j], mla_w_kvb[j], mla_q_head_norm[j], mla_k_head_norm[j],
                              mla_w_out[j])
        x = x + conv_ffn(rms_norm(x, ffn_norm[i]), ffn_w_in[i], ffn_conv_w[i], ffn_conv_b[i], ffn_w_out[i])
    return x
```

```python
import numpy as np
from contextlib import ExitStack, contextmanager
import concourse.bass as bass
import concourse.mybir as mybir
from concourse.bass_utils import run_bass_kernel_spmd

F32 = mybir.dt.float32
BF16 = mybir.dt.bfloat16
AF = mybir.ActivationFunctionType
ALU = mybir.AluOpType

D = 1024
S = 2048
NC8 = 8
EPS = 1e-6
THETA = 10000.0
FFN = 2816
NF = FFN // 128
RING = 4
SLOT = 2048


def _kt(w):
    K, C = w.shape
    return np.ascontiguousarray(w.reshape(K // 128, 128, C).transpose(1, 0, 2))


def _col(v):
    n = v.shape[0] // 128
    return np.ascontiguousarray(v.reshape(n, 128).T)


class WPack:
    def __init__(self):
        self.parts = []
        self.index = {}
        self.off = 0

    def add(self, name, arr3):
        a = np.ascontiguousarray(arr3, dtype=np.float32)
        assert a.shape[0] == 128
        n = int(np.prod(a.shape[1:]))
        assert n <= SLOT, (name, a.shape)
        self.index[name] = (self.off, tuple(a.shape[1:]))
        self.parts.append(a.reshape(-1))
        self.off += 128 * n


def weight_index():
    idx = {}
    off = 0

    def add(name, kc, cols):
        nonlocal off
        idx[name] = (off, (kc, cols))
        off += 128 * kc * cols
    for h in range(4):
        add('rq%d' % h, 8, 256); add('rk%d' % h, 8, 256)
        add('rv%da' % h, 8, 256); add('rv%db' % h, 8, 256)
        add('rg%da' % h, 8, 256); add('rg%db' % h, 8, 256)
        add('ro%da' % h, 4, 512); add('ro%db' % h, 4, 512)
    for l in range(2):
        for f in range(NF):
            add('f%di%d' % (l, f), 8, 256)
        for d in range(8):
            add('f%do%d_0' % (l, d), 11, 128); add('f%do%d_1' % (l, d), 11, 128)
    add('mi0', 8, 256); add('mi1', 8, 256); add('mi2', 8, 256)
    for h in range(8):
        add('mq%d' % h, 3, 256)
    add('mkn', 2, 1024); add('mv', 2, 1024)
    for h in range(8):
        add('mo%d' % h, 1, 1024)
    return idx, off


def pack_weights(inp):
    wp = WPack()
    rwi = inp['ret_w_in'][0]; rwo = inp['ret_w_out'][0]
    for h in range(4):
        wp.add('rq%d' % h, _kt(rwi[:, h * 256:(h + 1) * 256]))
        wp.add('rk%d' % h, _kt(rwi[:, 1024 + h * 256:1024 + (h + 1) * 256]))
        wp.add('rv%da' % h, _kt(rwi[:, 2048 + h * 512:2048 + h * 512 + 256]))
        wp.add('rv%db' % h, _kt(rwi[:, 2048 + h * 512 + 256:2048 + (h + 1) * 512]))
        wp.add('rg%da' % h, _kt(rwi[:, 4096 + h * 512:4096 + h * 512 + 256]))
        wp.add('rg%db' % h, _kt(rwi[:, 4096 + h * 512 + 256:4096 + (h + 1) * 512]))
        wp.add('ro%da' % h, _kt(rwo[h * 512:(h + 1) * 512, 0:512]))
        wp.add('ro%db' % h, _kt(rwo[h * 512:(h + 1) * 512, 512:1024]))
    for l in range(2):
        wi = inp['ffn_w_in'][l]; wo = inp['ffn_w_out'][l]
        for f in range(NF):
            wp.add('f%di%d' % (l, f), _kt(np.concatenate(
                [wi[:, f * 128:(f + 1) * 128], wi[:, FFN + f * 128:FFN + (f + 1) * 128]], axis=1)))
        for d in range(8):
            wp.add('f%do%d_0' % (l, d), _kt(wo[0:1408, d * 128:(d + 1) * 128]))
            wp.add('f%do%d_1' % (l, d), _kt(wo[1408:2816, d * 128:(d + 1) * 128]))
    mwi = inp['mla_w_in'][0]
    wp.add('mi0', _kt(mwi[:, 0:256])); wp.add('mi1', _kt(mwi[:, 256:512]))
    wp.add('mi2', _kt(np.concatenate([mwi[:, 512:704], np.zeros((1024, 64), np.float32)], axis=1)))
    wqb = inp['mla_w_qb'][0]
    for h in range(8):
        wp.add('mq%d' % h, _kt(np.concatenate([wqb[:, h * 192:(h + 1) * 192], np.zeros((384, 64), np.float32)], axis=1)))
    wkvb = inp['mla_w_kvb'][0].reshape(256, 8, 256)
    wp.add('mkn', _kt(np.ascontiguousarray(wkvb[:, :, 0:128]).reshape(256, 1024)))
    wp.add('mv', _kt(np.ascontiguousarray(wkvb[:, :, 128:256]).reshape(256, 1024)))
    mwo = inp['mla_w_out'][0]
    for h in range(8):
        wp.add('mo%d' % h, _kt(mwo[h * 128:(h + 1) * 128, :]))
    idx, tot = weight_index()
    assert tot == wp.off
    for k in idx:
        assert idx[k] == wp.index[k], k
    return np.concatenate(wp.parts)


P_RETN, P_RETGN, P_MLAN, P_QN, P_KVN, P_QHN, P_QHR, P_KHN, P_KHR, P_FFNN, P_CW, P_CB = (
    0, 8, 24, 32, 35, 37, 38, 39, 40, 41, 57, 57 + 132)
NPRM = 57 + 132 + 44


def pack_params(inp):
    p = np.zeros((128, NPRM), np.float32)
    p[:, P_RETN:P_RETN + 8] = _col(inp['ret_norm'][0])
    p[:, P_RETGN:P_RETGN + 16] = _col(inp['ret_gn'][0].reshape(-1))
    p[:, P_MLAN:P_MLAN + 8] = _col(inp['mla_norm'][0])
    p[:, P_QN:P_QN + 3] = _col(inp['mla_q_norm'][0])
    p[:, P_KVN:P_KVN + 2] = _col(inp['mla_kv_norm'][0])
    p[:, P_QHN] = inp['mla_q_head_norm'][0][0:128]
    p[0:64, P_QHR] = inp['mla_q_head_norm'][0][128:192]
    p[:, P_KHN] = inp['mla_k_head_norm'][0][0:128]
    p[0:64, P_KHR] = inp['mla_k_head_norm'][0][128:192]
    for l in range(2):
        p[:, P_FFNN + 8 * l:P_FFNN + 8 * l + 8] = _col(inp['ffn_norm'][l])
        for k in range(3):
            p[:, P_CW + (l * 3 + k) * NF:P_CW + (l * 3 + k + 1) * NF] = _col(inp['ffn_conv_w'][l, k])
        p[:, P_CB + l * NF:P_CB + (l + 1) * NF] = _col(inp['ffn_conv_b'][l])
    return p


def const_tables():
    pos = np.arange(S, dtype=np.float64)
    inv = THETA ** (-np.arange(128, dtype=np.float64) / 128.0)
    ang = inv[:, None] * pos[None, :]
    rrope = np.stack([np.cos(ang), np.sin(ang)], axis=1).astype(np.float32)
    inv = THETA ** (-np.arange(32, dtype=np.float64) / 32.0)
    ang = np.concatenate([inv, inv])[:, None] * pos[None, :]
    mrope = np.zeros((128, 2, S), np.float32)
    mrope[0:64] = np.stack([np.cos(ang), np.sin(ang)], axis=1).astype(np.float32)
    dtab = np.zeros((4, 128, S), np.float64)
    p = np.arange(128)[:, None]
    m = np.arange(S)[None, :]
    for h in range(4):
        lg = np.log1p(-2.0 ** (-5.0 - h))
        t = np.exp(lg * (m - p).astype(np.float64))
        md = np.arange(128)[None, :]
        allowed = (p // 64) <= (md // 64)
        t[:, 0:128] = np.where(allowed, np.exp(lg * np.abs(md - p)), 0.0)
        dtab[h] = t * (256.0 ** -0.5)
    dtab = dtab.astype(np.float32)
    misc = np.zeros((128, 512), np.float32)
    misc[:, 0:128] = ((p // 64) <= (np.arange(128)[None, :] // 64)).astype(np.float32)
    for i in range(32):
        misc[32 + i, 128 + i] = -1.0
        misc[i, 128 + 32 + i] = 1.0
    misc[:, 256:384] = np.eye(128, dtype=np.float32)
    misc[:, 384:512] = np.where(misc[:, 0:128] > 0, 0.0, -30000.0)
    return rrope, mrope, dtab, misc


class Buf:
    __slots__ = ('name', 'w', 'r', 'dsem', 'dcnt')

    def __init__(self, name):
        self.name = name; self.w = None; self.r = {}; self.dsem = None; self.dcnt = 0


class KB:
    def __init__(self, nc, es):
        self.nc = nc; self.es = es
        self.eng = {'pe': nc.tensor, 'act': nc.scalar, 'dve': nc.vector, 'pool': nc.gpsimd, 'sp': nc.sync}
        self.semh = {}
        for e in self.eng:
            self.semh[e] = es.enter_context(nc.semaphore('sem_' + e))
        self.cnt = {e: 0 for e in self.eng}
        self.pend = {e: False for e in self.eng}
        self.seen = {e: {} for e in self.eng}
        self.bar = {}
        self.nops = 0

    def _waits(self, e, reads, writes):
        deps = {}

        def add(tok):
            if tok is None:
                return
            k, v = tok
            if deps.get(k, 0) < v:
                deps[k] = v
        for b in reads:
            add(b.w)
        for b in writes:
            add(b.w)
            for t in b.r.values():
                add(t)
        if e != 'pool':
            for k, v in self.bar.items():
                add((k, v))
        eng = self.eng[e]; seen = self.seen[e]
        for k, v in deps.items():
            if k == e and e == 'pe':
                continue
            if seen.get(k, 0) >= v:
                continue
            if k in self.cnt:
                assert v <= self.cnt[k], ('future dependency', e, k, v, self.cnt[k])
            eng.wait_ge(self.semh[k], v)
            seen[k] = v

    def op(self, e, fn, reads=(), writes=(), signal=True):
        self._waits(e, reads, writes)
        ins = fn(self.eng[e])
        self.nops += 1
        if signal:
            self.cnt[e] += 1
            ins.then_inc(self.semh[e], 1)
            tok = (e, self.cnt[e]); self.pend[e] = False
        else:
            tok = (e, self.cnt[e] + 1); self.pend[e] = True
        for b in reads:
            b.r[e] = tok
        for b in writes:
            b.w = tok; b.r = {}
        return ins

    def dma(self, e, out, in_, reads=(), writes=(), **kw):
        tgt = writes[0]
        if tgt.dsem is None:
            self.nsem = getattr(self, 'nsem', 0) + 1
            tgt.dsem = 'd%d_%s' % (self.nsem, tgt.name)
            self.semh[tgt.dsem] = self.es.enter_context(self.nc.semaphore(tgt.dsem))
        self._waits(e, reads, writes)
        ins = self.eng[e].dma_start(out=out, in_=in_, **kw)
        tgt.dcnt += 16
        ins.then_inc(self.semh[tgt.dsem], 16)
        tok = (tgt.dsem, tgt.dcnt)
        for b in reads:
            b.r[tgt.dsem] = tok
        for b in writes:
            b.w = tok; b.r = {}
        return ins

    def barrier(self):
        for e in ('pe', 'act', 'dve'):
            assert not self.pend[e], e
            self.bar[e] = self.cnt[e]

    def wait_all(self, e, bufs):
        self._waits(e, bufs, ())


class Defer:
    def __init__(self):
        self.q = []; self.t = 0; self.n = 0

    def add(self, delay, fn, tag=0):
        self.n += 1
        self.q.append((self.t + delay, self.n, tag, fn))

    def tick(self):
        self.t += 1
        due = sorted([x for x in self.q if x[0] <= self.t], key=lambda x: (x[0], x[1]))
        self.q = [x for x in self.q if x[0] > self.t]
        for x in due:
            x[3]()

    def flush(self, maxtag=None):
        while True:
            sel = sorted([x for x in self.q if maxtag is None or x[2] <= maxtag], key=lambda x: (x[0], x[1]))
            if not sel:
                return
            self.q = [x for x in self.q if not (maxtag is None or x[2] <= maxtag)]
            for x in sel:
                x[3]()


class Rot:
    def __init__(self, items):
        self.items = list(items); self.i = 0

    def __call__(self):
        x = self.items[self.i % len(self.items)]; self.i += 1
        return x


def build_program(nseq=2, layers=('ret', 'ffn0', 'mla', 'ffn1')):
    nc = bass.Bass("TRN2", target_bir_lowering=False)
    widx, wtot = weight_index()
    xin = nc.dram_tensor("xin", [nseq, 128, 8, S], F32, kind="ExternalInput").ap()
    wts = nc.dram_tensor("wts", [wtot], F32, kind="ExternalInput").ap()
    prm_d = nc.dram_tensor("prm", [128, NPRM], F32, kind="ExternalInput").ap()
    rrope_d = nc.dram_tensor("rrope", [128, 2, S], F32, kind="ExternalInput").ap()
    mrope_d = nc.dram_tensor("mrope", [128, 2, S], F32, kind="ExternalInput").ap()
    dtab_d = nc.dram_tensor("dtab", [4, 128, S], F32, kind="ExternalInput").ap()
    misc_d = nc.dram_tensor("misc", [128, 512], F32, kind="ExternalInput").ap()
    xout = nc.dram_tensor("xout", [nseq, 128, 8, S], F32, kind="ExternalOutput").ap()

    with ExitStack() as es:
        kb = KB(nc, es)

        uid = [0]

        def sb(name, shape, dt, stack=es):
            uid[0] += 1
            return stack.enter_context(nc.sbuf_tensor("%s_%d" % (name, uid[0]), shape, dt))

        xT = sb("xT", [128, 8, S], F32)
        xb = [Buf('x%d' % i) for i in range(4)]
        ob = [Buf('o%d' % i) for i in range(4)]
        ring = sb("ring", [128, RING, SLOT], BF16)
        ringb = [Buf('ring%d' % i) for i in range(RING)]
        prm = sb("prm_s", [128, NPRM], F32); prmb = Buf('prm')
        identb = sb("identb", [128, 128], BF16)
        negmb = sb("negmb", [128, 128], BF16)
        ones = sb("ones", [128, 128], BF16); constb = Buf('const')
        psum = es.enter_context(nc.psum_tensor("psum", [128, 8, 512], F32))
        pb = [Buf('ps%d' % i) for i in range(8)]

        kb.dma('sp', prm[:], prm_d, writes=[prmb])
        kb.op('dve', lambda e: e.memset(ones[:], 1.0), writes=[constb])
        rotb = sb("rotb", [128, 128], BF16)

        ring_i = [0]

        def wtile(name):
            off, (kc, cols) = widx[name]
            n = kc * cols
            slot = ring_i[0] % RING; ring_i[0] += 1
            src = wts[off:off + 128 * n].rearrange("(p n) -> p n", p=128)
            kb.dma('pool', ring[:, slot, 0:n], src, writes=[ringb[slot]], max_dma_last_dim=8192)
            return ring[:, slot, 0:n].rearrange("p (k c) -> p k c", k=kc), ringb[slot]

        @contextmanager
        def scope():
            with ExitStack() as st:
                yield st
                kb.barrier()

        def MM(out, lhsT, rhs, start, stop, reads, writes, signal=True):
            return kb.op('pe', lambda e: e.matmul(out, lhsT, rhs, start=start, stop=stop), reads, writes, signal)

        def ACT(out, in_, func, reads, writes, **kw):
            return kb.op('act', lambda e: e.activation(out=out, in_=in_, func=func, **kw), reads, writes)

        def TT(out, a, b, op, reads, writes):
            return kb.op('dve', lambda e: e.tensor_tensor(out=out, in0=a, in1=b, op=op), reads, writes)

        def STT(out, in0, scalar, in1, op0, op1, reads, writes):
            return kb.op('dve', lambda e: e.scalar_tensor_tensor(out=out, in0=in0, scalar=scalar, in1=in1,
                                                                 op0=op0, op1=op1), reads, writes)

        def RECIP(out, in_, reads, writes):
            return kb.op('dve', lambda e: e.reciprocal(out=out, in_=in_), reads, writes)

        def bank(i):
            return psum[:, i, :], pb[i]

        sqt = sb("sqt", [128, 2, 512], BF16); sqtb = [Buf('sqt0'), Buf('sqt1')]
        rstd = sb("rstd", [128, 512], F32); rstdb = Buf('rstd')

        lnb_t = sb("lnb", [128, 512], F32); lnbb = Buf('lnb')

        def rms_rstd(ps_ap, ps_buf, nfeat, npart=128, out=None, outb=None):
            if out is None:
                out, outb = rstd, rstdb
            ACT(lnb_t[0:npart, :], ps_ap, AF.Ln, [ps_buf], [lnbb], bias=EPS, scale=1.0 / nfeat)
            ACT(out[0:npart, :], lnb_t[0:npart, :], AF.Exp, [lnbb], [outb], scale=-0.5)

        def rmsnorm_tile(t0, gain_col, hT, hbuf, hcol0, pbank):
            tile_i = t0 // 512
            ps, psb = bank(pbank)
            for c in range(8):
                ACT(sqt[:, c % 2, :], xT[:, c, t0:t0 + 512], AF.Square, [xb[tile_i]], [sqtb[c % 2]])
                MM(ps, ones[:], sqt[:, c % 2, :], c == 0, c == 7, [sqtb[c % 2], constb], [psb])
            rms_rstd(ps, psb, D)
            for c in range(8):
                STT(hT[:, c, hcol0:hcol0 + 512], xT[:, c, t0:t0 + 512], prm[:, gain_col + c:gain_col + c + 1],
                    rstd[:], ALU.mult, ALU.mult, [xb[tile_i], prmb, rstdb], [hbuf])

        def ffn_layer(l):
            with scope() as st:
                hT = sb("f_hT", [128, 8, S], BF16, st); hb = [Buf('f_h%d' % i) for i in range(4)]
                uT = sb("f_uT", [128, NF, 1024], BF16, st); ub = [Buf('f_u%d' % f) for f in range(NF)]
                halo = sb("f_halo", [128, 2, NF, 2], F32, st); halob = [[Buf('f_halo%d_%d' % (a, f)) for f in range(NF)] for a in range(2)]
                cbuf = sb("f_c", [128, 2, 512], F32, st); cb = [Buf('f_c0'), Buf('f_c1')]
                sbuf = sb("f_s", [128, 2, 512], F32, st); sbb = [Buf('f_s0'), Buf('f_s1')]
                sq8 = sb("f_sq8", [128, 8, 512], BF16, st); sq8b = Buf('f_sq8')
                rota = Rot([0, 1, 2]); rotg = Rot([3, 4, 5]); roty = Rot([6, 7])
                step = 0
                cw = lambda k, f: prm[:, P_CW + (l * 3 + k) * NF + f:P_CW + (l * 3 + k) * NF + f + 1]
                cbias = lambda f: prm[:, P_CB + l * NF + f:P_CB + l * NF + f + 1]
                gcol = P_FFNN + 8 * l

                def norm_a(ti, cs=range(8)):
                    for c in cs:
                        ACT(sq8[:, c, :], xT[:, c, ti * 512:(ti + 1) * 512], AF.Square, [xb[ti]], [sq8b])

                def norm_b(ti):
                    ps, psb = bank(6 + ti % 2)
                    for c in range(8):
                        MM(ps, ones[:], sq8[:, c, :], c == 0, c == 7, [sq8b, constb], [psb], signal=(c == 7))
                    rms_rstd(ps, psb, D)

                def norm_c(ti, cs=range(8)):
                    for c in cs:
                        STT(hT[:, c, ti * 512:(ti + 1) * 512], xT[:, c, ti * 512:(ti + 1) * 512], prm[:, gcol + c:gcol + c + 1],
                            rstd[:], ALU.mult, ALU.mult, [xb[ti], prmb, rstdb], [hb[ti]])

                for ti in range(2):
                    norm_a(ti); norm_b(ti); norm_c(ti)
                for T in range(2):
                    for f in range(NF):
                        if T == 0:
                            if f <= 7:
                                norm_a(2, [f])
                            if f == 8:
                                norm_b(2)
                            if 9 <= f <= 16:
                                norm_c(2, [f - 9]); norm_a(3, [f - 9])
                            if f == 17:
                                norm_b(3)
                            if f >= 18:
                                norm_c(3, [2 * (f - 18), 2 * (f - 18) + 1])
                        wt, wbuf = wtile('f%di%d' % (l, f))
                        for tt in range(2):
                            pa, pab = bank(rota()); pg, pgb = bank(rotg())
                            hs = slice(T * 1024 + tt * 512, T * 1024 + (tt + 1) * 512)
                            us = slice(tt * 512, (tt + 1) * 512)
                            hbi = hb[T * 2 + tt]
                            for c in range(8):
                                MM(pa, wt[:, c, 0:128], hT[:, c, hs], c == 0, c == 7, [wbuf, hbi], [pab], signal=(c == 7))
                            for c in range(8):
                                MM(pg, wt[:, c, 128:256], hT[:, c, hs], c == 0, c == 7, [wbuf, hbi], [pgb], signal=(c == 7))
                            k = step % 2; step += 1
                            cc = cbuf[:, k, :]
                            ACT(cc, pg, AF.Identity, [pgb, prmb], [cb[k]], scale=cw(2, f), bias=cbias(f))
                            hp = (T * 2 + tt) % 2
                            STT(cc[:, 1:512], pg[:, 0:511], cw(1, f), cc[:, 1:512], ALU.mult, ALU.add, [pgb, prmb, cb[k]], [cb[k]])
                            STT(cc[:, 2:512], pg[:, 0:510], cw(0, f), cc[:, 2:512], ALU.mult, ALU.add, [pgb, prmb, cb[k]], [cb[k]])
                            kb.op('dve', lambda e, hp=hp, f=f, pg=pg: e.tensor_copy(out=halo[:, hp, f, :], in_=pg[:, 510:512]),
                                  [pgb], [halob[hp][f]])
                            if not (T == 0 and tt == 0):
                                STT(cc[:, 0:1], halo[:, 1 - hp, f, 1:2], cw(1, f), cc[:, 0:1], ALU.mult, ALU.add, [halob[1 - hp][f], prmb, cb[k]], [cb[k]])
                                STT(cc[:, 0:2], halo[:, 1 - hp, f, 0:2], cw(0, f), cc[:, 0:2], ALU.mult, ALU.add, [halob[1 - hp][f], prmb, cb[k]], [cb[k]])
                            ACT(sbuf[:, k, :], cc, AF.Silu, [cb[k]], [sbb[k]])
                            TT(uT[:, f, us], sbuf[:, k, :], pa, ALU.mult, [sbb[k], pab], [ub[f]])
                    for d in range(8):
                        wa, wab = wtile('f%do%d_0' % (l, d)); wb_, wbb = wtile('f%do%d_1' % (l, d))
                        for tt in range(2):
                            py, pyb = bank(roty())
                            us = slice(tt * 512, (tt + 1) * 512)
                            ts = slice(T * 1024 + tt * 512, T * 1024 + (tt + 1) * 512)
                            for kk in range(NF):
                                w, wbf = (wa, wab) if kk < 11 else (wb_, wbb)
                                MM(py, w[:, kk % 11, :], uT[:, kk, us], kk == 0, kk == NF - 1, [wbf, ub[kk]], [pyb],
                                   signal=(kk == NF - 1))
                            xi = xb[T * 2 + tt]
                            TT(xT[:, d, ts], xT[:, d, ts], py, ALU.add, [xi, pyb], [xi])

        def ret_layer():
            with scope() as st:
                hT = sb("r_hT", [128, 8, S], BF16, st); hb = [Buf('r_h%d' % i) for i in range(4)]
                rope = sb("r_rope", [128, 2, S], F32, st); ropeb = Buf('r_rope')
                dt_ = sb("r_dt", [128, S], F32, st); dtb = Buf('r_dt')
                qT = sb("r_q", [128, 2, S], BF16, st); qb = [Buf('r_q%d' % i) for i in range(4)]
                kT = sb("r_k", [128, 2, S], BF16, st); kbf = [Buf('r_k%d' % i) for i in range(4)]
                vt = sb("r_v", [128, 16, 512], BF16, st); vb = [Buf('r_v%d' % i) for i in range(16)]
                t12 = sb("r_t", [128, 2, 512], F32, st); tb_ = [Buf('r_t0'), Buf('r_t1')]
                P = sb("r_P", [128, 2, 512], BF16, st); Pb = [Buf('r_P0'), Buf('r_P1')]
                sg = sb("r_sg", [128, 2, 4, 512], BF16, st); sgb = [Buf('r_sg0'), Buf('r_sg1')]
                sqn = sb("r_sqn", [128, 4, 512], BF16, st); sqnv = [Buf('r_sqn%d' % i) for i in range(4)]
                wv_ = sb("r_w", [128, 512], F32, st); wvb = Buf('r_w')
                u = sb("r_u", [128, 2, 4, 512], BF16, st); ub = [Buf('r_u0'), Buf('r_u1')]
                dq = Defer()
                scr = sb("r_scr", [128, 2], F32, st); scrb = Buf('r_scr')
                kb.dma('sp', rope[:], rrope_d, writes=[ropeb])
                for tt in range(4):
                    rmsnorm_tile(tt * 512, P_RETN, hT, hb[tt], tt * 512, 6 + tt % 2)
                rotp = Rot([0, 1, 2, 3, 4, 5])
                rots = Rot([4, 5]); rotx = Rot([6, 7])
                pcnt = [0]
                gw = {}

                def gproj(h, it, par, groups, banks=None):
                    evs = []
                    if (h, it) not in gw:
                        gw.clear()
                        gw[(h, it)] = (wtile('rg%da' % h), wtile('rg%db' % h))
                    (wga, wgab), (wgb_, wgbb) = gw[(h, it)]
                    i0 = it * 512
                    for gc in groups:
                        w, wbf = (wga, wgab) if gc < 2 else (wgb_, wgbb)
                        bi = rotx() if banks is None else banks[gc]
                        pg, pgb = bank(bi)
                        for c in range(8):
                            MM(pg, w[:, c, (gc % 2) * 128:(gc % 2 + 1) * 128], hT[:, c, i0:i0 + 512], c == 0, c == 7,
                               [wbf, hb[it]], [pgb], signal=(c == 7))
                        evs.append(lambda pg=pg, pgb=pgb, gc=gc: ACT(sg[:, par, gc, :], pg, AF.Silu, [pgb], [sgb[par]]))
                    return evs

                tiles = [(h, it) for h in range(4) for it in range(4)]
                for h in range(4):
                    kb.dma('sp', dt_[:], dtab_d[h], writes=[dtb])
                    for nm, dst, dstb in (('rq%d' % h, qT, qb), ('rk%d' % h, kT, kbf)):
                        w, wbf = wtile(nm)
                        for tt in range(4):
                            ts = slice(tt * 512, (tt + 1) * 512)
                            p1, p1b = bank(rotp()); p2, p2b = bank(rotp())
                            for c in range(8):
                                MM(p1, w[:, c, 0:128], hT[:, c, ts], c == 0, c == 7, [wbf, hb[tt]], [p1b], signal=(c == 7))
                            for c in range(8):
                                MM(p2, w[:, c, 128:256], hT[:, c, ts], c == 0, c == 7, [wbf, hb[tt]], [p2b], signal=(c == 7))
                            cs = rope[:, 0, ts]; sn = rope[:, 1, ts]
                            TT(t12[:, 0, :], p1, cs, ALU.mult, [p1b, ropeb], [tb_[0]])
                            TT(t12[:, 1, :], p2, sn, ALU.mult, [p2b, ropeb], [tb_[1]])
                            TT(dst[:, 0, ts], t12[:, 0, :], t12[:, 1, :], ALU.subtract, [tb_[0], tb_[1]], [dstb[tt]])
                            TT(t12[:, 0, :], p2, cs, ALU.mult, [p2b, ropeb], [tb_[0]])
                            TT(t12[:, 1, :], p1, sn, ALU.mult, [p1b, ropeb], [tb_[1]])
                            TT(dst[:, 1, ts], t12[:, 0, :], t12[:, 1, :], ALU.add, [tb_[0], tb_[1]], [dstb[tt]])
                            dq.tick()
                    dq.flush()
                    wva, wvab = wtile('rv%da' % h); wvb_, wvbb = wtile('rv%db' % h)
                    for tb in range(16):
                        pv, pvb = bank(rotp())
                        for half, (w, wbf) in enumerate(((wva, wvab), (wvb_, wvbb))):
                            for c in range(8):
                                MM(pv[:, half * 256:(half + 1) * 256], hT[:, c, tb * 128:(tb + 1) * 128], w[:, c, :],
                                   c == 0, c == 7, [wbf, hb[tb // 4]], [pvb], signal=(c == 7))
                        ACT(vt[:, tb, :], pv, AF.Copy, [pvb], [vb[tb]])
                    if h == 0:
                        for gc in range(4):
                            for ev in gproj(0, 0, 0, [gc]):
                                ev()
                    for it in range(4):
                        g = h * 4 + it; par = g % 2
                        i0 = it * 512
                        njb = 4 * it + 4

                        def emit_S(jb):
                            c0 = max(jb * 128 - i0, 0); N = 512 - c0; ilo = i0 + c0
                            bi = rots()
                            ps, psb = bank(bi)
                            for c in range(2):
                                MM(ps[:, 0:N], kT[:, c, jb * 128:(jb + 1) * 128], qT[:, c, ilo:ilo + N], c == 0, c == 1,
                                   [kbf[jb // 4], qb[it]], [psb], signal=(c == 1))
                            return ps, psb

                        nxt = emit_S(0)
                        for jb in range(njb):
                            ps, psb = nxt
                            if jb + 1 < njb:
                                nxt = emit_S(jb + 1)
                            c0 = max(jb * 128 - i0, 0); N = 512 - c0; ilo = i0 + c0
                            k = pcnt[0] % 2; pcnt[0] += 1
                            doff = ilo - jb * 128
                            TT(P[:, k, 0:N], ps[:, 0:N], dt_[:, doff:doff + N], ALU.mult, [psb, dtb], [Pb[k]])
                            if jb == njb - 2:
                                ACT(scr[:, 0:1], ones[:, 0:1], AF.Ln, [constb], [scrb])
                            dq.tick()
                            for vc in range(4):
                                MM(psum[:, vc, c0:c0 + N], vt[:, jb, vc * 128:(vc + 1) * 128], P[:, k, 0:N], jb == 0, jb == njb - 1,
                                   [vb[jb], Pb[k]], [pb[vc]], signal=(vc == 3))
                        dq.flush(maxtag=g - 2)
                        nx = tiles[g + 1] if g + 1 < 16 else None
                        gb = {0: 6, 1: 4, 2: 5, 3: 6}
                        evs = gproj(nx[0], nx[1], 1 - par, [0], gb) if nx else []
                        pss, pssb = bank(7)
                        for vc in range(4):
                            ACT(sqn[:, vc, :], psum[:, vc, :], AF.Square, [pb[vc]], [sqnv[vc]])
                        for vc in range(4):
                            MM(pss, ones[:], sqn[:, vc, :], vc == 0, vc == 3, [sqnv[vc], constb], [pssb], signal=(vc == 3))
                        rms_rstd(pss, pssb, 512)
                        if nx:
                            evs += gproj(nx[0], nx[1], 1 - par, [1], gb)
                            evs += gproj(nx[0], nx[1], 1 - par, [2], gb)
                        for ev in evs:
                            ev()
                        for vc in range(4):
                            TT(wv_[:], sg[:, par, vc, :], rstd[:], ALU.mult, [sgb[par], rstdb], [wvb])
                            gcol = P_RETGN + h * 4 + vc
                            STT(u[:, par, vc, :], psum[:, vc, :], prm[:, gcol:gcol + 1], wv_[:], ALU.mult, ALU.mult,
                                [pb[vc], prmb, wvb], [ub[par]])
                        if nx:
                            for ev in gproj(nx[0], nx[1], 1 - par, [3], gb):
                                ev()
                        dq.tick()
                        wo = {}

                        def ychunk(d, h=h, i0=i0, it=it, par=par, wo=wo):
                            key = 'a' if d < 4 else 'b'
                            if key not in wo:
                                wo[key] = wtile('ro%d%s' % (h, key))
                            w, wbf = wo[key]
                            py, pyb = bank(rotx())
                            for vc in range(4):
                                MM(py, w[:, vc, (d % 4) * 128:(d % 4 + 1) * 128], u[:, par, vc, :], vc == 0, vc == 3,
                                   [wbf, ub[par]], [pyb], signal=(vc == 3))
                            TT(xT[:, d, i0:i0 + 512], xT[:, d, i0:i0 + 512], py, ALU.add, [xb[it], pyb], [xb[it]])
                        late = (4 * nx[1] + 4 + 1) if (nx and nx[0] == h) else 4
                        for d in range(8):
                            dq.add(1 + d // 2 if d < 6 else late, (lambda d=d, f=ychunk: f(d)), tag=g)
                dq.flush()

        def mla_layer():
            with scope() as so:
                cqn = sb("m_cqn", [128, 3, S], BF16, so); cqb = [Buf('m_cq%d' % i) for i in range(4)]
                ckvn = sb("m_ckvn", [128, 2, S], BF16, so); ckb = [Buf('m_ck%d' % i) for i in range(4)]
                KrT = sb("m_kr", [128, S], BF16, so); krb = [Buf('m_kr%d' % i) for i in range(4)]
                sqkr = sb("m_sqkr", [128, S], BF16, so); sqkrb = [Buf('m_sqkr%d' % i) for i in range(4)]
                mrope = sb("m_rope", [128, 2, S], F32, so); mropeb = Buf('m_rope')
                rstdk = sb("m_rstdk", [128, 16, 4], F32, so); rstdkb = Buf('m_rstdk')
                rtk = sb("m_rtk", [128, 16, 4], F32, so); rtkb = Buf('m_rtk')
                xg = sb("m_xg", [128, 512], BF16, so); xgb = Buf('m_xg')
                t12 = sb("m_t", [128, 2, 512], F32, so); tb_ = [Buf('m_t0'), Buf('m_t1')]
                kb.dma('sp', mrope[:], mrope_d, writes=[mropeb])

                def rope64_mm(src_ap, src_buf, pbank):
                    pr, prb = bank(pbank)
                    MM(pr, rotb[:], src_ap, True, True, [constb, src_buf], [prb])
                    return pr, prb

                def rope64_ew(src_ap, src_buf, pr, prb, dst_ap, dst_buf, ts):
                    TT(t12[:, 0, :], src_ap, mrope[:, 0, ts], ALU.mult, [src_buf, mropeb], [tb_[0]])
                    TT(t12[:, 1, :], pr, mrope[:, 1, ts], ALU.mult, [prb, mropeb], [tb_[1]])
                    TT(dst_ap, t12[:, 0, :], t12[:, 1, :], ALU.add, [tb_[0], tb_[1]], [dst_buf])

                with scope() as s1:
                    hT = sb("m_hT", [128, 8, S], BF16, s1); hb = [Buf('m_h%d' % i) for i in range(4)]
                    sq3 = sb("m_sq3", [128, 5, 512], BF16, s1); sq3b = Buf('m_sq3'); sq3c = Buf('m_sq3c')
                    rstd2 = sb("m_rstd2", [128, 512], F32, s1); rstd2b = Buf('m_rstd2')
                    for tt in range(4):
                        rmsnorm_tile(tt * 512, P_MLAN, hT, hb[tt], tt * 512, 6 + tt % 2)
                    wi = [wtile('mi0'), wtile('mi1'), wtile('mi2')]
                    for tt in range(4):
                        ts = slice(tt * 512, (tt + 1) * 512)
                        def proj(fcs):
                            for fc in fcs:
                                w, wbf = wi[fc // 2]; col = (fc % 2) * 128
                                for c in range(8):
                                    MM(psum[:, fc, :], w[:, c, col:col + 128], hT[:, c, ts], c == 0, c == 7, [wbf, hb[tt]], [pb[fc]],
                                       signal=(c == 7))
                        proj(range(0, 3))
                        ACT(sq3[:, 0:3, :], psum[:, 0:3, :], AF.Square, [pb[0], pb[1], pb[2]], [sq3b])
                        proj(range(3, 6))
                        ACT(sq3[:, 3:5, :], psum[:, 3:5, :], AF.Square, [pb[3], pb[4]], [sq3c])
                        pss, pssb = bank(6)
                        for k in range(3):
                            MM(pss, ones[:], sq3[:, k, :], k == 0, k == 2, [sq3b, constb], [pssb], signal=(k == 2))
                        rms_rstd(pss, pssb, 384)
                        ACT(sqkr[:, ts], psum[:, 5, :], AF.Square, [pb[5]], [sqkrb[tt]])
                        ACT(xg[:], psum[:, 5, :], AF.Identity, [pb[5], prmb], [xgb], scale=prm[:, P_KHR:P_KHR + 1])
                        for k in range(3):
                            STT(cqn[:, k, ts], psum[:, k, :], prm[:, P_QN + k:P_QN + k + 1], rstd[:], ALU.mult, ALU.mult,
                                [pb[k], prmb, rstdb], [cqb[tt]])
                        pss, pssb = bank(7)
                        for k in range(2):
                            MM(pss, ones[:], sq3[:, 3 + k, :], k == 0, k == 1, [sq3c, constb], [pssb], signal=(k == 1))
                        rms_rstd(pss, pssb, 256, out=rstd2, outb=rstd2b)
                        for k in range(2):
                            STT(ckvn[:, k, ts], psum[:, 3 + k, :], prm[:, P_KVN + k:P_KVN + k + 1], rstd2[:], ALU.mult, ALU.mult,
                                [pb[3 + k], prmb, rstd2b], [ckb[tt]])
                        pr, prb = rope64_mm(xg[:], xgb, 6)
                        rope64_ew(xg[:], xgb, pr, prb, KrT[:, ts], krb[tt], ts)

                for G in range(2):
                    with scope() as s2:
                        KnT = sb("m_kn", [128, 4, S], BF16, s2); knb = [[Buf('m_kn%d_%d' % (a, i)) for i in range(4)] for a in range(4)]
                        Vt = sb("m_v", [128, 16, 512], BF16, s2); vb = [Buf('m_v%d' % i) for i in range(16)]
                        QnT = sb("m_qn", [128, 2, S], BF16, s2); qnb = [[Buf('m_qn%d_%d' % (a, i)) for i in range(4)] for a in range(2)]
                        QrT = sb("m_qr", [128, 2, S], BF16, s2); qrb = [[Buf('m_qr%d_%d' % (a, i)) for i in range(4)] for a in range(2)]
                        onT = sb("m_on", [128, 2, S], BF16, s2); onb = [[Buf('m_on%d_%d' % (a, i)) for i in range(4)] for a in range(2)]
                        sqk = sb("m_sqk", [128, 2, 512], BF16, s2); sqkb = [Buf('m_sqk0'), Buf('m_sqk1')]
                        sq2 = sb("m_sq2", [128, 512], BF16, s2); sq2b = Buf('m_sq2')
                        P = sb("m_P", [128, 3, 512], BF16, s2); Pb = [Buf('m_P0'), Buf('m_P1'), Buf('m_P2')]
                        rden, rdenb = rstd, rstdb
                        rstdq = sb("m_rstdq", [128, 512], F32, s2); rstdqb = Buf('m_rstdq')
                        Osb = sb("m_osb", [128, 512], F32, s2); Osbb = Buf('m_osb')
                        lnd = sb("m_lnd", [128, 512], F32, s2); lndb = Buf('m_lnd')
                        sqq = sb("m_sqq", [128, 512], BF16, s2); sqqb = Buf('m_sqq')
                        dq = Defer()
                        def qchain(h, tt, qp):
                            ts = slice(tt * 512, (tt + 1) * 512)
                            wq, wqb_ = wtile('mq%d' % h); cb0 = 0
                            pqn, pqnb = bank(5); pqr, pqrb = bank(6)
                            for c in range(3):
                                MM(pqn, wq[:, c, cb0:cb0 + 128], cqn[:, c, ts], c == 0, c == 2, [wqb_, cqb[tt]], [pqnb], signal=(c == 2))
                            for c in range(3):
                                MM(pqr, wq[:, c, cb0 + 128:cb0 + 256], cqn[:, c, ts], c == 0, c == 2, [wqb_, cqb[tt]], [pqrb],
                                   signal=(c == 2))
                            ACT(sqq[:], pqn, AF.Square, [pqnb], [sqqb])
                            dq.add(1, lambda: ACT(sq2[:], pqr, AF.Square, [pqrb], [sq2b]))

                            pssh = [None]

                            def stB():
                                pss, pssb = bank(7)
                                MM(pss, ones[:], sqq[:], True, False, [sqqb, constb], [pssb])
                                MM(pss, ones[:], sq2[:], False, True, [sq2b, constb], [pssb])
                                ACT(lnb_t[:], pss, AF.Ln, [pssb], [lnbb], bias=EPS, scale=1.0 / 192)

                            def stB2():
                                ACT(rstdq[:], lnb_t[:], AF.Exp, [lnbb], [rstdqb], scale=-0.5)

                            def stC():
                                STT(QnT[:, qp, ts], pqn, prm[:, P_QHN:P_QHN + 1], rstdq[:], ALU.mult, ALU.mult,
                                    [pqnb, prmb, rstdqb], [qnb[qp][tt]])
                                STT(xg[:], pqr, prm[:, P_QHR:P_QHR + 1], rstdq[:], ALU.mult, ALU.mult,
                                    [pqrb, prmb, rstdqb], [xgb])

                            def stD():
                                pr, prb = rope64_mm(xg[:], xgb, 7)
                                dq.add(2, lambda: rope64_ew(xg[:], xgb, pr, prb, QrT[:, qp, ts], qrb[qp][tt], ts))
                            dq.add(3, stB)
                            dq.add(4, stB2)
                            dq.add(6, stC)
                            dq.add(7, stD)

                        rotp = Rot([2, 3, 4])
                        wkn, wknb = wtile('mkn')
                        pssk, psskb = bank(1)
                        nsq = 0
                        kticks = [0]

                        def ktick():
                            if kticks[0] % 8 == 0 and kticks[0] // 8 < 4:
                                qchain(4 * G, kticks[0] // 8, 0)
                            kticks[0] += 1
                            dq.tick()

                        prev = None
                        for q4 in range(4):
                            ACT(Osb[:, q4 * 128:(q4 + 1) * 128], ones[:], AF.Identity, [constb, prmb], [Osbb], scale=prm[:, P_KHN:P_KHN + 1])
                        for hl in range(4):
                            h = 4 * G + hl
                            for tt in range(4):
                                ts = slice(tt * 512, (tt + 1) * 512)
                                pk, pkb = bank(rotp())
                                for c in range(2):
                                    MM(pk, wkn[:, c, h * 128:(h + 1) * 128], ckvn[:, c, ts], c == 0, c == 1, [wknb, ckb[tt]], [pkb],
                                       signal=(c == 1))
                                k = nsq % 2; nsq += 1
                                ACT(sqk[:, k, :], pk, AF.Square, [pkb], [sqkb[k]])

                                def tiny(hl=hl, tt=tt, k=k, pk=pk, pkb=pkb, ts=ts):
                                    TT(KnT[:, hl, ts], pk, Osb[:], ALU.mult, [pkb, Osbb, sqkb[k]], [knb[hl][tt]])
                                    for b in range(4):
                                        tbk = tt * 4 + b
                                        col = tbk * 4 + hl
                                        MM(pssk[:, col:col + 1], sqk[:, k, b * 128:(b + 1) * 128], ones[:, 0:1], True, False,
                                           [sqkb[k], constb], [psskb])
                                        MM(pssk[:, col:col + 1], sqkr[:, tbk * 128:(tbk + 1) * 128], ones[:, 0:1], False, True,
                                           [sqkrb[tt], constb], [psskb])
                                if prev is not None:
                                    prev()
                                prev = tiny
                                ktick()
                        prev()
                        f2 = lambda a: a[:].rearrange("p a b -> p (a b)")
                        ACT(f2(rtk), pssk[:, 0:64], AF.Ln, [psskb], [rtkb], bias=EPS, scale=1.0 / 192)
                        ACT(f2(rstdk), f2(rtk), AF.Exp, [rtkb], [rstdkb], scale=-0.5)
                        kb.op('dve', lambda e: e.tensor_scalar(out=f2(rstdk), in0=f2(rstdk),
                                                               scalar1=float(192.0 ** -0.5), scalar2=None, op0=ALU.mult),
                              [rstdkb], [rstdkb])
                        wv, wvb = wtile('mv')
                        for tbk in range(16):
                            pv, pvb = bank(rotp())
                            for c in range(2):
                                MM(pv, ckvn[:, c, tbk * 128:(tbk + 1) * 128], wv[:, c, G * 512:(G + 1) * 512], c == 0, c == 1,
                                   [wvb, ckb[tbk // 4]], [pvb], signal=(c == 1))
                            ACT(Vt[:, tbk, :], pv, AF.Copy, [pvb], [vb[tbk]])
                            ktick()

                        dq.flush()
                        rots = Rot([2, 3, 4]); roty = Rot([5, 6, 7])
                        pcnt = 0
                        for hl in range(4):
                            h = 4 * G + hl; qp = hl % 2
                            hstep = 0
                            for it in range(4):
                                i0 = it * 512
                                njb = 4 * it + 4

                                def emit_S(jb, i0=i0, it=it):
                                    c0 = max(jb * 128 - i0, 0); N = 512 - c0; ilo = i0 + c0
                                    ps, psb = bank(rots())
                                    MM(ps[:, 0:N], KnT[:, hl, jb * 128:(jb + 1) * 128], QnT[:, qp, ilo:ilo + N], True, False,
                                       [knb[hl][jb // 4], qnb[qp][it]], [psb], signal=False)
                                    diag = jb >= 4 * it
                                    MM(ps[:, 0:N], KrT[:, jb * 128:(jb + 1) * 128], QrT[:, qp, ilo:ilo + N], False, not diag,
                                       [krb[jb // 4], qrb[qp][it]], [psb], signal=not diag)
                                    if diag:
                                        MM(ps[:, 0:128], identb[:], negmb[:], False, True, [constb], [psb])
                                    return ps, psb
                                pend = [emit_S(0), emit_S(1)]
                                for jb in range(njb):
                                    ps, psb = pend.pop(0)
                                    if jb + 2 < njb:
                                        pend.append(emit_S(jb + 2))
                                    c0 = max(jb * 128 - i0, 0); N = 512 - c0
                                    k = pcnt % 3; pcnt += 1
                                    ACT(P[:, k, 0:N], ps[:, 0:N], AF.Exp, [psb, rstdkb], [Pb[k]], scale=rstdk[:, jb, hl:hl + 1])
                                    if hl + 1 < 4 and hstep % 10 == 0:
                                        qchain(h + 1, hstep // 10, 1 - qp)
                                    hstep += 1
                                    dq.tick()
                                    MM(psum[:, 0, c0:c0 + N], Vt[:, jb, hl * 128:(hl + 1) * 128], P[:, k, 0:N], jb == 0, jb == njb - 1,
                                       [vb[jb], Pb[k]], [pb[0]], signal=False)
                                    MM(psum[:, 1, c0:c0 + N], ones[:], P[:, k, 0:N], jb == 0, jb == njb - 1, [constb, Pb[k]], [pb[1]])
                                kb.op('dve', lambda e: e.tensor_copy(out=Osb[:], in_=psum[:, 0, :]), [pb[0]], [Osbb])
                                ACT(lnd[:], psum[:, 1, :], AF.Ln, [pb[1]], [lndb])

                                def fin(qp=qp, i0=i0, it=it):
                                    ACT(rden[:], lnd[:], AF.Exp, [lndb], [rdenb], scale=-1.0)
                                    TT(onT[:, qp, i0:i0 + 512], Osb[:], rden[:], ALU.mult, [Osbb, rdenb], [onb[qp][it]])
                                dq.add(1, fin)
                            dq.flush()
                            if hl % 2 == 1:
                                wo0 = wtile('mo%d' % (h - 1)); wo1 = wtile('mo%d' % h)
                                for it in range(4):
                                    i0 = it * 512
                                    for d in range(8):
                                        py, pyb = bank(roty())
                                        MM(py, wo0[0][:, 0, d * 128:(d + 1) * 128], onT[:, 0, i0:i0 + 512], True, False,
                                           [wo0[1], onb[0][it]], [pyb], signal=False)
                                        MM(py, wo1[0][:, 0, d * 128:(d + 1) * 128], onT[:, 1, i0:i0 + 512], False, True,
                                           [wo1[1], onb[1][it]], [pyb])
                                        TT(xT[:, d, i0:i0 + 512], xT[:, d, i0:i0 + 512], py, ALU.add, [xb[it], pyb], [xb[it]])

        tmpst = ExitStack()
        miscf = sb("miscf", [128, 512], F32, tmpst); miscb = Buf('misc')
        kb.dma('sp', miscf[:], misc_d, writes=[miscb])
        kb.op('dve', lambda e: e.tensor_copy(out=identb[:], in_=miscf[:, 256:384]), reads=[miscb], writes=[constb])
        kb.op('dve', lambda e: e.tensor_copy(out=negmb[:], in_=miscf[:, 384:512]), reads=[miscb], writes=[constb])
        kb.op('dve', lambda e: e.tensor_copy(out=rotb[:], in_=miscf[:, 128:256]), reads=[miscb], writes=[constb])
        kb.barrier()
        tmpst.close()

        for s in range(nseq):
            for tt in range(4):
                ts = slice(tt * 512, (tt + 1) * 512)
                kb.dma('sp', xT[:, :, ts], xin[s][:, :, ts], writes=[xb[tt]])
            for ly in layers:
                if ly == 'ret':
                    ret_layer()
                elif ly == 'mla':
                    mla_layer()
                elif ly == 'ffn0':
                    ffn_layer(0)
                elif ly == 'ffn1':
                    ffn_layer(1)
            for tt in range(4):
                ts = slice(tt * 512, (tt + 1) * 512)
                kb.dma('sp', xout[s][:, :, ts], xT[:, :, ts], reads=[xb[tt]], writes=[ob[tt]])
        kb.wait_all('sp', ob)
        kb.wait_all('act', ob)
    return nc


_PROG = {}


def _get_prog(nseq, layers):
    key = (nseq, tuple(layers))
    if key not in _PROG:
        _PROG[key] = build_program(nseq, layers)
    return _PROG[key]


def kernel(**inp):
    inp = {k: np.asarray(v) for k, v in inp.items()}
    x = inp['x'].astype(np.float32, copy=False)
    B = x.shape[0]
    nseq = B // NC8
    xl = np.ascontiguousarray(x.reshape(B, S, 8, 128).transpose(0, 3, 2, 1))
    wts = pack_weights(inp)
    prm = pack_params(inp)
    rrope, mrope, dtab, misc = const_tables()
    nc = _get_prog(nseq, ('ret', 'ffn0', 'mla', 'ffn1'))
    in_maps = []
    for c in range(NC8):
        in_maps.append({"xin": xl[c * nseq:(c + 1) * nseq], "wts": wts, "prm": prm, "rrope": rrope,
                        "mrope": mrope, "dtab": dtab, "misc": misc})
    res = run_bass_kernel_spmd(nc, in_maps, core_ids=list(range(NC8)))
    outs = [np.asarray(r["xout"]) for r in res.results]
    o = np.concatenate(outs, axis=0)
    return np.ascontiguousarray(o.transpose(0, 3, 2, 1).reshape(B, S, D)).astype(np.float32, copy=False)
```

```python
import numpy as np
from contextlib import ExitStack, contextmanager
import concourse.bass as bass
import concourse.mybir as mybir
from concourse.bass_utils import run_bass_kernel_spmd

F32 = mybir.dt.float32
BF16 = mybir.dt.bfloat16
AF = mybir.ActivationFunctionType
ALU = mybir.AluOpType

D = 1024
S = 2048
NC8 = 8
EPS = 1e-6
THETA = 10000.0
FFN = 2816
NF = FFN // 128
RING = 4
SLOT = 2048


def _kt(w):
    K, C = w.shape
    return np.ascontiguousarray(w.reshape(K // 128, 128, C).transpose(1, 0, 2))


def _col(v):
    n = v.shape[0] // 128
    return np.ascontiguousarray(v.reshape(n, 128).T)


class WPack:
    def __init__(self):
        self.parts = []
        self.index = {}
        self.off = 0

    def add(self, name, arr3):
        a = np.ascontiguousarray(arr3, dtype=np.float32)
        assert a.shape[0] == 128
        n = int(np.prod(a.shape[1:]))
        assert n <= SLOT, (name, a.shape)
        self.index[name] = (self.off, tuple(a.shape[1:]))
        self.parts.append(a.reshape(-1))
        self.off += 128 * n


def weight_index():
    idx = {}
    off = 0

    def add(name, kc, cols):
        nonlocal off
        idx[name] = (off, (kc, cols))
        off += 128 * kc * cols
    for h in range(4):
        add('rq%d' % h, 8, 256); add('rk%d' % h, 8, 256)
        add('rv%da' % h, 8, 256); add('rv%db' % h, 8, 256)
        add('rg%da' % h, 8, 256); add('rg%db' % h, 8, 256)
        add('ro%da' % h, 4, 512); add('ro%db' % h, 4, 512)
    for l in range(2):
        for f in range(NF):
            add('f%di%d' % (l, f), 8, 256)
        for d in range(8):
            add('f%do%d_0' % (l, d), 11, 128); add('f%do%d_1' % (l, d), 11, 128)
    add('mi0', 8, 256); add('mi1', 8, 256); add('mi2', 8, 256)
    for h in range(8):
        add('mq%d' % h, 3, 256)
    add('mkn', 2, 1024); add('mv', 2, 1024)
    for h in range(8):
        add('mo%d' % h, 1, 1024)
    return idx, off


def pack_weights(inp):
    wp = WPack()
    rwi = inp['ret_w_in'][0]; rwo = inp['ret_w_out'][0]
    for h in range(4):
        wp.add('rq%d' % h, _kt(rwi[:, h * 256:(h + 1) * 256]))
        wp.add('rk%d' % h, _kt(rwi[:, 1024 + h * 256:1024 + (h + 1) * 256]))
        wp.add('rv%da' % h, _kt(rwi[:, 2048 + h * 512:2048 + h * 512 + 256]))
        wp.add('rv%db' % h, _kt(rwi[:, 2048 + h * 512 + 256:2048 + (h + 1) * 512]))
        wp.add('rg%da' % h, _kt(rwi[:, 4096 + h * 512:4096 + h * 512 + 256]))
        wp.add('rg%db' % h, _kt(rwi[:, 4096 + h * 512 + 256:4096 + (h + 1) * 512]))
        wp.add('ro%da' % h, _kt(rwo[h * 512:(h + 1) * 512, 0:512]))
        wp.add('ro%db' % h, _kt(rwo[h * 512:(h + 1) * 512, 512:1024]))
    for l in range(2):
        wi = inp['ffn_w_in'][l]; wo = inp['ffn_w_out'][l]
        for f in range(NF):
            wp.add('f%di%d' % (l, f), _kt(np.concatenate(
                [wi[:, f * 128:(f + 1) * 128], wi[:, FFN + f * 128:FFN + (f + 1) * 128]], axis=1)))
        for d in range(8):
            wp.add('f%do%d_0' % (l, d), _kt(wo[0:1408, d * 128:(d + 1) * 128]))
            wp.add('f%do%d_1' % (l, d), _kt(wo[1408:2816, d * 128:(d + 1) * 128]))
    mwi = inp['mla_w_in'][0]
    wp.add('mi0', _kt(mwi[:, 0:256])); wp.add('mi1', _kt(mwi[:, 256:512]))
    wp.add('mi2', _kt(np.concatenate([mwi[:, 512:704], np.zeros((1024, 64), np.float32)], axis=1)))
    wqb = inp['mla_w_qb'][0]
    for h in range(8):
        wp.add('mq%d' % h, _kt(np.concatenate([wqb[:, h * 192:(h + 1) * 192], np.zeros((384, 64), np.float32)], axis=1)))
    wkvb = inp['mla_w_kvb'][0].reshape(256, 8, 256)
    wp.add('mkn', _kt(np.ascontiguousarray(wkvb[:, :, 0:128]).reshape(256, 1024)))
    wp.add('mv', _kt(np.ascontiguousarray(wkvb[:, :, 128:256]).reshape(256, 1024)))
    mwo = inp['mla_w_out'][0]
    for h in range(8):
        wp.add('mo%d' % h, _kt(mwo[h * 128:(h + 1) * 128, :]))
    idx, tot = weight_index()
    assert tot == wp.off
    for k in idx:
        assert idx[k] == wp.index[k], k
    return np.concatenate(wp.parts)


P_RETN, P_RETGN, P_MLAN, P_QN, P_KVN, P_QHN, P_QHR, P_KHN, P_KHR, P_FFNN, P_CW, P_CB = (
    0, 8, 24, 32, 35, 37, 38, 39, 40, 41, 57, 57 + 132)
NPRM = 57 + 132 + 44


def pack_params(inp):
    p = np.zeros((128, NPRM), np.float32)
    p[:, P_RETN:P_RETN + 8] = _col(inp['ret_norm'][0])
    p[:, P_RETGN:P_RETGN + 16] = _col(inp['ret_gn'][0].reshape(-1))
    p[:, P_MLAN:P_MLAN + 8] = _col(inp['mla_norm'][0])
    p[:, P_QN:P_QN + 3] = _col(inp['mla_q_norm'][0])
    p[:, P_KVN:P_KVN + 2] = _col(inp['mla_kv_norm'][0])
    p[:, P_QHN] = inp['mla_q_head_norm'][0][0:128]
    p[0:64, P_QHR] = inp['mla_q_head_norm'][0][128:192]
    p[:, P_KHN] = inp['mla_k_head_norm'][0][0:128]
    p[0:64, P_KHR] = inp['mla_k_head_norm'][0][128:192]
    for l in range(2):
        p[:, P_FFNN + 8 * l:P_FFNN + 8 * l + 8] = _col(inp['ffn_norm'][l])
        for k in range(3):
            p[:, P_CW + (l * 3 + k) * NF:P_CW + (l * 3 + k + 1) * NF] = _col(inp['ffn_conv_w'][l, k])
        p[:, P_CB + l * NF:P_CB + (l + 1) * NF] = _col(inp['ffn_conv_b'][l])
    return p


def const_tables():
    pos = np.arange(S, dtype=np.float64)
    inv = THETA ** (-np.arange(128, dtype=np.float64) / 128.0)
    ang = inv[:, None] * pos[None, :]
    rrope = np.stack([np.cos(ang), np.sin(ang)], axis=1).astype(np.float32)
    inv = THETA ** (-np.arange(32, dtype=np.float64) / 32.0)
    ang = np.concatenate([inv, inv])[:, None] * pos[None, :]
    mrope = np.zeros((128, 2, S), np.float32)
    mrope[0:64] = np.stack([np.cos(ang), np.sin(ang)], axis=1).astype(np.float32)
    dtab = np.zeros((4, 128, S), np.float64)
    p = np.arange(128)[:, None]
    m = np.arange(S)[None, :]
    for h in range(4):
        lg = np.log1p(-2.0 ** (-5.0 - h))
        t = np.exp(lg * (m - p).astype(np.float64))
        md = np.arange(128)[None, :]
        allowed = (p // 64) <= (md // 64)
        t[:, 0:128] = np.where(allowed, np.exp(lg * np.abs(md - p)), 0.0)
        dtab[h] = t * (256.0 ** -0.5)
    dtab = dtab.astype(np.float32)
    misc = np.zeros((128, 512), np.float32)
    misc[:, 0:128] = ((p // 64) <= (np.arange(128)[None, :] // 64)).astype(np.float32)
    for i in range(32):
        misc[32 + i, 128 + i] = -1.0
        misc[i, 128 + 32 + i] = 1.0
    misc[:, 256:384] = np.eye(128, dtype=np.float32)
    misc[:, 384:512] = np.where(misc[:, 0:128] > 0, 0.0, -30000.0)
    return rrope, mrope, dtab, misc


class Buf:
    __slots__ = ('name', 'w', 'r', 'dsem', 'dcnt')

    def __init__(self, name):
        self.name = name; self.w = None; self.r = {}; self.dsem = None; self.dcnt = 0


class KB:
    def __init__(self, nc, es):
        self.nc = nc; self.es = es
        self.eng = {'pe': nc.tensor, 'act': nc.scalar, 'dve': nc.vector, 'pool': nc.gpsimd, 'sp': nc.sync}
        self.semh = {}
        for e in self.eng:
            self.semh[e] = es.enter_context(nc.semaphore('sem_' + e))
        self.cnt = {e: 0 for e in self.eng}
        self.pend = {e: False for e in self.eng}
        self.seen = {e: {} for e in self.eng}
        self.bar = {}
        self.nops = 0

    def _waits(self, e, reads, writes):
        deps = {}

        def add(tok):
            if tok is None:
                return
            k, v = tok
            if deps.get(k, 0) < v:
                deps[k] = v
        for b in reads:
            add(b.w)
        for b in writes:
            add(b.w)
            for t in b.r.values():
                add(t)
        if e != 'pool':
            for k, v in self.bar.items():
                add((k, v))
        eng = self.eng[e]; seen = self.seen[e]
        for k, v in deps.items():
            if k == e and e == 'pe':
                continue
            if seen.get(k, 0) >= v:
                continue
            if k in self.cnt:
                assert v <= self.cnt[k], ('future dependency', e, k, v, self.cnt[k])
            eng.wait_ge(self.semh[k], v)
            seen[k] = v

    def op(self, e, fn, reads=(), writes=(), signal=True):
        self._waits(e, reads, writes)
        ins = fn(self.eng[e])
        self.nops += 1
        if signal:
            self.cnt[e] += 1
            ins.then_inc(self.semh[e], 1)
            tok = (e, self.cnt[e]); self.pend[e] = False
        else:
            tok = (e, self.cnt[e] + 1); self.pend[e] = True
        for b in reads:
            b.r[e] = tok
        for b in writes:
            b.w = tok; b.r = {}
        return ins

    def dma(self, e, out, in_, reads=(), writes=(), **kw):
        tgt = writes[0]
        if tgt.dsem is None:
            self.nsem = getattr(self, 'nsem', 0) + 1
            tgt.dsem = 'd%d_%s' % (self.nsem, tgt.name)
            self.semh[tgt.dsem] = self.es.enter_context(self.nc.semaphore(tgt.dsem))
        self._waits(e, reads, writes)
        ins = self.eng[e].dma_start(out=out, in_=in_, **kw)
        tgt.dcnt += 16
        ins.then_inc(self.semh[tgt.dsem], 16)
        tok = (tgt.dsem, tgt.dcnt)
        for b in reads:
            b.r[tgt.dsem] = tok
        for b in writes:
            b.w = tok; b.r = {}
        return ins

    def barrier(self):
        for e in ('pe', 'act', 'dve'):
            assert not self.pend[e], e
            self.bar[e] = self.cnt[e]

    def wait_all(self, e, bufs):
        self._waits(e, bufs, ())


class Defer:
    def __init__(self):
        self.q = []; self.t = 0; self.n = 0

    def add(self, delay, fn, tag=0):
        self.n += 1
        self.q.append((self.t + delay, self.n, tag, fn))

    def tick(self):
        self.t += 1
        due = sorted([x for x in self.q if x[0] <= self.t], key=lambda x: (x[0], x[1]))
        self.q = [x for x in self.q if x[0] > self.t]
        for x in due:
            x[3]()

    def flush(self, maxtag=None):
        while True:
            sel = sorted([x for x in self.q if maxtag is None or x[2] <= maxtag], key=lambda x: (x[0], x[1]))
            if not sel:
                return
            self.q = [x for x in self.q if not (maxtag is None or x[2] <= maxtag)]
            for x in sel:
                x[3]()


class Rot:
    def __init__(self, items):
        self.items = list(items); self.i = 0

    def __call__(self):
        x = self.items[self.i % len(self.items)]; self.i += 1
        return x


def build_program(nseq=2, layers=('ret', 'ffn0', 'mla', 'ffn1')):
    nc = bass.Bass("TRN2", target_bir_lowering=False)
    widx, wtot = weight_index()
    xin = nc.dram_tensor("xin", [nseq, 128, 8, S], F32, kind="ExternalInput").ap()
    wts = nc.dram_tensor("wts", [wtot], F32, kind="ExternalInput").ap()
    prm_d = nc.dram_tensor("prm", [128, NPRM], F32, kind="ExternalInput").ap()
    rrope_d = nc.dram_tensor("rrope", [128, 2, S], F32, kind="ExternalInput").ap()
    mrope_d = nc.dram_tensor("mrope", [128, 2, S], F32, kind="ExternalInput").ap()
    dtab_d = nc.dram_tensor("dtab", [4, 128, S], F32, kind="ExternalInput").ap()
    misc_d = nc.dram_tensor("misc", [128, 512], F32, kind="ExternalInput").ap()
    xout = nc.dram_tensor("xout", [nseq, 128, 8, S], F32, kind="ExternalOutput").ap()

    with ExitStack() as es:
        kb = KB(nc, es)

        uid = [0]

        def sb(name, shape, dt, stack=es):
            uid[0] += 1
            return stack.enter_context(nc.sbuf_tensor("%s_%d" % (name, uid[0]), shape, dt))

        xT = sb("xT", [128, 8, S], F32)
        xb = [Buf('x%d' % i) for i in range(4)]
        ob = [Buf('o%d' % i) for i in range(4)]
        ring = sb("ring", [128, RING, SLOT], BF16)
        ringb = [Buf('ring%d' % i) for i in range(RING)]
        prm = sb("prm_s", [128, NPRM], F32); prmb = Buf('prm')
        identb = sb("identb", [128, 128], BF16)
        negmb = sb("negmb", [128, 128], BF16)
        ones = sb("ones", [128, 128], BF16); constb = Buf('const')
        psum = es.enter_context(nc.psum_tensor("psum", [128, 8, 512], F32))
        pb = [Buf('ps%d' % i) for i in range(8)]

        kb.dma('sp', prm[:], prm_d, writes=[prmb])
        kb.op('dve', lambda e: e.memset(ones[:], 1.0), writes=[constb])
        rotb = sb("rotb", [128, 128], BF16)

        ring_i = [0]

        def wtile(name):
            off, (kc, cols) = widx[name]
            n = kc * cols
            slot = ring_i[0] % RING; ring_i[0] += 1
            src = wts[off:off + 128 * n].rearrange("(p n) -> p n", p=128)
            kb.dma('pool', ring[:, slot, 0:n], src, writes=[ringb[slot]], max_dma_last_dim=8192)
            return ring[:, slot, 0:n].rearrange("p (k c) -> p k c", k=kc), ringb[slot]

        @contextmanager
        def scope():
            with ExitStack() as st:
                yield st
                kb.barrier()

        def MM(out, lhsT, rhs, start, stop, reads, writes, signal=True):
            return kb.op('pe', lambda e: e.matmul(out, lhsT, rhs, start=start, stop=stop), reads, writes, signal)

        def ACT(out, in_, func, reads, writes, **kw):
            return kb.op('act', lambda e: e.activation(out=out, in_=in_, func=func, **kw), reads, writes)

        def TT(out, a, b, op, reads, writes):
            return kb.op('dve', lambda e: e.tensor_tensor(out=out, in0=a, in1=b, op=op), reads, writes)

        def STT(out, in0, scalar, in1, op0, op1, reads, writes):
            return kb.op('dve', lambda e: e.scalar_tensor_tensor(out=out, in0=in0, scalar=scalar, in1=in1,
                                                                 op0=op0, op1=op1), reads, writes)

        def RECIP(out, in_, reads, writes):
            return kb.op('dve', lambda e: e.reciprocal(out=out, in_=in_), reads, writes)

        def bank(i):
            return psum[:, i, :], pb[i]

        sqt = sb("sqt", [128, 2, 512], BF16); sqtb = [Buf('sqt0'), Buf('sqt1')]
        rstd = sb("rstd", [128, 512], F32); rstdb = Buf('rstd')

        lnb_t = sb("lnb", [128, 512], F32); lnbb = Buf('lnb')

        def rms_rstd(ps_ap, ps_buf, nfeat, npart=128, out=None, outb=None):
            if out is None:
                out, outb = rstd, rstdb
            ACT(lnb_t[0:npart, :], ps_ap, AF.Ln, [ps_buf], [lnbb], bias=EPS, scale=1.0 / nfeat)
            ACT(out[0:npart, :], lnb_t[0:npart, :], AF.Exp, [lnbb], [outb], scale=-0.5)

        def rmsnorm_tile(t0, gain_col, hT, hbuf, hcol0, pbank):
            tile_i = t0 // 512
            ps, psb = bank(pbank)
            for c in range(8):
                ACT(sqt[:, c % 2, :], xT[:, c, t0:t0 + 512], AF.Square, [xb[tile_i]], [sqtb[c % 2]])
                MM(ps, ones[:], sqt[:, c % 2, :], c == 0, c == 7, [sqtb[c % 2], constb], [psb])
            rms_rstd(ps, psb, D)
            for c in range(8):
                STT(hT[:, c, hcol0:hcol0 + 512], xT[:, c, t0:t0 + 512], prm[:, gain_col + c:gain_col + c + 1],
                    rstd[:], ALU.mult, ALU.mult, [xb[tile_i], prmb, rstdb], [hbuf])

        def ffn_layer(l):
            with scope() as st:
                hT = sb("f_hT", [128, 8, S], BF16, st); hb = [Buf('f_h%d' % i) for i in range(4)]
                uT = sb("f_uT", [128, NF, 1024], BF16, st); ub = [Buf('f_u%d' % f) for f in range(NF)]
                halo = sb("f_halo", [128, 2, NF, 2], F32, st); halob = [[Buf('f_halo%d_%d' % (a, f)) for f in range(NF)] for a in range(2)]
                cbuf = sb("f_c", [128, 2, 512], F32, st); cb = [Buf('f_c0'), Buf('f_c1')]
                sbuf = sb("f_s", [128, 2, 512], F32, st); sbb = [Buf('f_s0'), Buf('f_s1')]
                sq8 = sb("f_sq8", [128, 8, 512], BF16, st); sq8b = Buf('f_sq8')
                rota = Rot([0, 1, 2]); rotg = Rot([3, 4, 5]); roty = Rot([6, 7])
                step = 0
                cw = lambda k, f: prm[:, P_CW + (l * 3 + k) * NF + f:P_CW + (l * 3 + k) * NF + f + 1]
                cbias = lambda f: prm[:, P_CB + l * NF + f:P_CB + l * NF + f + 1]
                gcol = P_FFNN + 8 * l

                def norm_a(ti, cs=range(8)):
                    for c in cs:
                        ACT(sq8[:, c, :], xT[:, c, ti * 512:(ti + 1) * 512], AF.Square, [xb[ti]], [sq8b])

                def norm_b(ti):
                    ps, psb = bank(6 + ti % 2)
                    for c in range(8):
                        MM(ps, ones[:], sq8[:, c, :], c == 0, c == 7, [sq8b, constb], [psb], signal=(c == 7))
                    rms_rstd(ps, psb, D)

                def norm_c(ti, cs=range(8)):
                    for c in cs:
                        STT(hT[:, c, ti * 512:(ti + 1) * 512], xT[:, c, ti * 512:(ti + 1) * 512], prm[:, gcol + c:gcol + c + 1],
                            rstd[:], ALU.mult, ALU.mult, [xb[ti], prmb, rstdb], [hb[ti]])

                fscr = sb("f_scr", [128, 2], F32, st); fscrb = Buf('f_scr')
                ACT(fscr[:, 0:1], ones[:, 0:1], AF.Ln, [constb], [fscrb])
                for ti in range(2):
                    ACT(sq8[:], xT[:, :, ti * 512:(ti + 1) * 512], AF.Square, [xb[ti]], [sq8b])
                    norm_b(ti); norm_c(ti)
                for T in range(2):
                    for f in range(NF):
                        if T == 0:
                            if f <= 7:
                                norm_a(2, [f])
                            if f == 8:
                                norm_b(2)
                            if 9 <= f <= 16:
                                norm_c(2, [f - 9]); norm_a(3, [f - 9])
                            if f == 17:
                                norm_b(3)
                            if f >= 18:
                                norm_c(3, [2 * (f - 18), 2 * (f - 18) + 1])
                        wt, wbuf = wtile('f%di%d' % (l, f))
                        for tt in range(2):
                            pa, pab = bank(rota()); pg, pgb = bank(rotg())
                            hs = slice(T * 1024 + tt * 512, T * 1024 + (tt + 1) * 512)
                            us = slice(tt * 512, (tt + 1) * 512)
                            hbi = hb[T * 2 + tt]
                            for c in range(8):
                                MM(pa, wt[:, c, 0:128], hT[:, c, hs], c == 0, c == 7, [wbuf, hbi], [pab], signal=(c == 7))
                            for c in range(8):
                                MM(pg, wt[:, c, 128:256], hT[:, c, hs], c == 0, c == 7, [wbuf, hbi], [pgb], signal=(c == 7))
                            k = step % 2; step += 1
                            cc = cbuf[:, k, :]
                            ACT(cc, pg, AF.Identity, [pgb, prmb], [cb[k]], scale=cw(2, f), bias=cbias(f))
                            hp = (T * 2 + tt) % 2
                            STT(cc[:, 1:512], pg[:, 0:511], cw(1, f), cc[:, 1:512], ALU.mult, ALU.add, [pgb, prmb, cb[k]], [cb[k]])
                            STT(cc[:, 2:512], pg[:, 0:510], cw(0, f), cc[:, 2:512], ALU.mult, ALU.add, [pgb, prmb, cb[k]], [cb[k]])
                            kb.op('dve', lambda e, hp=hp, f=f, pg=pg: e.tensor_copy(out=halo[:, hp, f, :], in_=pg[:, 510:512]),
                                  [pgb], [halob[hp][f]])
                            if not (T == 0 and tt == 0):
                                STT(cc[:, 0:1], halo[:, 1 - hp, f, 1:2], cw(1, f), cc[:, 0:1], ALU.mult, ALU.add, [halob[1 - hp][f], prmb, cb[k]], [cb[k]])
                                STT(cc[:, 0:2], halo[:, 1 - hp, f, 0:2], cw(0, f), cc[:, 0:2], ALU.mult, ALU.add, [halob[1 - hp][f], prmb, cb[k]], [cb[k]])
                            ACT(sbuf[:, k, :], cc, AF.Silu, [cb[k]], [sbb[k]])
                            TT(uT[:, f, us], sbuf[:, k, :], pa, ALU.mult, [sbb[k], pab], [ub[f]])
                    for d in range(8):
                        wa, wab = wtile('f%do%d_0' % (l, d)); wb_, wbb = wtile('f%do%d_1' % (l, d))
                        for tt in range(2):
                            py, pyb = bank(roty())
                            us = slice(tt * 512, (tt + 1) * 512)
                            ts = slice(T * 1024 + tt * 512, T * 1024 + (tt + 1) * 512)
                            for kk in range(NF):
                                w, wbf = (wa, wab) if kk < 11 else (wb_, wbb)
                                MM(py, w[:, kk % 11, :], uT[:, kk, us], kk == 0, kk == NF - 1, [wbf, ub[kk]], [pyb],
                                   signal=(kk == NF - 1))
                            xi = xb[T * 2 + tt]
                            TT(xT[:, d, ts], xT[:, d, ts], py, ALU.add, [xi, pyb], [xi])

        def ret_layer():
            with scope() as st:
                hT = sb("r_hT", [128, 8, S], BF16, st); hb = [Buf('r_h%d' % i) for i in range(4)]
                rope = sb("r_rope", [128, 2, S], F32, st); ropeb = Buf('r_rope')
                dt_ = sb("r_dt", [128, S], F32, st); dtb = Buf('r_dt')
                qT = sb("r_q", [128, 2, S], BF16, st); qb = [Buf('r_q%d' % i) for i in range(4)]
                kT = sb("r_k", [128, 2, S], BF16, st); kbf = [Buf('r_k%d' % i) for i in range(4)]
                vt = sb("r_v", [128, 16, 512], BF16, st); vb = [Buf('r_v%d' % i) for i in range(16)]
                t12 = sb("r_t", [128, 2, 512], F32, st); tb_ = [Buf('r_t0'), Buf('r_t1')]
                P = sb("r_P", [128, 2, 512], BF16, st); Pb = [Buf('r_P0'), Buf('r_P1')]
                sg = sb("r_sg", [128, 2, 4, 512], BF16, st); sgb = [Buf('r_sg0'), Buf('r_sg1')]
                sqn = sb("r_sqn", [128, 4, 512], BF16, st); sqnv = [Buf('r_sqn%d' % i) for i in range(4)]
                wv_ = sb("r_w", [128, 512], F32, st); wvb = Buf('r_w')
                u = sb("r_u", [128, 2, 4, 512], BF16, st); ub = [Buf('r_u0'), Buf('r_u1')]
                dq = Defer()
                scr = sb("r_scr", [128, 2], F32, st); scrb = Buf('r_scr')
                kb.dma('sp', rope[:], rrope_d, writes=[ropeb])
                for tt in range(4):
                    rmsnorm_tile(tt * 512, P_RETN, hT, hb[tt], tt * 512, 6 + tt % 2)
                rotp = Rot([0, 1, 2, 3, 4, 5])
                rots = Rot([4, 5]); rotx = Rot([6, 7])
                pcnt = [0]
                gw = {}

                def gproj(h, it, par, groups, banks=None):
                    evs = []
                    if (h, it) not in gw:
                        gw.clear()
                        gw[(h, it)] = (wtile('rg%da' % h), wtile('rg%db' % h))
                    (wga, wgab), (wgb_, wgbb) = gw[(h, it)]
                    i0 = it * 512
                    for gc in groups:
                        w, wbf = (wga, wgab) if gc < 2 else (wgb_, wgbb)
                        bi = rotx() if banks is None else banks[gc]
                        pg, pgb = bank(bi)
                        for c in range(8):
                            MM(pg, w[:, c, (gc % 2) * 128:(gc % 2 + 1) * 128], hT[:, c, i0:i0 + 512], c == 0, c == 7,
                               [wbf, hb[it]], [pgb], signal=(c == 7))
                        evs.append(lambda pg=pg, pgb=pgb, gc=gc: ACT(sg[:, par, gc, :], pg, AF.Silu, [pgb], [sgb[par]]))
                    return evs

                tiles = [(h, it) for h in range(4) for it in range(4)]
                for h in range(4):
                    kb.dma('sp', dt_[:], dtab_d[h], writes=[dtb])
                    for nm, dst, dstb in (('rq%d' % h, qT, qb), ('rk%d' % h, kT, kbf)):
                        w, wbf = wtile(nm)
                        for tt in range(4):
                            ts = slice(tt * 512, (tt + 1) * 512)
                            p1, p1b = bank(rotp()); p2, p2b = bank(rotp())
                            for c in range(8):
                                MM(p1, w[:, c, 0:128], hT[:, c, ts], c == 0, c == 7, [wbf, hb[tt]], [p1b], signal=(c == 7))
                            for c in range(8):
                                MM(p2, w[:, c, 128:256], hT[:, c, ts], c == 0, c == 7, [wbf, hb[tt]], [p2b], signal=(c == 7))
                            cs = rope[:, 0, ts]; sn = rope[:, 1, ts]
                            TT(t12[:, 0, :], p1, cs, ALU.mult, [p1b, ropeb], [tb_[0]])
                            TT(t12[:, 1, :], p2, sn, ALU.mult, [p2b, ropeb], [tb_[1]])
                            TT(dst[:, 0, ts], t12[:, 0, :], t12[:, 1, :], ALU.subtract, [tb_[0], tb_[1]], [dstb[tt]])
                            TT(t12[:, 0, :], p2, cs, ALU.mult, [p2b, ropeb], [tb_[0]])
                            TT(t12[:, 1, :], p1, sn, ALU.mult, [p1b, ropeb], [tb_[1]])
                            TT(dst[:, 1, ts], t12[:, 0, :], t12[:, 1, :], ALU.add, [tb_[0], tb_[1]], [dstb[tt]])
                            dq.tick()
                    dq.flush()
                    wva, wvab = wtile('rv%da' % h); wvb_, wvbb = wtile('rv%db' % h)
                    for tb in range(16):
                        pv, pvb = bank(rotp())
                        for half, (w, wbf) in enumerate(((wva, wvab), (wvb_, wvbb))):
                            for c in range(8):
                                MM(pv[:, half * 256:(half + 1) * 256], hT[:, c, tb * 128:(tb + 1) * 128], w[:, c, :],
                                   c == 0, c == 7, [wbf, hb[tb // 4]], [pvb], signal=(c == 7))
                        ACT(vt[:, tb, :], pv, AF.Copy, [pvb], [vb[tb]])
                    if h == 0:
                        for gc in range(4):
                            for ev in gproj(0, 0, 0, [gc]):
                                ev()
                    for it in range(4):
                        g = h * 4 + it; par = g % 2
                        i0 = it * 512
                        njb = 4 * it + 4

                        def emit_S(jb):
                            c0 = max(jb * 128 - i0, 0); N = 512 - c0; ilo = i0 + c0
                            bi = rots()
                            ps, psb = bank(bi)
                            for c in range(2):
                                MM(ps[:, 0:N], kT[:, c, jb * 128:(jb + 1) * 128], qT[:, c, ilo:ilo + N], c == 0, c == 1,
                                   [kbf[jb // 4], qb[it]], [psb], signal=(c == 1))
                            return ps, psb

                        nxt = emit_S(0)
                        for jb in range(njb):
                            ps, psb = nxt
                            if jb + 1 < njb:
                                nxt = emit_S(jb + 1)
                            c0 = max(jb * 128 - i0, 0); N = 512 - c0; ilo = i0 + c0
                            k = pcnt[0] % 2; pcnt[0] += 1
                            doff = ilo - jb * 128
                            TT(P[:, k, 0:N], ps[:, 0:N], dt_[:, doff:doff + N], ALU.mult, [psb, dtb], [Pb[k]])
                            if jb == njb - 2:
                                ACT(scr[:, 0:1], ones[:, 0:1], AF.Ln, [constb], [scrb])
                            dq.tick()
                            for vc in range(4):
                                MM(psum[:, vc, c0:c0 + N], vt[:, jb, vc * 128:(vc + 1) * 128], P[:, k, 0:N], jb == 0, jb == njb - 1,
                                   [vb[jb], Pb[k]], [pb[vc]], signal=(vc == 3))
                        dq.flush(maxtag=g - 2)
                        nx = tiles[g + 1] if g + 1 < 16 else None
                        gb = {0: 6, 1: 4, 2: 5, 3: 6}
                        evs = gproj(nx[0], nx[1], 1 - par, [0], gb) if nx else []
                        pss, pssb = bank(7)
                        for vc in range(4):
                            ACT(sqn[:, vc, :], psum[:, vc, :], AF.Square, [pb[vc]], [sqnv[vc]])
                        for vc in range(4):
                            MM(pss, ones[:], sqn[:, vc, :], vc == 0, vc == 3, [sqnv[vc], constb], [pssb], signal=(vc == 3))
                        rms_rstd(pss, pssb, 512)
                        if nx:
                            evs += gproj(nx[0], nx[1], 1 - par, [1], gb)
                            evs += gproj(nx[0], nx[1], 1 - par, [2], gb)
                        for ev in evs:
                            ev()
                        for vc in range(4):
                            TT(wv_[:], sg[:, par, vc, :], rstd[:], ALU.mult, [sgb[par], rstdb], [wvb])
                            gcol = P_RETGN + h * 4 + vc
                            STT(u[:, par, vc, :], psum[:, vc, :], prm[:, gcol:gcol + 1], wv_[:], ALU.mult, ALU.mult,
                                [pb[vc], prmb, wvb], [ub[par]])
                        if nx:
                            for ev in gproj(nx[0], nx[1], 1 - par, [3], gb):
                                ev()
                        dq.tick()
                        wo = {}

                        def ychunk(d, h=h, i0=i0, it=it, par=par, wo=wo):
                            key = 'a' if d < 4 else 'b'
                            if key not in wo:
                                wo[key] = wtile('ro%d%s' % (h, key))
                            w, wbf = wo[key]
                            py, pyb = bank(rotx())
                            for vc in range(4):
                                MM(py, w[:, vc, (d % 4) * 128:(d % 4 + 1) * 128], u[:, par, vc, :], vc == 0, vc == 3,
                                   [wbf, ub[par]], [pyb], signal=(vc == 3))
                            TT(xT[:, d, i0:i0 + 512], xT[:, d, i0:i0 + 512], py, ALU.add, [xb[it], pyb], [xb[it]])
                        late = (4 * nx[1] + 4 + 1) if (nx and nx[0] == h) else 4
                        for d in range(8):
                            dq.add(1 + d // 2 if d < 6 else late, (lambda d=d, f=ychunk: f(d)), tag=g)
                dq.flush()

        def mla_layer():
            with scope() as so:
                cqn = sb("m_cqn", [128, 3, S], BF16, so); cqb = [Buf('m_cq%d' % i) for i in range(4)]
                ckvn = sb("m_ckvn", [128, 2, S], BF16, so); ckb = [Buf('m_ck%d' % i) for i in range(4)]
                KrT = sb("m_kr", [128, S], BF16, so); krb = [Buf('m_kr%d' % i) for i in range(4)]
                sqkr = sb("m_sqkr", [128, S], BF16, so); sqkrb = [Buf('m_sqkr%d' % i) for i in range(4)]
                mrope = sb("m_rope", [128, 2, S], F32, so); mropeb = Buf('m_rope')
                rstdk = sb("m_rstdk", [128, 16, 4], F32, so); rstdkb = Buf('m_rstdk')
                rtk = sb("m_rtk", [128, 16, 4], F32, so); rtkb = Buf('m_rtk')
                xg = sb("m_xg", [128, 512], BF16, so); xgb = Buf('m_xg')
                t12 = sb("m_t", [128, 2, 512], F32, so); tb_ = [Buf('m_t0'), Buf('m_t1')]
                kb.dma('sp', mrope[:], mrope_d, writes=[mropeb])

                def rope64_mm(src_ap, src_buf, pbank):
                    pr, prb = bank(pbank)
                    MM(pr, rotb[:], src_ap, True, True, [constb, src_buf], [prb])
                    return pr, prb

                def rope64_ew(src_ap, src_buf, pr, prb, dst_ap, dst_buf, ts):
                    TT(t12[:, 0, :], src_ap, mrope[:, 0, ts], ALU.mult, [src_buf, mropeb], [tb_[0]])
                    TT(t12[:, 1, :], pr, mrope[:, 1, ts], ALU.mult, [prb, mropeb], [tb_[1]])
                    TT(dst_ap, t12[:, 0, :], t12[:, 1, :], ALU.add, [tb_[0], tb_[1]], [dst_buf])

                with scope() as s1:
                    hT = sb("m_hT", [128, 8, S], BF16, s1); hb = [Buf('m_h%d' % i) for i in range(4)]
                    sq3 = sb("m_sq3", [128, 5, 512], BF16, s1); sq3b = Buf('m_sq3'); sq3c = Buf('m_sq3c')
                    rstd2 = sb("m_rstd2", [128, 512], F32, s1); rstd2b = Buf('m_rstd2')
                    for tt in range(4):
                        rmsnorm_tile(tt * 512, P_MLAN, hT, hb[tt], tt * 512, 6 + tt % 2)
                    wi = [wtile('mi0'), wtile('mi1'), wtile('mi2')]
                    for tt in range(4):
                        ts = slice(tt * 512, (tt + 1) * 512)
                        def proj(fcs):
                            for fc in fcs:
                                w, wbf = wi[fc // 2]; col = (fc % 2) * 128
                                for c in range(8):
                                    MM(psum[:, fc, :], w[:, c, col:col + 128], hT[:, c, ts], c == 0, c == 7, [wbf, hb[tt]], [pb[fc]],
                                       signal=(c == 7))
                        proj(range(0, 3))
                        ACT(sq3[:, 0:3, :], psum[:, 0:3, :], AF.Square, [pb[0], pb[1], pb[2]], [sq3b])
                        proj(range(3, 6))
                        ACT(sq3[:, 3:5, :], psum[:, 3:5, :], AF.Square, [pb[3], pb[4]], [sq3c])
                        pss, pssb = bank(6)
                        for k in range(3):
                            MM(pss, ones[:], sq3[:, k, :], k == 0, k == 2, [sq3b, constb], [pssb], signal=(k == 2))
                        rms_rstd(pss, pssb, 384)
                        ACT(sqkr[:, ts], psum[:, 5, :], AF.Square, [pb[5]], [sqkrb[tt]])
                        ACT(xg[:], psum[:, 5, :], AF.Identity, [pb[5], prmb], [xgb], scale=prm[:, P_KHR:P_KHR + 1])
                        for k in range(3):
                            STT(cqn[:, k, ts], psum[:, k, :], prm[:, P_QN + k:P_QN + k + 1], rstd[:], ALU.mult, ALU.mult,
                                [pb[k], prmb, rstdb], [cqb[tt]])
                        pss, pssb = bank(7)
                        for k in range(2):
                            MM(pss, ones[:], sq3[:, 3 + k, :], k == 0, k == 1, [sq3c, constb], [pssb], signal=(k == 1))
                        rms_rstd(pss, pssb, 256, out=rstd2, outb=rstd2b)
                        for k in range(2):
                            STT(ckvn[:, k, ts], psum[:, 3 + k, :], prm[:, P_KVN + k:P_KVN + k + 1], rstd2[:], ALU.mult, ALU.mult,
                                [pb[3 + k], prmb, rstd2b], [ckb[tt]])
                        pr, prb = rope64_mm(xg[:], xgb, 6)
                        rope64_ew(xg[:], xgb, pr, prb, KrT[:, ts], krb[tt], ts)

                for G in range(2):
                    with scope() as s2:
                        KnT = sb("m_kn", [128, 4, S], BF16, s2); knb = [[Buf('m_kn%d_%d' % (a, i)) for i in range(4)] for a in range(4)]
                        Vt = sb("m_v", [128, 16, 512], BF16, s2); vb = [Buf('m_v%d' % i) for i in range(16)]
                        QnT = sb("m_qn", [128, 2, S], BF16, s2); qnb = [[Buf('m_qn%d_%d' % (a, i)) for i in range(4)] for a in range(2)]
                        QrT = sb("m_qr", [128, 2, S], BF16, s2); qrb = [[Buf('m_qr%d_%d' % (a, i)) for i in range(4)] for a in range(2)]
                        onT = sb("m_on", [128, 2, S], BF16, s2); onb = [[Buf('m_on%d_%d' % (a, i)) for i in range(4)] for a in range(2)]
                        sqk = sb("m_sqk", [128, 2, 512], BF16, s2); sqkb = [Buf('m_sqk0'), Buf('m_sqk1')]
                        sq2 = sb("m_sq2", [128, 512], BF16, s2); sq2b = Buf('m_sq2')
                        P = sb("m_P", [128, 3, 512], BF16, s2); Pb = [Buf('m_P0'), Buf('m_P1'), Buf('m_P2')]
                        rden, rdenb = rstd, rstdb
                        rstdq = sb("m_rstdq", [128, 512], F32, s2); rstdqb = Buf('m_rstdq')
                        Osb = sb("m_osb", [128, 512], F32, s2); Osbb = Buf('m_osb')
                        lnd = sb("m_lnd", [128, 512], F32, s2); lndb = Buf('m_lnd')
                        sqq = sb("m_sqq", [128, 512], BF16, s2); sqqb = Buf('m_sqq')
                        dq = Defer()
                        def qchain(h, tt, qp):
                            ts = slice(tt * 512, (tt + 1) * 512)
                            wq, wqb_ = wtile('mq%d' % h); cb0 = 0
                            pqn, pqnb = bank(5); pqr, pqrb = bank(6)
                            for c in range(3):
                                MM(pqn, wq[:, c, cb0:cb0 + 128], cqn[:, c, ts], c == 0, c == 2, [wqb_, cqb[tt]], [pqnb], signal=(c == 2))
                            for c in range(3):
                                MM(pqr, wq[:, c, cb0 + 128:cb0 + 256], cqn[:, c, ts], c == 0, c == 2, [wqb_, cqb[tt]], [pqrb],
                                   signal=(c == 2))
                            ACT(sqq[:], pqn, AF.Square, [pqnb], [sqqb])
                            dq.add(1, lambda: ACT(sq2[:], pqr, AF.Square, [pqrb], [sq2b]))

                            pssh = [None]

                            def stB():
                                pss, pssb = bank(7)
                                MM(pss, ones[:], sqq[:], True, False, [sqqb, constb], [pssb])
                                MM(pss, ones[:], sq2[:], False, True, [sq2b, constb], [pssb])
                                ACT(lnb_t[:], pss, AF.Ln, [pssb], [lnbb], bias=EPS, scale=1.0 / 192)

                            def stB2():
                                ACT(rstdq[:], lnb_t[:], AF.Exp, [lnbb], [rstdqb], scale=-0.5)

                            def stC():
                                STT(QnT[:, qp, ts], pqn, prm[:, P_QHN:P_QHN + 1], rstdq[:], ALU.mult, ALU.mult,
                                    [pqnb, prmb, rstdqb], [qnb[qp][tt]])
                                STT(xg[:], pqr, prm[:, P_QHR:P_QHR + 1], rstdq[:], ALU.mult, ALU.mult,
                                    [pqrb, prmb, rstdqb], [xgb])

                            def stD():
                                pr, prb = rope64_mm(xg[:], xgb, 7)
                                dq.add(2, lambda: rope64_ew(xg[:], xgb, pr, prb, QrT[:, qp, ts], qrb[qp][tt], ts))
                            dq.add(3, stB)
                            dq.add(4, stB2)
                            dq.add(6, stC)
                            dq.add(7, stD)

                        rotp = Rot([2, 3, 4])
                        wkn, wknb = wtile('mkn')
                        pssk, psskb = bank(1)
                        nsq = 0
                        kticks = [0]

                        def ktick():
                            if kticks[0] % 8 == 0 and kticks[0] // 8 < 4:
                                qchain(4 * G, kticks[0] // 8, 0)
                            kticks[0] += 1
                            dq.tick()

                        prev = None
                        for q4 in range(4):
                            ACT(Osb[:, q4 * 128:(q4 + 1) * 128], ones[:], AF.Identity, [constb, prmb], [Osbb], scale=prm[:, P_KHN:P_KHN + 1])
                        for hl in range(4):
                            h = 4 * G + hl
                            for tt in range(4):
                                ts = slice(tt * 512, (tt + 1) * 512)
                                pk, pkb = bank(rotp())
                                for c in range(2):
                                    MM(pk, wkn[:, c, h * 128:(h + 1) * 128], ckvn[:, c, ts], c == 0, c == 1, [wknb, ckb[tt]], [pkb],
                                       signal=(c == 1))
                                k = nsq % 2; nsq += 1
                                ACT(sqk[:, k, :], pk, AF.Square, [pkb], [sqkb[k]])

                                def tiny(hl=hl, tt=tt, k=k, pk=pk, pkb=pkb, ts=ts):
                                    TT(KnT[:, hl, ts], pk, Osb[:], ALU.mult, [pkb, Osbb, sqkb[k]], [knb[hl][tt]])
                                    for b in range(4):
                                        tbk = tt * 4 + b
                                        col = tbk * 4 + hl
                                        MM(pssk[:, col:col + 1], sqk[:, k, b * 128:(b + 1) * 128], ones[:, 0:1], True, False,
                                           [sqkb[k], constb], [psskb])
                                        MM(pssk[:, col:col + 1], sqkr[:, tbk * 128:(tbk + 1) * 128], ones[:, 0:1], False, True,
                                           [sqkrb[tt], constb], [psskb])
                                if prev is not None:
                                    prev()
                                prev = tiny
                                ktick()
                        prev()
                        f2 = lambda a: a[:].rearrange("p a b -> p (a b)")
                        ACT(f2(rtk), pssk[:, 0:64], AF.Ln, [psskb], [rtkb], bias=EPS, scale=1.0 / 192)
                        ACT(f2(rstdk), f2(rtk), AF.Exp, [rtkb], [rstdkb], scale=-0.5)
                        kb.op('dve', lambda e: e.tensor_scalar(out=f2(rstdk), in0=f2(rstdk),
                                                               scalar1=float(192.0 ** -0.5), scalar2=None, op0=ALU.mult),
                              [rstdkb], [rstdkb])
                        wv, wvb = wtile('mv')
                        for tbk in range(16):
                            pv, pvb = bank(rotp())
                            for c in range(2):
                                MM(pv, ckvn[:, c, tbk * 128:(tbk + 1) * 128], wv[:, c, G * 512:(G + 1) * 512], c == 0, c == 1,
                                   [wvb, ckb[tbk // 4]], [pvb], signal=(c == 1))
                            ACT(Vt[:, tbk, :], pv, AF.Copy, [pvb], [vb[tbk]])
                            ktick()

                        dq.flush()
                        rots = Rot([2, 3, 4]); roty = Rot([5, 6, 7])
                        pcnt = 0
                        for hl in range(4):
                            h = 4 * G + hl; qp = hl % 2
                            hstep = 0
                            for it in range(4):
                                i0 = it * 512
                                njb = 4 * it + 4

                                def emit_S(jb, i0=i0, it=it):
                                    c0 = max(jb * 128 - i0, 0); N = 512 - c0; ilo = i0 + c0
                                    ps, psb = bank(rots())
                                    MM(ps[:, 0:N], KnT[:, hl, jb * 128:(jb + 1) * 128], QnT[:, qp, ilo:ilo + N], True, False,
                                       [knb[hl][jb // 4], qnb[qp][it]], [psb], signal=False)
                                    diag = jb >= 4 * it
                                    MM(ps[:, 0:N], KrT[:, jb * 128:(jb + 1) * 128], QrT[:, qp, ilo:ilo + N], False, not diag,
                                       [krb[jb // 4], qrb[qp][it]], [psb], signal=not diag)
                                    if diag:
                                        MM(ps[:, 0:128], identb[:], negmb[:], False, True, [constb], [psb])
                                    return ps, psb
                                pend = [emit_S(0), emit_S(1)]
                                for jb in range(njb):
                                    ps, psb = pend.pop(0)
                                    if jb + 2 < njb:
                                        pend.append(emit_S(jb + 2))
                                    c0 = max(jb * 128 - i0, 0); N = 512 - c0
                                    k = pcnt % 3; pcnt += 1
                                    ACT(P[:, k, 0:N], ps[:, 0:N], AF.Exp, [psb, rstdkb], [Pb[k]], scale=rstdk[:, jb, hl:hl + 1])
                                    if hl + 1 < 4 and hstep % 10 == 0:
                                        qchain(h + 1, hstep // 10, 1 - qp)
                                    hstep += 1
                                    dq.tick()
                                    MM(psum[:, 0, c0:c0 + N], Vt[:, jb, hl * 128:(hl + 1) * 128], P[:, k, 0:N], jb == 0, jb == njb - 1,
                                       [vb[jb], Pb[k]], [pb[0]], signal=False)
                                    MM(psum[:, 1, c0:c0 + N], ones[:], P[:, k, 0:N], jb == 0, jb == njb - 1, [constb, Pb[k]], [pb[1]])
                                kb.op('dve', lambda e: e.tensor_copy(out=Osb[:], in_=psum[:, 0, :]), [pb[0]], [Osbb])
                                ACT(lnd[:], psum[:, 1, :], AF.Ln, [pb[1]], [lndb])

                                def fin(qp=qp, i0=i0, it=it):
                                    ACT(rden[:], lnd[:], AF.Exp, [lndb], [rdenb], scale=-1.0)
                                    TT(onT[:, qp, i0:i0 + 512], Osb[:], rden[:], ALU.mult, [Osbb, rdenb], [onb[qp][it]])
                                dq.add(1, fin)
                            dq.flush()
                            if hl % 2 == 1:
                                wo0 = wtile('mo%d' % (h - 1)); wo1 = wtile('mo%d' % h)
                                for it in range(4):
                                    i0 = it * 512
                                    for d in range(8):
                                        py, pyb = bank(roty())
                                        MM(py, wo0[0][:, 0, d * 128:(d + 1) * 128], onT[:, 0, i0:i0 + 512], True, False,
                                           [wo0[1], onb[0][it]], [pyb], signal=False)
                                        MM(py, wo1[0][:, 0, d * 128:(d + 1) * 128], onT[:, 1, i0:i0 + 512], False, True,
                                           [wo1[1], onb[1][it]], [pyb])
                                        TT(xT[:, d, i0:i0 + 512], xT[:, d, i0:i0 + 512], py, ALU.add, [xb[it], pyb], [xb[it]])

        tmpst = ExitStack()
        miscf = sb("miscf", [128, 512], F32, tmpst); miscb = Buf('misc')
        kb.dma('sp', miscf[:], misc_d, writes=[miscb])
        kb.op('dve', lambda e: e.tensor_copy(out=identb[:], in_=miscf[:, 256:384]), reads=[miscb], writes=[constb])
        kb.op('dve', lambda e: e.tensor_copy(out=negmb[:], in_=miscf[:, 384:512]), reads=[miscb], writes=[constb])
        kb.op('dve', lambda e: e.tensor_copy(out=rotb[:], in_=miscf[:, 128:256]), reads=[miscb], writes=[constb])
        kb.barrier()
        tmpst.close()

        for s in range(nseq):
            for tt in range(4):
                ts = slice(tt * 512, (tt + 1) * 512)
                kb.dma('sp', xT[:, :, ts], xin[s][:, :, ts], writes=[xb[tt]])
            for ly in layers:
                if ly == 'ret':
                    ret_layer()
                elif ly == 'mla':
                    mla_layer()
                elif ly == 'ffn0':
                    ffn_layer(0)
                elif ly == 'ffn1':
                    ffn_layer(1)
            for tt in range(4):
                ts = slice(tt * 512, (tt + 1) * 512)
                kb.dma('sp', xout[s][:, :, ts], xT[:, :, ts], reads=[xb[tt]], writes=[ob[tt]])
        kb.wait_all('sp', ob)
        kb.wait_all('act', ob)
    return nc


_PROG = {}


def _get_prog(nseq, layers):
    key = (nseq, tuple(layers))
    if key not in _PROG:
        _PROG[key] = build_program(nseq, layers)
    return _PROG[key]


def kernel(**inp):
    inp = {k: np.asarray(v) for k, v in inp.items()}
    x = inp['x'].astype(np.float32, copy=False)
    B = x.shape[0]
    nseq = B // NC8
    xl = np.ascontiguousarray(x.reshape(B, S, 8, 128).transpose(0, 3, 2, 1))
    wts = pack_weights(inp)
    prm = pack_params(inp)
    rrope, mrope, dtab, misc = const_tables()
    nc = _get_prog(nseq, ('ret', 'ffn0', 'mla', 'ffn1'))
    in_maps = []
    for c in range(NC8):
        in_maps.append({"xin": xl[c * nseq:(c + 1) * nseq], "wts": wts, "prm": prm, "rrope": rrope,
                        "mrope": mrope, "dtab": dtab, "misc": misc})
    res = run_bass_kernel_spmd(nc, in_maps, core_ids=list(range(NC8)))
    outs = [np.asarray(r["xout"]) for r in res.results]
    o = np.concatenate(outs, axis=0)
    return np.ascontiguousarray(o.transpose(0, 3, 2, 1).reshape(B, S, D)).astype(np.float32, copy=False)
```

```python
import numpy as np
from contextlib import ExitStack, contextmanager
import concourse.bass as bass
import concourse.mybir as mybir
from concourse.bass_utils import run_bass_kernel_spmd

F32 = mybir.dt.float32
BF16 = mybir.dt.bfloat16
AF = mybir.ActivationFunctionType
ALU = mybir.AluOpType

D = 1024
S = 2048
NC8 = 8
EPS = 1e-6
THETA = 10000.0
FFN = 2816
NF = FFN // 128
RING = 4
SLOT = 2048


def _kt(w):
    K, C = w.shape
    return np.ascontiguousarray(w.reshape(K // 128, 128, C).transpose(1, 0, 2))


def _col(v):
    n = v.shape[0] // 128
    return np.ascontiguousarray(v.reshape(n, 128).T)


class WPack:
    def __init__(self):
        self.parts = []
        self.index = {}
        self.off = 0

    def add(self, name, arr3):
        a = np.ascontiguousarray(arr3, dtype=np.float32)
        assert a.shape[0] == 128
        n = int(np.prod(a.shape[1:]))
        assert n <= SLOT, (name, a.shape)
        self.index[name] = (self.off, tuple(a.shape[1:]))
        self.parts.append(a.reshape(-1))
        self.off += 128 * n


def weight_index():
    idx = {}
    off = 0

    def add(name, kc, cols):
        nonlocal off
        idx[name] = (off, (kc, cols))
        off += 128 * kc * cols
    for h in range(4):
        add('rq%d' % h, 8, 256); add('rk%d' % h, 8, 256)
        add('rv%da' % h, 8, 256); add('rv%db' % h, 8, 256)
        add('rg%da' % h, 8, 256); add('rg%db' % h, 8, 256)
        add('ro%da' % h, 4, 512); add('ro%db' % h, 4, 512)
    for l in range(2):
        for f in range(NF):
            add('f%di%d' % (l, f), 8, 256)
        for d in range(8):
            add('f%do%d_0' % (l, d), 11, 128); add('f%do%d_1' % (l, d), 11, 128)
    add('mi0', 8, 256); add('mi1', 8, 256); add('mi2', 8, 256)
    for h in range(8):
        add('mq%d' % h, 3, 256)
    add('mkn', 2, 1024); add('mv', 2, 1024)
    for h in range(8):
        add('mo%d' % h, 1, 1024)
    return idx, off


def pack_weights(inp):
    wp = WPack()
    rwi = inp['ret_w_in'][0]; rwo = inp['ret_w_out'][0]
    for h in range(4):
        wp.add('rq%d' % h, _kt(rwi[:, h * 256:(h + 1) * 256]))
        wp.add('rk%d' % h, _kt(rwi[:, 1024 + h * 256:1024 + (h + 1) * 256]))
        wp.add('rv%da' % h, _kt(rwi[:, 2048 + h * 512:2048 + h * 512 + 256]))
        wp.add('rv%db' % h, _kt(rwi[:, 2048 + h * 512 + 256:2048 + (h + 1) * 512]))
        wp.add('rg%da' % h, _kt(rwi[:, 4096 + h * 512:4096 + h * 512 + 256]))
        wp.add('rg%db' % h, _kt(rwi[:, 4096 + h * 512 + 256:4096 + (h + 1) * 512]))
        wp.add('ro%da' % h, _kt(rwo[h * 512:(h + 1) * 512, 0:512]))
        wp.add('ro%db' % h, _kt(rwo[h * 512:(h + 1) * 512, 512:1024]))
    for l in range(2):
        wi = inp['ffn_w_in'][l]; wo = inp['ffn_w_out'][l]
        for f in range(NF):
            wp.add('f%di%d' % (l, f), _kt(np.concatenate(
                [wi[:, f * 128:(f + 1) * 128], wi[:, FFN + f * 128:FFN + (f + 1) * 128]], axis=1)))
        for d in range(8):
            wp.add('f%do%d_0' % (l, d), _kt(wo[0:1408, d * 128:(d + 1) * 128]))
            wp.add('f%do%d_1' % (l, d), _kt(wo[1408:2816, d * 128:(d + 1) * 128]))
    mwi = inp['mla_w_in'][0]
    wp.add('mi0', _kt(mwi[:, 0:256])); wp.add('mi1', _kt(mwi[:, 256:512]))
    wp.add('mi2', _kt(np.concatenate([mwi[:, 512:704], np.zeros((1024, 64), np.float32)], axis=1)))
    wqb = inp['mla_w_qb'][0]
    for h in range(8):
        wp.add('mq%d' % h, _kt(np.concatenate([wqb[:, h * 192:(h + 1) * 192], np.zeros((384, 64), np.float32)], axis=1)))
    wkvb = inp['mla_w_kvb'][0].reshape(256, 8, 256)
    wp.add('mkn', _kt(np.ascontiguousarray(wkvb[:, :, 0:128]).reshape(256, 1024)))
    wp.add('mv', _kt(np.ascontiguousarray(wkvb[:, :, 128:256]).reshape(256, 1024)))
    mwo = inp['mla_w_out'][0]
    for h in range(8):
        wp.add('mo%d' % h, _kt(mwo[h * 128:(h + 1) * 128, :]))
    idx, tot = weight_index()
    assert tot == wp.off
    for k in idx:
        assert idx[k] == wp.index[k], k
    return np.concatenate(wp.parts)


P_RETN, P_RETGN, P_MLAN, P_QN, P_KVN, P_QHN, P_QHR, P_KHN, P_KHR, P_FFNN, P_CW, P_CB = (
    0, 8, 24, 32, 35, 37, 38, 39, 40, 41, 57, 57 + 132)
NPRM = 57 + 132 + 44


def pack_params(inp):
    p = np.zeros((128, NPRM), np.float32)
    p[:, P_RETN:P_RETN + 8] = _col(inp['ret_norm'][0])
    p[:, P_RETGN:P_RETGN + 16] = _col(inp['ret_gn'][0].reshape(-1))
    p[:, P_MLAN:P_MLAN + 8] = _col(inp['mla_norm'][0])
    p[:, P_QN:P_QN + 3] = _col(inp['mla_q_norm'][0])
    p[:, P_KVN:P_KVN + 2] = _col(inp['mla_kv_norm'][0])
    p[:, P_QHN] = inp['mla_q_head_norm'][0][0:128]
    p[0:64, P_QHR] = inp['mla_q_head_norm'][0][128:192]
    p[:, P_KHN] = inp['mla_k_head_norm'][0][0:128]
    p[0:64, P_KHR] = inp['mla_k_head_norm'][0][128:192]
    for l in range(2):
        p[:, P_FFNN + 8 * l:P_FFNN + 8 * l + 8] = _col(inp['ffn_norm'][l])
        for k in range(3):
            p[:, P_CW + (l * 3 + k) * NF:P_CW + (l * 3 + k + 1) * NF] = _col(inp['ffn_conv_w'][l, k])
        p[:, P_CB + l * NF:P_CB + (l + 1) * NF] = _col(inp['ffn_conv_b'][l])
    return p


def const_tables():
    pos = np.arange(S, dtype=np.float64)
    inv = THETA ** (-np.arange(128, dtype=np.float64) / 128.0)
    ang = inv[:, None] * pos[None, :]
    rrope = np.stack([np.cos(ang), np.sin(ang)], axis=1).astype(np.float32)
    inv = THETA ** (-np.arange(32, dtype=np.float64) / 32.0)
    ang = np.concatenate([inv, inv])[:, None] * pos[None, :]
    mrope = np.zeros((128, 2, S), np.float32)
    mrope[0:64] = np.stack([np.cos(ang), np.sin(ang)], axis=1).astype(np.float32)
    dtab = np.zeros((4, 128, S), np.float64)
    p = np.arange(128)[:, None]
    m = np.arange(S)[None, :]
    for h in range(4):
        lg = np.log1p(-2.0 ** (-5.0 - h))
        t = np.exp(lg * (m - p).astype(np.float64))
        md = np.arange(128)[None, :]
        allowed = (p // 64) <= (md // 64)
        t[:, 0:128] = np.where(allowed, np.exp(lg * np.abs(md - p)), 0.0)
        dtab[h] = t * (256.0 ** -0.5)
    dtab = dtab.astype(np.float32)
    misc = np.zeros((128, 512), np.float32)
    misc[:, 0:128] = ((p // 64) <= (np.arange(128)[None, :] // 64)).astype(np.float32)
    for i in range(32):
        misc[32 + i, 128 + i] = -1.0
        misc[i, 128 + 32 + i] = 1.0
    misc[:, 256:384] = np.eye(128, dtype=np.float32)
    misc[:, 384:512] = np.where(misc[:, 0:128] > 0, 0.0, -30000.0)
    return rrope, mrope, dtab, misc


class Buf:
    __slots__ = ('name', 'w', 'r', 'dsem', 'dcnt')

    def __init__(self, name):
        self.name = name; self.w = None; self.r = {}; self.dsem = None; self.dcnt = 0


class KB:
    def __init__(self, nc, es):
        self.nc = nc; self.es = es
        self.eng = {'pe': nc.tensor, 'act': nc.scalar, 'dve': nc.vector, 'pool': nc.gpsimd, 'sp': nc.sync}
        self.semh = {}
        for e in self.eng:
            self.semh[e] = es.enter_context(nc.semaphore('sem_' + e))
        self.cnt = {e: 0 for e in self.eng}
        self.pend = {e: False for e in self.eng}
        self.seen = {e: {} for e in self.eng}
        self.bar = {}
        self.nops = 0

    def _waits(self, e, reads, writes):
        deps = {}

        def add(tok):
            if tok is None:
                return
            k, v = tok
            if deps.get(k, 0) < v:
                deps[k] = v
        for b in reads:
            add(b.w)
        for b in writes:
            add(b.w)
            for t in b.r.values():
                add(t)
        if e != 'pool':
            for k, v in self.bar.items():
                add((k, v))
        eng = self.eng[e]; seen = self.seen[e]
        for k, v in deps.items():
            if k == e and e == 'pe':
                continue
            if seen.get(k, 0) >= v:
                continue
            if k in self.cnt:
                assert v <= self.cnt[k], ('future dependency', e, k, v, self.cnt[k])
            eng.wait_ge(self.semh[k], v)
            seen[k] = v

    def op(self, e, fn, reads=(), writes=(), signal=True):
        self._waits(e, reads, writes)
        ins = fn(self.eng[e])
        self.nops += 1
        if signal:
            self.cnt[e] += 1
            ins.then_inc(self.semh[e], 1)
            tok = (e, self.cnt[e]); self.pend[e] = False
        else:
            tok = (e, self.cnt[e] + 1); self.pend[e] = True
        for b in reads:
            b.r[e] = tok
        for b in writes:
            b.w = tok; b.r = {}
        return ins

    def dma(self, e, out, in_, reads=(), writes=(), **kw):
        tgt = writes[0]
        if tgt.dsem is None:
            self.nsem = getattr(self, 'nsem', 0) + 1
            tgt.dsem = 'd%d_%s' % (self.nsem, tgt.name)
            self.semh[tgt.dsem] = self.es.enter_context(self.nc.semaphore(tgt.dsem))
        self._waits(e, reads, writes)
        ins = self.eng[e].dma_start(out=out, in_=in_, **kw)
        tgt.dcnt += 16
        ins.then_inc(self.semh[tgt.dsem], 16)
        tok = (tgt.dsem, tgt.dcnt)
        for b in reads:
            b.r[tgt.dsem] = tok
        for b in writes:
            b.w = tok; b.r = {}
        return ins

    def barrier(self):
        for e in ('pe', 'act', 'dve'):
            assert not self.pend[e], e
            self.bar[e] = self.cnt[e]

    def wait_all(self, e, bufs):
        self._waits(e, bufs, ())


class Defer:
    def __init__(self):
        self.q = []; self.t = 0; self.n = 0

    def add(self, delay, fn, tag=0):
        self.n += 1
        self.q.append((self.t + delay, self.n, tag, fn))

    def tick(self):
        self.t += 1
        due = sorted([x for x in self.q if x[0] <= self.t], key=lambda x: (x[0], x[1]))
        self.q = [x for x in self.q if x[0] > self.t]
        for x in due:
            x[3]()

    def flush(self, maxtag=None):
        while True:
            sel = sorted([x for x in self.q if maxtag is None or x[2] <= maxtag], key=lambda x: (x[0], x[1]))
            if not sel:
                return
            self.q = [x for x in self.q if not (maxtag is None or x[2] <= maxtag)]
            for x in sel:
                x[3]()


class Rot:
    def __init__(self, items):
        self.items = list(items); self.i = 0

    def __call__(self):
        x = self.items[self.i % len(self.items)]; self.i += 1
        return x


def build_program(nseq=2, layers=('ret', 'ffn0', 'mla', 'ffn1')):
    nc = bass.Bass("TRN2", target_bir_lowering=False)
    widx, wtot = weight_index()
    xin = nc.dram_tensor("xin", [nseq, 128, 8, S], F32, kind="ExternalInput").ap()
    wts = nc.dram_tensor("wts", [wtot], F32, kind="ExternalInput").ap()
    prm_d = nc.dram_tensor("prm", [128, NPRM], F32, kind="ExternalInput").ap()
    rrope_d = nc.dram_tensor("rrope", [128, 2, S], F32, kind="ExternalInput").ap()
    mrope_d = nc.dram_tensor("mrope", [128, 2, S], F32, kind="ExternalInput").ap()
    dtab_d = nc.dram_tensor("dtab", [4, 128, S], F32, kind="ExternalInput").ap()
    misc_d = nc.dram_tensor("misc", [128, 512], F32, kind="ExternalInput").ap()
    xout = nc.dram_tensor("xout", [nseq, 128, 8, S], F32, kind="ExternalOutput").ap()

    with ExitStack() as es:
        kb = KB(nc, es)

        uid = [0]

        def sb(name, shape, dt, stack=es):
            uid[0] += 1
            return stack.enter_context(nc.sbuf_tensor("%s_%d" % (name, uid[0]), shape, dt))

        xT = sb("xT", [128, 8, S], F32)
        xb = [Buf('x%d' % i) for i in range(4)]
        ob = [Buf('o%d' % i) for i in range(4)]
        ring = sb("ring", [128, RING, SLOT], BF16)
        ringb = [Buf('ring%d' % i) for i in range(RING)]
        prm = sb("prm_s", [128, NPRM], F32); prmb = Buf('prm')
        identb = sb("identb", [128, 128], BF16)
        negmb = sb("negmb", [128, 128], BF16)
        ones = sb("ones", [128, 128], BF16); constb = Buf('const')
        psum = es.enter_context(nc.psum_tensor("psum", [128, 8, 512], F32))
        pb = [Buf('ps%d' % i) for i in range(8)]

        kb.dma('sp', prm[:], prm_d, writes=[prmb])
        kb.op('dve', lambda e: e.memset(ones[:], 1.0), writes=[constb])
        rotb = sb("rotb", [128, 128], BF16)

        ring_i = [0]

        def wtile(name):
            off, (kc, cols) = widx[name]
            n = kc * cols
            slot = ring_i[0] % RING; ring_i[0] += 1
            src = wts[off:off + 128 * n].rearrange("(p n) -> p n", p=128)
            kb.dma('pool', ring[:, slot, 0:n], src, writes=[ringb[slot]], max_dma_last_dim=8192)
            return ring[:, slot, 0:n].rearrange("p (k c) -> p k c", k=kc), ringb[slot]

        @contextmanager
        def scope():
            with ExitStack() as st:
                yield st
                kb.barrier()

        def MM(out, lhsT, rhs, start, stop, reads, writes, signal=True):
            return kb.op('pe', lambda e: e.matmul(out, lhsT, rhs, start=start, stop=stop), reads, writes, signal)

        def ACT(out, in_, func, reads, writes, **kw):
            return kb.op('act', lambda e: e.activation(out=out, in_=in_, func=func, **kw), reads, writes)

        def TT(out, a, b, op, reads, writes):
            return kb.op('dve', lambda e: e.tensor_tensor(out=out, in0=a, in1=b, op=op), reads, writes)

        def STT(out, in0, scalar, in1, op0, op1, reads, writes):
            return kb.op('dve', lambda e: e.scalar_tensor_tensor(out=out, in0=in0, scalar=scalar, in1=in1,
                                                                 op0=op0, op1=op1), reads, writes)

        def RECIP(out, in_, reads, writes):
            return kb.op('dve', lambda e: e.reciprocal(out=out, in_=in_), reads, writes)

        def bank(i):
            return psum[:, i, :], pb[i]

        sqt = sb("sqt", [128, 2, 512], BF16); sqtb = [Buf('sqt0'), Buf('sqt1')]
        rstd = sb("rstd", [128, 512], F32); rstdb = Buf('rstd')

        lnb_t = sb("lnb", [128, 512], F32); lnbb = Buf('lnb')

        def rms_rstd(ps_ap, ps_buf, nfeat, npart=128, out=None, outb=None):
            if out is None:
                out, outb = rstd, rstdb
            ACT(lnb_t[0:npart, :], ps_ap, AF.Ln, [ps_buf], [lnbb], bias=EPS, scale=1.0 / nfeat)
            ACT(out[0:npart, :], lnb_t[0:npart, :], AF.Exp, [lnbb], [outb], scale=-0.5)

        def rmsnorm_tile(t0, gain_col, hT, hbuf, hcol0, pbank):
            tile_i = t0 // 512
            ps, psb = bank(pbank)
            for c in range(8):
                ACT(sqt[:, c % 2, :], xT[:, c, t0:t0 + 512], AF.Square, [xb[tile_i]], [sqtb[c % 2]])
                MM(ps, ones[:], sqt[:, c % 2, :], c == 0, c == 7, [sqtb[c % 2], constb], [psb])
            rms_rstd(ps, psb, D)
            for c in range(8):
                STT(hT[:, c, hcol0:hcol0 + 512], xT[:, c, t0:t0 + 512], prm[:, gain_col + c:gain_col + c + 1],
                    rstd[:], ALU.mult, ALU.mult, [xb[tile_i], prmb, rstdb], [hbuf])

        def ffn_layer(l):
            with scope() as st:
                hT = sb("f_hT", [128, 8, S], BF16, st); hb = [[Buf('f_h%d_%d' % (i, c)) for c in range(8)] for i in range(4)]
                uT = sb("f_uT", [128, NF, 1024], BF16, st); ub = [Buf('f_u%d' % f) for f in range(NF)]
                halo = sb("f_halo", [128, 2, NF, 2], F32, st); halob = [[Buf('f_halo%d_%d' % (a, f)) for f in range(NF)] for a in range(2)]
                cbuf = sb("f_c", [128, 2, 512], F32, st); cb = [Buf('f_c0'), Buf('f_c1')]
                sbuf = sb("f_s", [128, 2, 512], F32, st); sbb = [Buf('f_s0'), Buf('f_s1')]
                sq8 = sb("f_sq8", [128, 8, 512], BF16, st); sq8b = Buf('f_sq8')
                rota = Rot([0, 1, 2]); rotg = Rot([3, 4, 5]); roty = Rot([6, 7])
                step = 0
                cw = lambda k, f: prm[:, P_CW + (l * 3 + k) * NF + f:P_CW + (l * 3 + k) * NF + f + 1]
                cbias = lambda f: prm[:, P_CB + l * NF + f:P_CB + l * NF + f + 1]
                gcol = P_FFNN + 8 * l

                def norm_a(ti, cs=range(8)):
                    for c in cs:
                        ACT(sq8[:, c, :], xT[:, c, ti * 512:(ti + 1) * 512], AF.Square, [xb[ti]], [sq8b])

                def norm_b(ti):
                    ps, psb = bank(6 + ti % 2)
                    for c in range(8):
                        MM(ps, ones[:], sq8[:, c, :], c == 0, c == 7, [sq8b, constb], [psb], signal=(c == 7))
                    rms_rstd(ps, psb, D)

                def norm_c(ti, cs=range(8)):
                    for c in cs:
                        STT(hT[:, c, ti * 512:(ti + 1) * 512], xT[:, c, ti * 512:(ti + 1) * 512], prm[:, gcol + c:gcol + c + 1],
                            rstd[:], ALU.mult, ALU.mult, [xb[ti], prmb, rstdb], [hb[ti][c]])

                fscr = sb("f_scr", [128, 2], F32, st); fscrb = Buf('f_scr')
                ACT(fscr[:, 0:1], ones[:, 0:1], AF.Ln, [constb], [fscrb])
                for ti in range(2):
                    ACT(sq8[:], xT[:, :, ti * 512:(ti + 1) * 512], AF.Square, [xb[ti]], [sq8b])
                    norm_b(ti); norm_c(ti)
                for T in range(2):
                    for f in range(NF):
                        if T == 0:
                            if f <= 7:
                                norm_a(2, [f])
                            if f == 8:
                                norm_b(2)
                            if 9 <= f <= 16:
                                norm_c(2, [f - 9]); norm_a(3, [f - 9])
                            if f == 17:
                                norm_b(3)
                            if f >= 18:
                                norm_c(3, [2 * (f - 18), 2 * (f - 18) + 1])
                        wt, wbuf = wtile('f%di%d' % (l, f))
                        for tt in range(2):
                            pa, pab = bank(rota()); pg, pgb = bank(rotg())
                            hs = slice(T * 1024 + tt * 512, T * 1024 + (tt + 1) * 512)
                            us = slice(tt * 512, (tt + 1) * 512)
                            hbi = hb[T * 2 + tt]
                            for c in range(8):
                                MM(pa, wt[:, c, 0:128], hT[:, c, hs], c == 0, c == 7, [wbuf, hbi[c]], [pab], signal=(c == 7))
                            for c in range(8):
                                MM(pg, wt[:, c, 128:256], hT[:, c, hs], c == 0, c == 7, [wbuf, hbi[c]], [pgb], signal=(c == 7))
                            k = step % 2; step += 1
                            cc = cbuf[:, k, :]
                            ACT(cc, pg, AF.Identity, [pgb, prmb], [cb[k]], scale=cw(2, f), bias=cbias(f))
                            hp = (T * 2 + tt) % 2
                            STT(cc[:, 1:512], pg[:, 0:511], cw(1, f), cc[:, 1:512], ALU.mult, ALU.add, [pgb, prmb, cb[k]], [cb[k]])
                            STT(cc[:, 2:512], pg[:, 0:510], cw(0, f), cc[:, 2:512], ALU.mult, ALU.add, [pgb, prmb, cb[k]], [cb[k]])
                            kb.op('dve', lambda e, hp=hp, f=f, pg=pg: e.tensor_copy(out=halo[:, hp, f, :], in_=pg[:, 510:512]),
                                  [pgb], [halob[hp][f]])
                            if not (T == 0 and tt == 0):
                                STT(cc[:, 0:1], halo[:, 1 - hp, f, 1:2], cw(1, f), cc[:, 0:1], ALU.mult, ALU.add, [halob[1 - hp][f], prmb, cb[k]], [cb[k]])
                                STT(cc[:, 0:2], halo[:, 1 - hp, f, 0:2], cw(0, f), cc[:, 0:2], ALU.mult, ALU.add, [halob[1 - hp][f], prmb, cb[k]], [cb[k]])
                            ACT(sbuf[:, k, :], cc, AF.Silu, [cb[k]], [sbb[k]])
                            TT(uT[:, f, us], sbuf[:, k, :], pa, ALU.mult, [sbb[k], pab], [ub[f]])
                    for d in range(8):
                        wa, wab = wtile('f%do%d_0' % (l, d)); wb_, wbb = wtile('f%do%d_1' % (l, d))
                        for tt in range(2):
                            py, pyb = bank(roty())
                            us = slice(tt * 512, (tt + 1) * 512)
                            ts = slice(T * 1024 + tt * 512, T * 1024 + (tt + 1) * 512)
                            for kk in range(NF):
                                w, wbf = (wa, wab) if kk < 11 else (wb_, wbb)
                                MM(py, w[:, kk % 11, :], uT[:, kk, us], kk == 0, kk == NF - 1, [wbf, ub[kk]], [pyb],
                                   signal=(kk == NF - 1))
                            xi = xb[T * 2 + tt]
                            TT(xT[:, d, ts], xT[:, d, ts], py, ALU.add, [xi, pyb], [xi])

        def ret_layer():
            with scope() as st:
                hT = sb("r_hT", [128, 8, S], BF16, st); hb = [Buf('r_h%d' % i) for i in range(4)]
                rope = sb("r_rope", [128, 2, S], F32, st); ropeb = Buf('r_rope')
                dt_ = sb("r_dt", [128, S], F32, st); dtb = Buf('r_dt')
                qT = sb("r_q", [128, 2, S], BF16, st); qb = [Buf('r_q%d' % i) for i in range(4)]
                kT = sb("r_k", [128, 2, S], BF16, st); kbf = [Buf('r_k%d' % i) for i in range(4)]
                vt = sb("r_v", [128, 16, 512], BF16, st); vb = [Buf('r_v%d' % i) for i in range(16)]
                t12 = sb("r_t", [128, 2, 512], F32, st); tb_ = [Buf('r_t0'), Buf('r_t1')]
                P = sb("r_P", [128, 2, 512], BF16, st); Pb = [Buf('r_P0'), Buf('r_P1')]
                sg = sb("r_sg", [128, 2, 4, 512], BF16, st); sgb = [Buf('r_sg0'), Buf('r_sg1')]
                sqn = sb("r_sqn", [128, 4, 512], BF16, st); sqnv = [Buf('r_sqn%d' % i) for i in range(4)]
                wv_ = sb("r_w", [128, 512], F32, st); wvb = Buf('r_w')
                u = sb("r_u", [128, 2, 4, 512], BF16, st); ub = [Buf('r_u0'), Buf('r_u1')]
                dq = Defer()
                scr = sb("r_scr", [128, 2], F32, st); scrb = Buf('r_scr')
                kb.dma('sp', rope[:], rrope_d, writes=[ropeb])
                for tt in range(4):
                    rmsnorm_tile(tt * 512, P_RETN, hT, hb[tt], tt * 512, 6 + tt % 2)
                rotp = Rot([0, 1, 2, 3, 4, 5])
                rots = Rot([4, 5]); rotx = Rot([6, 7])
                pcnt = [0]
                gw = {}

                def gproj(h, it, par, groups, banks=None):
                    evs = []
                    if (h, it) not in gw:
                        gw.clear()
                        gw[(h, it)] = (wtile('rg%da' % h), wtile('rg%db' % h))
                    (wga, wgab), (wgb_, wgbb) = gw[(h, it)]
                    i0 = it * 512
                    for gc in groups:
                        w, wbf = (wga, wgab) if gc < 2 else (wgb_, wgbb)
                        bi = rotx() if banks is None else banks[gc]
                        pg, pgb = bank(bi)
                        for c in range(8):
                            MM(pg, w[:, c, (gc % 2) * 128:(gc % 2 + 1) * 128], hT[:, c, i0:i0 + 512], c == 0, c == 7,
                               [wbf, hb[it]], [pgb], signal=(c == 7))
                        evs.append(lambda pg=pg, pgb=pgb, gc=gc: ACT(sg[:, par, gc, :], pg, AF.Silu, [pgb], [sgb[par]]))
                    return evs

                tiles = [(h, it) for h in range(4) for it in range(4)]
                for h in range(4):
                    kb.dma('sp', dt_[:], dtab_d[h], writes=[dtb])
                    for nm, dst, dstb in (('rq%d' % h, qT, qb), ('rk%d' % h, kT, kbf)):
                        w, wbf = wtile(nm)
                        for tt in range(4):
                            ts = slice(tt * 512, (tt + 1) * 512)
                            p1, p1b = bank(rotp()); p2, p2b = bank(rotp())
                            for c in range(8):
                                MM(p1, w[:, c, 0:128], hT[:, c, ts], c == 0, c == 7, [wbf, hb[tt]], [p1b], signal=(c == 7))
                            for c in range(8):
                                MM(p2, w[:, c, 128:256], hT[:, c, ts], c == 0, c == 7, [wbf, hb[tt]], [p2b], signal=(c == 7))
                            cs = rope[:, 0, ts]; sn = rope[:, 1, ts]
                            TT(t12[:, 0, :], p1, cs, ALU.mult, [p1b, ropeb], [tb_[0]])
                            TT(t12[:, 1, :], p2, sn, ALU.mult, [p2b, ropeb], [tb_[1]])
                            TT(dst[:, 0, ts], t12[:, 0, :], t12[:, 1, :], ALU.subtract, [tb_[0], tb_[1]], [dstb[tt]])
                            TT(t12[:, 0, :], p2, cs, ALU.mult, [p2b, ropeb], [tb_[0]])
                            TT(t12[:, 1, :], p1, sn, ALU.mult, [p1b, ropeb], [tb_[1]])
                            TT(dst[:, 1, ts], t12[:, 0, :], t12[:, 1, :], ALU.add, [tb_[0], tb_[1]], [dstb[tt]])
                            dq.tick()
                    dq.flush()
                    wva, wvab = wtile('rv%da' % h); wvb_, wvbb = wtile('rv%db' % h)
                    for tb in range(16):
                        pv, pvb = bank(rotp())
                        for half, (w, wbf) in enumerate(((wva, wvab), (wvb_, wvbb))):
                            for c in range(8):
                                MM(pv[:, half * 256:(half + 1) * 256], hT[:, c, tb * 128:(tb + 1) * 128], w[:, c, :],
                                   c == 0, c == 7, [wbf, hb[tb // 4]], [pvb], signal=(c == 7))
                        ACT(vt[:, tb, :], pv, AF.Copy, [pvb], [vb[tb]])
                    if h == 0:
                        for gc in range(4):
                            for ev in gproj(0, 0, 0, [gc]):
                                ev()
                    for it in range(4):
                        g = h * 4 + it; par = g % 2
                        i0 = it * 512
                        njb = 4 * it + 4

                        def emit_S(jb):
                            c0 = max(jb * 128 - i0, 0); N = 512 - c0; ilo = i0 + c0
                            bi = rots()
                            ps, psb = bank(bi)
                            for c in range(2):
                                MM(ps[:, 0:N], kT[:, c, jb * 128:(jb + 1) * 128], qT[:, c, ilo:ilo + N], c == 0, c == 1,
                                   [kbf[jb // 4], qb[it]], [psb], signal=(c == 1))
                            return ps, psb

                        nxt = emit_S(0)
                        for jb in range(njb):
                            ps, psb = nxt
                            if jb + 1 < njb:
                                nxt = emit_S(jb + 1)
                            c0 = max(jb * 128 - i0, 0); N = 512 - c0; ilo = i0 + c0
                            k = pcnt[0] % 2; pcnt[0] += 1
                            doff = ilo - jb * 128
                            TT(P[:, k, 0:N], ps[:, 0:N], dt_[:, doff:doff + N], ALU.mult, [psb, dtb], [Pb[k]])
                            if jb == njb - 2:
                                ACT(scr[:, 0:1], ones[:, 0:1], AF.Ln, [constb], [scrb])
                            dq.tick()
                            for vc in range(4):
                                MM(psum[:, vc, c0:c0 + N], vt[:, jb, vc * 128:(vc + 1) * 128], P[:, k, 0:N], jb == 0, jb == njb - 1,
                                   [vb[jb], Pb[k]], [pb[vc]], signal=(vc == 3))
                        dq.flush(maxtag=g - 2)
                        nx = tiles[g + 1] if g + 1 < 16 else None
                        gb = {0: 6, 1: 4, 2: 5, 3: 6}
                        evs = gproj(nx[0], nx[1], 1 - par, [0], gb) if nx else []
                        pss, pssb = bank(7)
                        for vc in range(4):
                            ACT(sqn[:, vc, :], psum[:, vc, :], AF.Square, [pb[vc]], [sqnv[vc]])
                        for vc in range(4):
                            MM(pss, ones[:], sqn[:, vc, :], vc == 0, vc == 3, [sqnv[vc], constb], [pssb], signal=(vc == 3))
                        rms_rstd(pss, pssb, 512)
                        if nx:
                            evs += gproj(nx[0], nx[1], 1 - par, [1], gb)
                            evs += gproj(nx[0], nx[1], 1 - par, [2], gb)
                        for ev in evs:
                            ev()
                        for vc in range(4):
                            TT(wv_[:], sg[:, par, vc, :], rstd[:], ALU.mult, [sgb[par], rstdb], [wvb])
                            gcol = P_RETGN + h * 4 + vc
                            STT(u[:, par, vc, :], psum[:, vc, :], prm[:, gcol:gcol + 1], wv_[:], ALU.mult, ALU.mult,
                                [pb[vc], prmb, wvb], [ub[par]])
                        if nx:
                            for ev in gproj(nx[0], nx[1], 1 - par, [3], gb):
                                ev()
                        dq.tick()
                        wo = {}

                        def ychunk(d, h=h, i0=i0, it=it, par=par, wo=wo):
                            key = 'a' if d < 4 else 'b'
                            if key not in wo:
                                wo[key] = wtile('ro%d%s' % (h, key))
                            w, wbf = wo[key]
                            py, pyb = bank(rotx())
                            for vc in range(4):
                                MM(py, w[:, vc, (d % 4) * 128:(d % 4 + 1) * 128], u[:, par, vc, :], vc == 0, vc == 3,
                                   [wbf, ub[par]], [pyb], signal=(vc == 3))
                            TT(xT[:, d, i0:i0 + 512], xT[:, d, i0:i0 + 512], py, ALU.add, [xb[it], pyb], [xb[it]])
                        late = (4 * nx[1] + 4 + 1) if (nx and nx[0] == h) else 4
                        for d in range(8):
                            dq.add(1 + d // 2 if d < 6 else late, (lambda d=d, f=ychunk: f(d)), tag=g)
                dq.flush()

        def mla_layer():
            with scope() as so:
                cqn = sb("m_cqn", [128, 3, S], BF16, so); cqb = [Buf('m_cq%d' % i) for i in range(4)]
                ckvn = sb("m_ckvn", [128, 2, S], BF16, so); ckb = [Buf('m_ck%d' % i) for i in range(4)]
                KrT = sb("m_kr", [128, S], BF16, so); krb = [Buf('m_kr%d' % i) for i in range(4)]
                sqkr = sb("m_sqkr", [128, S], BF16, so); sqkrb = [Buf('m_sqkr%d' % i) for i in range(4)]
                mrope = sb("m_rope", [128, 2, S], F32, so); mropeb = Buf('m_rope')
                rstdk = sb("m_rstdk", [128, 16, 4], F32, so); rstdkb = Buf('m_rstdk')
                rtk = sb("m_rtk", [128, 16, 4], F32, so); rtkb = Buf('m_rtk')
                xg = sb("m_xg", [128, 512], BF16, so); xgb = Buf('m_xg')
                t12 = sb("m_t", [128, 2, 512], F32, so); tb_ = [Buf('m_t0'), Buf('m_t1')]
                kb.dma('sp', mrope[:], mrope_d, writes=[mropeb])

                def rope64_mm(src_ap, src_buf, pbank):
                    pr, prb = bank(pbank)
                    MM(pr, rotb[:], src_ap, True, True, [constb, src_buf], [prb])
                    return pr, prb

                def rope64_ew(src_ap, src_buf, pr, prb, dst_ap, dst_buf, ts):
                    TT(t12[:, 0, :], src_ap, mrope[:, 0, ts], ALU.mult, [src_buf, mropeb], [tb_[0]])
                    TT(t12[:, 1, :], pr, mrope[:, 1, ts], ALU.mult, [prb, mropeb], [tb_[1]])
                    TT(dst_ap, t12[:, 0, :], t12[:, 1, :], ALU.add, [tb_[0], tb_[1]], [dst_buf])

                with scope() as s1:
                    hT = sb("m_hT", [128, 8, S], BF16, s1); hb = [Buf('m_h%d' % i) for i in range(4)]
                    sq3 = sb("m_sq3", [128, 5, 512], BF16, s1); sq3b = Buf('m_sq3'); sq3c = Buf('m_sq3c')
                    rstd2 = sb("m_rstd2", [128, 512], F32, s1); rstd2b = Buf('m_rstd2')
                    for tt in range(4):
                        rmsnorm_tile(tt * 512, P_MLAN, hT, hb[tt], tt * 512, 6 + tt % 2)
                    wi = [wtile('mi0'), wtile('mi1'), wtile('mi2')]
                    for tt in range(4):
                        ts = slice(tt * 512, (tt + 1) * 512)
                        def proj(fcs):
                            for fc in fcs:
                                w, wbf = wi[fc // 2]; col = (fc % 2) * 128
                                for c in range(8):
                                    MM(psum[:, fc, :], w[:, c, col:col + 128], hT[:, c, ts], c == 0, c == 7, [wbf, hb[tt]], [pb[fc]],
                                       signal=(c == 7))
                        proj(range(0, 3))
                        ACT(sq3[:, 0:3, :], psum[:, 0:3, :], AF.Square, [pb[0], pb[1], pb[2]], [sq3b])
                        proj(range(3, 6))
                        ACT(sq3[:, 3:5, :], psum[:, 3:5, :], AF.Square, [pb[3], pb[4]], [sq3c])
                        pss, pssb = bank(6)
                        for k in range(3):
                            MM(pss, ones[:], sq3[:, k, :], k == 0, k == 2, [sq3b, constb], [pssb], signal=(k == 2))
                        rms_rstd(pss, pssb, 384)
                        ACT(sqkr[:, ts], psum[:, 5, :], AF.Square, [pb[5]], [sqkrb[tt]])
                        ACT(xg[:], psum[:, 5, :], AF.Identity, [pb[5], prmb], [xgb], scale=prm[:, P_KHR:P_KHR + 1])
                        for k in range(3):
                            STT(cqn[:, k, ts], psum[:, k, :], prm[:, P_QN + k:P_QN + k + 1], rstd[:], ALU.mult, ALU.mult,
                                [pb[k], prmb, rstdb], [cqb[tt]])
                        pss, pssb = bank(7)
                        for k in range(2):
                            MM(pss, ones[:], sq3[:, 3 + k, :], k == 0, k == 1, [sq3c, constb], [pssb], signal=(k == 1))
                        rms_rstd(pss, pssb, 256, out=rstd2, outb=rstd2b)
                        for k in range(2):
                            STT(ckvn[:, k, ts], psum[:, 3 + k, :], prm[:, P_KVN + k:P_KVN + k + 1], rstd2[:], ALU.mult, ALU.mult,
                                [pb[3 + k], prmb, rstd2b], [ckb[tt]])
                        pr, prb = rope64_mm(xg[:], xgb, 6)
                        rope64_ew(xg[:], xgb, pr, prb, KrT[:, ts], krb[tt], ts)

                for G in range(2):
                    with scope() as s2:
                        KnT = sb("m_kn", [128, 4, S], BF16, s2); knb = [[Buf('m_kn%d_%d' % (a, i)) for i in range(4)] for a in range(4)]
                        Vt = sb("m_v", [128, 16, 512], BF16, s2); vb = [Buf('m_v%d' % i) for i in range(16)]
                        QnT = sb("m_qn", [128, 2, S], BF16, s2); qnb = [[Buf('m_qn%d_%d' % (a, i)) for i in range(4)] for a in range(2)]
                        QrT = sb("m_qr", [128, 2, S], BF16, s2); qrb = [[Buf('m_qr%d_%d' % (a, i)) for i in range(4)] for a in range(2)]
                        onT = sb("m_on", [128, 2, S], BF16, s2); onb = [[Buf('m_on%d_%d' % (a, i)) for i in range(4)] for a in range(2)]
                        sqk = sb("m_sqk", [128, 2, 512], BF16, s2); sqkb = [Buf('m_sqk0'), Buf('m_sqk1')]
                        sq2 = sb("m_sq2", [128, 512], BF16, s2); sq2b = Buf('m_sq2')
                        P = sb("m_P", [128, 3, 512], BF16, s2); Pb = [Buf('m_P0'), Buf('m_P1'), Buf('m_P2')]
                        rden, rdenb = rstd, rstdb
                        rstdq = sb("m_rstdq", [128, 512], F32, s2); rstdqb = Buf('m_rstdq')
                        Osb = sb("m_osb", [128, 512], F32, s2); Osbb = Buf('m_osb')
                        lnd = sb("m_lnd", [128, 512], F32, s2); lndb = Buf('m_lnd')
                        sqq = sb("m_sqq", [128, 512], BF16, s2); sqqb = Buf('m_sqq')
                        dq = Defer()
                        def qchain(h, tt, qp):
                            ts = slice(tt * 512, (tt + 1) * 512)
                            wq, wqb_ = wtile('mq%d' % h); cb0 = 0
                            pqn, pqnb = bank(5); pqr, pqrb = bank(6)
                            for c in range(3):
                                MM(pqn, wq[:, c, cb0:cb0 + 128], cqn[:, c, ts], c == 0, c == 2, [wqb_, cqb[tt]], [pqnb], signal=(c == 2))
                            for c in range(3):
                                MM(pqr, wq[:, c, cb0 + 128:cb0 + 256], cqn[:, c, ts], c == 0, c == 2, [wqb_, cqb[tt]], [pqrb],
                                   signal=(c == 2))
                            ACT(sqq[:], pqn, AF.Square, [pqnb], [sqqb])
                            dq.add(1, lambda: ACT(sq2[:], pqr, AF.Square, [pqrb], [sq2b]))

                            pssh = [None]

                            def stB():
                                pss, pssb = bank(7)
                                MM(pss, ones[:], sqq[:], True, False, [sqqb, constb], [pssb])
                                MM(pss, ones[:], sq2[:], False, True, [sq2b, constb], [pssb])
                                ACT(lnb_t[:], pss, AF.Ln, [pssb], [lnbb], bias=EPS, scale=1.0 / 192)

                            def stB2():
                                ACT(rstdq[:], lnb_t[:], AF.Exp, [lnbb], [rstdqb], scale=-0.5)

                            def stC():
                                STT(QnT[:, qp, ts], pqn, prm[:, P_QHN:P_QHN + 1], rstdq[:], ALU.mult, ALU.mult,
                                    [pqnb, prmb, rstdqb], [qnb[qp][tt]])
                                STT(xg[:], pqr, prm[:, P_QHR:P_QHR + 1], rstdq[:], ALU.mult, ALU.mult,
                                    [pqrb, prmb, rstdqb], [xgb])

                            def stD():
                                pr, prb = rope64_mm(xg[:], xgb, 7)
                                dq.add(2, lambda: rope64_ew(xg[:], xgb, pr, prb, QrT[:, qp, ts], qrb[qp][tt], ts))
                            dq.add(3, stB)
                            dq.add(4, stB2)
                            dq.add(6, stC)
                            dq.add(7, stD)

                        rotp = Rot([2, 3, 4])
                        wkn, wknb = wtile('mkn')
                        pssk, psskb = bank(1)
                        nsq = 0
                        kticks = [0]

                        def ktick():
                            if kticks[0] % 8 == 0 and kticks[0] // 8 < 4:
                                qchain(4 * G, kticks[0] // 8, 0)
                            kticks[0] += 1
                            dq.tick()

                        prev = None
                        for q4 in range(4):
                            ACT(Osb[:, q4 * 128:(q4 + 1) * 128], ones[:], AF.Identity, [constb, prmb], [Osbb], scale=prm[:, P_KHN:P_KHN + 1])
                        for hl in range(4):
                            h = 4 * G + hl
                            for tt in range(4):
                                ts = slice(tt * 512, (tt + 1) * 512)
                                pk, pkb = bank(rotp())
                                for c in range(2):
                                    MM(pk, wkn[:, c, h * 128:(h + 1) * 128], ckvn[:, c, ts], c == 0, c == 1, [wknb, ckb[tt]], [pkb],
                                       signal=(c == 1))
                                k = nsq % 2; nsq += 1
                                ACT(sqk[:, k, :], pk, AF.Square, [pkb], [sqkb[k]])

                                def tiny(hl=hl, tt=tt, k=k, pk=pk, pkb=pkb, ts=ts):
                                    TT(KnT[:, hl, ts], pk, Osb[:], ALU.mult, [pkb, Osbb, sqkb[k]], [knb[hl][tt]])
                                    for b in range(4):
                                        tbk = tt * 4 + b
                                        col = tbk * 4 + hl
                                        MM(pssk[:, col:col + 1], sqk[:, k, b * 128:(b + 1) * 128], ones[:, 0:1], True, False,
                                           [sqkb[k], constb], [psskb])
                                        MM(pssk[:, col:col + 1], sqkr[:, tbk * 128:(tbk + 1) * 128], ones[:, 0:1], False, True,
                                           [sqkrb[tt], constb], [psskb])
                                if prev is not None:
                                    prev()
                                prev = tiny
                                ktick()
                        prev()
                        f2 = lambda a: a[:].rearrange("p a b -> p (a b)")
                        ACT(f2(rtk), pssk[:, 0:64], AF.Ln, [psskb], [rtkb], bias=EPS, scale=1.0 / 192)
                        ACT(f2(rstdk), f2(rtk), AF.Exp, [rtkb], [rstdkb], scale=-0.5)
                        kb.op('dve', lambda e: e.tensor_scalar(out=f2(rstdk), in0=f2(rstdk),
                                                               scalar1=float(192.0 ** -0.5), scalar2=None, op0=ALU.mult),
                              [rstdkb], [rstdkb])
                        wv, wvb = wtile('mv')
                        for tbk in range(16):
                            pv, pvb = bank(rotp())
                            for c in range(2):
                                MM(pv, ckvn[:, c, tbk * 128:(tbk + 1) * 128], wv[:, c, G * 512:(G + 1) * 512], c == 0, c == 1,
                                   [wvb, ckb[tbk // 4]], [pvb], signal=(c == 1))
                            ACT(Vt[:, tbk, :], pv, AF.Copy, [pvb], [vb[tbk]])
                            ktick()

                        dq.flush()
                        rots = Rot([2, 3, 4]); roty = Rot([5, 6, 7])
                        pcnt = 0
                        for hl in range(4):
                            h = 4 * G + hl; qp = hl % 2
                            hstep = 0
                            for it in range(4):
                                i0 = it * 512
                                njb = 4 * it + 4

                                def emit_S(jb, i0=i0, it=it):
                                    c0 = max(jb * 128 - i0, 0); N = 512 - c0; ilo = i0 + c0
                                    ps, psb = bank(rots())
                                    MM(ps[:, 0:N], KnT[:, hl, jb * 128:(jb + 1) * 128], QnT[:, qp, ilo:ilo + N], True, False,
                                       [knb[hl][jb // 4], qnb[qp][it]], [psb], signal=False)
                                    diag = jb >= 4 * it
                                    MM(ps[:, 0:N], KrT[:, jb * 128:(jb + 1) * 128], QrT[:, qp, ilo:ilo + N], False, not diag,
                                       [krb[jb // 4], qrb[qp][it]], [psb], signal=not diag)
                                    if diag:
                                        MM(ps[:, 0:128], identb[:], negmb[:], False, True, [constb], [psb])
                                    return ps, psb
                                pend = [emit_S(0), emit_S(1)]
                                for jb in range(njb):
                                    ps, psb = pend.pop(0)
                                    if jb + 2 < njb:
                                        pend.append(emit_S(jb + 2))
                                    c0 = max(jb * 128 - i0, 0); N = 512 - c0
                                    k = pcnt % 3; pcnt += 1
                                    ACT(P[:, k, 0:N], ps[:, 0:N], AF.Exp, [psb, rstdkb], [Pb[k]], scale=rstdk[:, jb, hl:hl + 1])
                                    if hl + 1 < 4 and hstep % 10 == 0:
                                        qchain(h + 1, hstep // 10, 1 - qp)
                                    hstep += 1
                                    dq.tick()
                                    MM(psum[:, 0, c0:c0 + N], Vt[:, jb, hl * 128:(hl + 1) * 128], P[:, k, 0:N], jb == 0, jb == njb - 1,
                                       [vb[jb], Pb[k]], [pb[0]], signal=False)
                                    MM(psum[:, 1, c0:c0 + N], ones[:], P[:, k, 0:N], jb == 0, jb == njb - 1, [constb, Pb[k]], [pb[1]])
                                kb.op('dve', lambda e: e.tensor_copy(out=Osb[:], in_=psum[:, 0, :]), [pb[0]], [Osbb])
                                ACT(lnd[:], psum[:, 1, :], AF.Ln, [pb[1]], [lndb])

                                def fin(qp=qp, i0=i0, it=it):
                                    ACT(rden[:], lnd[:], AF.Exp, [lndb], [rdenb], scale=-1.0)
                                    TT(onT[:, qp, i0:i0 + 512], Osb[:], rden[:], ALU.mult, [Osbb, rdenb], [onb[qp][it]])
                                dq.add(1, fin)
                            dq.flush()
                            if hl % 2 == 1:
                                wo0 = wtile('mo%d' % (h - 1)); wo1 = wtile('mo%d' % h)
                                for it in range(4):
                                    i0 = it * 512
                                    for d in range(8):
                                        py, pyb = bank(roty())
                                        MM(py, wo0[0][:, 0, d * 128:(d + 1) * 128], onT[:, 0, i0:i0 + 512], True, False,
                                           [wo0[1], onb[0][it]], [pyb], signal=False)
                                        MM(py, wo1[0][:, 0, d * 128:(d + 1) * 128], onT[:, 1, i0:i0 + 512], False, True,
                                           [wo1[1], onb[1][it]], [pyb])
                                        TT(xT[:, d, i0:i0 + 512], xT[:, d, i0:i0 + 512], py, ALU.add, [xb[it], pyb], [xb[it]])

        tmpst = ExitStack()
        miscf = sb("miscf", [128, 512], F32, tmpst); miscb = Buf('misc')
        kb.dma('sp', miscf[:], misc_d, writes=[miscb])
        kb.op('dve', lambda e: e.tensor_copy(out=identb[:], in_=miscf[:, 256:384]), reads=[miscb], writes=[constb])
        kb.op('dve', lambda e: e.tensor_copy(out=negmb[:], in_=miscf[:, 384:512]), reads=[miscb], writes=[constb])
        kb.op('dve', lambda e: e.tensor_copy(out=rotb[:], in_=miscf[:, 128:256]), reads=[miscb], writes=[constb])
        kb.barrier()
        tmpst.close()

        for s in range(nseq):
            for tt in range(4):
                ts = slice(tt * 512, (tt + 1) * 512)
                kb.dma('sp', xT[:, :, ts], xin[s][:, :, ts], writes=[xb[tt]])
            for ly in layers:
                if ly == 'ret':
                    ret_layer()
                elif ly == 'mla':
                    mla_layer()
                elif ly == 'ffn0':
                    ffn_layer(0)
                elif ly == 'ffn1':
                    ffn_layer(1)
            for tt in range(4):
                ts = slice(tt * 512, (tt + 1) * 512)
                kb.dma('sp', xout[s][:, :, ts], xT[:, :, ts], reads=[xb[tt]], writes=[ob[tt]])
        kb.wait_all('sp', ob)
        kb.wait_all('act', ob)
    return nc


_PROG = {}


def _get_prog(nseq, layers):
    key = (nseq, tuple(layers))
    if key not in _PROG:
        _PROG[key] = build_program(nseq, layers)
    return _PROG[key]


def kernel(**inp):
    inp = {k: np.asarray(v) for k, v in inp.items()}
    x = inp['x'].astype(np.float32, copy=False)
    B = x.shape[0]
    nseq = B // NC8
    xl = np.ascontiguousarray(x.reshape(B, S, 8, 128).transpose(0, 3, 2, 1))
    wts = pack_weights(inp)
    prm = pack_params(inp)
    rrope, mrope, dtab, misc = const_tables()
    nc = _get_prog(nseq, ('ret', 'ffn0', 'mla', 'ffn1'))
    in_maps = []
    for c in range(NC8):
        in_maps.append({"xin": xl[c * nseq:(c + 1) * nseq], "wts": wts, "prm": prm, "rrope": rrope,
                        "mrope": mrope, "dtab": dtab, "misc": misc})
    res = run_bass_kernel_spmd(nc, in_maps, core_ids=list(range(NC8)))
    outs = [np.asarray(r["xout"]) for r in res.results]
    o = np.concatenate(outs, axis=0)
    return np.ascontiguousarray(o.transpose(0, 3, 2, 1).reshape(B, S, D)).astype(np.float32, copy=False)
```

```python
import numpy as np
from contextlib import ExitStack, contextmanager
import concourse.bass as bass
import concourse.mybir as mybir
from concourse.bass_utils import run_bass_kernel_spmd

F32 = mybir.dt.float32
BF16 = mybir.dt.bfloat16
AF = mybir.ActivationFunctionType
ALU = mybir.AluOpType

D = 1024
S = 2048
NC8 = 8
EPS = 1e-6
THETA = 10000.0
FFN = 2816
NF = FFN // 128
RING = 4
SLOT = 2048


def _kt(w):
    K, C = w.shape
    return np.ascontiguousarray(w.reshape(K // 128, 128, C).transpose(1, 0, 2))


def _col(v):
    n = v.shape[0] // 128
    return np.ascontiguousarray(v.reshape(n, 128).T)


class WPack:
    def __init__(self):
        self.parts = []
        self.index = {}
        self.off = 0

    def add(self, name, arr3):
        a = np.ascontiguousarray(arr3, dtype=np.float32)
        assert a.shape[0] == 128
        n = int(np.prod(a.shape[1:]))
        assert n <= SLOT, (name, a.shape)
        self.index[name] = (self.off, tuple(a.shape[1:]))
        self.parts.append(a.reshape(-1))
        self.off += 128 * n


def weight_index():
    idx = {}
    off = 0

    def add(name, kc, cols):
        nonlocal off
        idx[name] = (off, (kc, cols))
        off += 128 * kc * cols
    for h in range(4):
        add('rq%d' % h, 8, 256); add('rk%d' % h, 8, 256)
        add('rv%da' % h, 8, 256); add('rv%db' % h, 8, 256)
        add('rg%da' % h, 8, 256); add('rg%db' % h, 8, 256)
        add('ro%da' % h, 4, 512); add('ro%db' % h, 4, 512)
    for l in range(2):
        for f in range(NF):
            add('f%di%d' % (l, f), 8, 256)
        for d in range(8):
            add('f%do%d_0' % (l, d), 11, 128); add('f%do%d_1' % (l, d), 11, 128)
    add('mi0', 8, 256); add('mi1', 8, 256); add('mi2', 8, 256)
    for h in range(8):
        add('mq%d' % h, 3, 256)
    add('mkn', 2, 1024); add('mv', 2, 1024)
    for h in range(8):
        add('mo%d' % h, 1, 1024)
    return idx, off


def pack_weights(inp):
    wp = WPack()
    rwi = inp['ret_w_in'][0]; rwo = inp['ret_w_out'][0]
    for h in range(4):
        wp.add('rq%d' % h, _kt(rwi[:, h * 256:(h + 1) * 256]))
        wp.add('rk%d' % h, _kt(rwi[:, 1024 + h * 256:1024 + (h + 1) * 256]))
        wp.add('rv%da' % h, _kt(rwi[:, 2048 + h * 512:2048 + h * 512 + 256]))
        wp.add('rv%db' % h, _kt(rwi[:, 2048 + h * 512 + 256:2048 + (h + 1) * 512]))
        wp.add('rg%da' % h, _kt(rwi[:, 4096 + h * 512:4096 + h * 512 + 256]))
        wp.add('rg%db' % h, _kt(rwi[:, 4096 + h * 512 + 256:4096 + (h + 1) * 512]))
        wp.add('ro%da' % h, _kt(rwo[h * 512:(h + 1) * 512, 0:512]))
        wp.add('ro%db' % h, _kt(rwo[h * 512:(h + 1) * 512, 512:1024]))
    for l in range(2):
        wi = inp['ffn_w_in'][l]; wo = inp['ffn_w_out'][l]
        for f in range(NF):
            wp.add('f%di%d' % (l, f), _kt(np.concatenate(
                [wi[:, f * 128:(f + 1) * 128], wi[:, FFN + f * 128:FFN + (f + 1) * 128]], axis=1)))
        for d in range(8):
            wp.add('f%do%d_0' % (l, d), _kt(wo[0:1408, d * 128:(d + 1) * 128]))
            wp.add('f%do%d_1' % (l, d), _kt(wo[1408:2816, d * 128:(d + 1) * 128]))
    mwi = inp['mla_w_in'][0]
    wp.add('mi0', _kt(mwi[:, 0:256])); wp.add('mi1', _kt(mwi[:, 256:512]))
    wp.add('mi2', _kt(np.concatenate([mwi[:, 512:704], np.zeros((1024, 64), np.float32)], axis=1)))
    wqb = inp['mla_w_qb'][0]
    for h in range(8):
        wp.add('mq%d' % h, _kt(np.concatenate([wqb[:, h * 192:(h + 1) * 192], np.zeros((384, 64), np.float32)], axis=1)))
    wkvb = inp['mla_w_kvb'][0].reshape(256, 8, 256)
    wp.add('mkn', _kt(np.ascontiguousarray(wkvb[:, :, 0:128]).reshape(256, 1024)))
    wp.add('mv', _kt(np.ascontiguousarray(wkvb[:, :, 128:256]).reshape(256, 1024)))
    mwo = inp['mla_w_out'][0]
    for h in range(8):
        wp.add('mo%d' % h, _kt(mwo[h * 128:(h + 1) * 128, :]))
    idx, tot = weight_index()
    assert tot == wp.off
    for k in idx:
        assert idx[k] == wp.index[k], k
    return np.concatenate(wp.parts)


P_RETN, P_RETGN, P_MLAN, P_QN, P_KVN, P_QHN, P_QHR, P_KHN, P_KHR, P_FFNN, P_CW, P_CB = (
    0, 8, 24, 32, 35, 37, 38, 39, 40, 41, 57, 57 + 132)
NPRM = 57 + 132 + 44


def pack_params(inp):
    p = np.zeros((128, NPRM), np.float32)
    p[:, P_RETN:P_RETN + 8] = _col(inp['ret_norm'][0])
    p[:, P_RETGN:P_RETGN + 16] = _col(inp['ret_gn'][0].reshape(-1))
    p[:, P_MLAN:P_MLAN + 8] = _col(inp['mla_norm'][0])
    p[:, P_QN:P_QN + 3] = _col(inp['mla_q_norm'][0])
    p[:, P_KVN:P_KVN + 2] = _col(inp['mla_kv_norm'][0])
    p[:, P_QHN] = inp['mla_q_head_norm'][0][0:128]
    p[0:64, P_QHR] = inp['mla_q_head_norm'][0][128:192]
    p[:, P_KHN] = inp['mla_k_head_norm'][0][0:128]
    p[0:64, P_KHR] = inp['mla_k_head_norm'][0][128:192]
    for l in range(2):
        p[:, P_FFNN + 8 * l:P_FFNN + 8 * l + 8] = _col(inp['ffn_norm'][l])
        for k in range(3):
            p[:, P_CW + (l * 3 + k) * NF:P_CW + (l * 3 + k + 1) * NF] = _col(inp['ffn_conv_w'][l, k])
        p[:, P_CB + l * NF:P_CB + (l + 1) * NF] = _col(inp['ffn_conv_b'][l])
    return p


def const_tables():
    pos = np.arange(S, dtype=np.float64)
    inv = THETA ** (-np.arange(128, dtype=np.float64) / 128.0)
    ang = inv[:, None] * pos[None, :]
    rrope = np.stack([np.cos(ang), np.sin(ang)], axis=1).astype(np.float32)
    inv = THETA ** (-np.arange(32, dtype=np.float64) / 32.0)
    ang = np.concatenate([inv, inv])[:, None] * pos[None, :]
    mrope = np.zeros((128, 2, S), np.float32)
    mrope[0:64] = np.stack([np.cos(ang), np.sin(ang)], axis=1).astype(np.float32)
    dtab = np.zeros((4, 128, S), np.float64)
    p = np.arange(128)[:, None]
    m = np.arange(S)[None, :]
    for h in range(4):
        lg = np.log1p(-2.0 ** (-5.0 - h))
        t = np.exp(lg * (m - p).astype(np.float64))
        md = np.arange(128)[None, :]
        allowed = (p // 64) <= (md // 64)
        t[:, 0:128] = np.where(allowed, np.exp(lg * np.abs(md - p)), 0.0)
        dtab[h] = t * (256.0 ** -0.5)
    dtab = dtab.astype(np.float32)
    misc = np.zeros((128, 512), np.float32)
    misc[:, 0:128] = ((p // 64) <= (np.arange(128)[None, :] // 64)).astype(np.float32)
    for i in range(32):
        misc[32 + i, 128 + i] = -1.0
        misc[i, 128 + 32 + i] = 1.0
    misc[:, 256:384] = np.eye(128, dtype=np.float32)
    misc[:, 384:512] = np.where(misc[:, 0:128] > 0, 0.0, -30000.0)
    return rrope, mrope, dtab, misc


class Buf:
    __slots__ = ('name', 'w', 'r', 'dsem', 'dcnt')

    def __init__(self, name):
        self.name = name; self.w = None; self.r = {}; self.dsem = None; self.dcnt = 0


class KB:
    def __init__(self, nc, es):
        self.nc = nc; self.es = es
        self.eng = {'pe': nc.tensor, 'act': nc.scalar, 'dve': nc.vector, 'pool': nc.gpsimd, 'sp': nc.sync}
        self.semh = {}
        for e in self.eng:
            self.semh[e] = es.enter_context(nc.semaphore('sem_' + e))
        self.cnt = {e: 0 for e in self.eng}
        self.pend = {e: False for e in self.eng}
        self.seen = {e: {} for e in self.eng}
        self.bar = {}
        self.nops = 0

    def _waits(self, e, reads, writes):
        deps = {}

        def add(tok):
            if tok is None:
                return
            k, v = tok
            if deps.get(k, 0) < v:
                deps[k] = v
        for b in reads:
            add(b.w)
        for b in writes:
            add(b.w)
            for t in b.r.values():
                add(t)
        if e != 'pool':
            for k, v in self.bar.items():
                add((k, v))
        eng = self.eng[e]; seen = self.seen[e]
        for k, v in deps.items():
            if k == e and e == 'pe':
                continue
            if seen.get(k, 0) >= v:
                continue
            if k in self.cnt:
                assert v <= self.cnt[k], ('future dependency', e, k, v, self.cnt[k])
            eng.wait_ge(self.semh[k], v)
            seen[k] = v

    def op(self, e, fn, reads=(), writes=(), signal=True):
        self._waits(e, reads, writes)
        ins = fn(self.eng[e])
        self.nops += 1
        if signal:
            self.cnt[e] += 1
            ins.then_inc(self.semh[e], 1)
            tok = (e, self.cnt[e]); self.pend[e] = False
        else:
            tok = (e, self.cnt[e] + 1); self.pend[e] = True
        for b in reads:
            b.r[e] = tok
        for b in writes:
            b.w = tok; b.r = {}
        return ins

    def dma(self, e, out, in_, reads=(), writes=(), **kw):
        tgt = writes[0]
        if tgt.dsem is None:
            self.nsem = getattr(self, 'nsem', 0) + 1
            tgt.dsem = 'd%d_%s' % (self.nsem, tgt.name)
            self.semh[tgt.dsem] = self.es.enter_context(self.nc.semaphore(tgt.dsem))
        self._waits(e, reads, writes)
        ins = self.eng[e].dma_start(out=out, in_=in_, **kw)
        tgt.dcnt += 16
        ins.then_inc(self.semh[tgt.dsem], 16)
        tok = (tgt.dsem, tgt.dcnt)
        for b in reads:
            b.r[tgt.dsem] = tok
        for b in writes:
            b.w = tok; b.r = {}
        return ins

    def barrier(self):
        for e in ('pe', 'act', 'dve'):
            assert not self.pend[e], e
            self.bar[e] = self.cnt[e]

    def wait_all(self, e, bufs):
        self._waits(e, bufs, ())


class Defer:
    def __init__(self):
        self.q = []; self.t = 0; self.n = 0

    def add(self, delay, fn, tag=0):
        self.n += 1
        self.q.append((self.t + delay, self.n, tag, fn))

    def tick(self):
        self.t += 1
        due = sorted([x for x in self.q if x[0] <= self.t], key=lambda x: (x[0], x[1]))
        self.q = [x for x in self.q if x[0] > self.t]
        for x in due:
            x[3]()

    def flush(self, maxtag=None):
        while True:
            sel = sorted([x for x in self.q if maxtag is None or x[2] <= maxtag], key=lambda x: (x[0], x[1]))
            if not sel:
                return
            self.q = [x for x in self.q if not (maxtag is None or x[2] <= maxtag)]
            for x in sel:
                x[3]()


class Rot:
    def __init__(self, items):
        self.items = list(items); self.i = 0

    def __call__(self):
        x = self.items[self.i % len(self.items)]; self.i += 1
        return x


def build_program(nseq=2, layers=('ret', 'ffn0', 'mla', 'ffn1')):
    nc = bass.Bass("TRN2", target_bir_lowering=False)
    widx, wtot = weight_index()
    xin = nc.dram_tensor("xin", [nseq, 128, 8, S], F32, kind="ExternalInput").ap()
    wts = nc.dram_tensor("wts", [wtot], F32, kind="ExternalInput").ap()
    prm_d = nc.dram_tensor("prm", [128, NPRM], F32, kind="ExternalInput").ap()
    rrope_d = nc.dram_tensor("rrope", [128, 2, S], F32, kind="ExternalInput").ap()
    mrope_d = nc.dram_tensor("mrope", [128, 2, S], F32, kind="ExternalInput").ap()
    dtab_d = nc.dram_tensor("dtab", [4, 128, S], F32, kind="ExternalInput").ap()
    misc_d = nc.dram_tensor("misc", [128, 512], F32, kind="ExternalInput").ap()
    xout = nc.dram_tensor("xout", [nseq, 128, 8, S], F32, kind="ExternalOutput").ap()

    with ExitStack() as es:
        kb = KB(nc, es)

        uid = [0]

        def sb(name, shape, dt, stack=es):
            uid[0] += 1
            return stack.enter_context(nc.sbuf_tensor("%s_%d" % (name, uid[0]), shape, dt))

        xT = sb("xT", [128, 8, S], F32)
        xb = [Buf('x%d' % i) for i in range(4)]
        ob = [Buf('o%d' % i) for i in range(4)]
        ring = sb("ring", [128, RING, SLOT], BF16)
        ringb = [Buf('ring%d' % i) for i in range(RING)]
        prm = sb("prm_s", [128, NPRM], F32); prmb = Buf('prm')
        identb = sb("identb", [128, 128], BF16)
        negmb = sb("negmb", [128, 128], BF16)
        ones = sb("ones", [128, 128], BF16); constb = Buf('const')
        psum = es.enter_context(nc.psum_tensor("psum", [128, 8, 512], F32))
        pb = [Buf('ps%d' % i) for i in range(8)]

        kb.dma('sp', prm[:], prm_d, writes=[prmb])
        kb.op('dve', lambda e: e.memset(ones[:], 1.0), writes=[constb])
        rotb = sb("rotb", [128, 128], BF16)

        ring_i = [0]

        def wtile(name):
            off, (kc, cols) = widx[name]
            n = kc * cols
            slot = ring_i[0] % RING; ring_i[0] += 1
            src = wts[off:off + 128 * n].rearrange("(p n) -> p n", p=128)
            kb.dma('pool', ring[:, slot, 0:n], src, writes=[ringb[slot]], max_dma_last_dim=8192)
            return ring[:, slot, 0:n].rearrange("p (k c) -> p k c", k=kc), ringb[slot]

        @contextmanager
        def scope():
            with ExitStack() as st:
                yield st
                kb.barrier()

        def MM(out, lhsT, rhs, start, stop, reads, writes, signal=True):
            return kb.op('pe', lambda e: e.matmul(out, lhsT, rhs, start=start, stop=stop), reads, writes, signal)

        def ACT(out, in_, func, reads, writes, **kw):
            return kb.op('act', lambda e: e.activation(out=out, in_=in_, func=func, **kw), reads, writes)

        def TT(out, a, b, op, reads, writes):
            return kb.op('dve', lambda e: e.tensor_tensor(out=out, in0=a, in1=b, op=op), reads, writes)

        def STT(out, in0, scalar, in1, op0, op1, reads, writes):
            return kb.op('dve', lambda e: e.scalar_tensor_tensor(out=out, in0=in0, scalar=scalar, in1=in1,
                                                                 op0=op0, op1=op1), reads, writes)

        def RECIP(out, in_, reads, writes):
            return kb.op('dve', lambda e: e.reciprocal(out=out, in_=in_), reads, writes)

        def bank(i):
            return psum[:, i, :], pb[i]

        sqt = sb("sqt", [128, 2, 512], BF16); sqtb = [Buf('sqt0'), Buf('sqt1')]
        rstd = sb("rstd", [128, 512], F32); rstdb = Buf('rstd')

        lnb_t = sb("lnb", [128, 512], F32); lnbb = Buf('lnb')

        def rms_rstd(ps_ap, ps_buf, nfeat, npart=128, out=None, outb=None):
            if out is None:
                out, outb = rstd, rstdb
            ACT(lnb_t[0:npart, :], ps_ap, AF.Ln, [ps_buf], [lnbb], bias=EPS, scale=1.0 / nfeat)
            ACT(out[0:npart, :], lnb_t[0:npart, :], AF.Exp, [lnbb], [outb], scale=-0.5)

        def rmsnorm_tile(t0, gain_col, hT, hbuf, hcol0, pbank, sqw=None, sqwb=None):
            tile_i = t0 // 512
            ps, psb = bank(pbank)
            if sqw is not None:
                ACT(sqw, xT[:, :, t0:t0 + 512], AF.Square, [xb[tile_i]], list(sqwb))
                for c in range(8):
                    MM(ps, ones[:], sqw[:, c, :], c == 0, c == 7, list(sqwb) + [constb], [psb], signal=(c == 7))
            else:
                for c in range(8):
                    ACT(sqt[:, c % 2, :], xT[:, c, t0:t0 + 512], AF.Square, [xb[tile_i]], [sqtb[c % 2]])
                    MM(ps, ones[:], sqt[:, c % 2, :], c == 0, c == 7, [sqtb[c % 2], constb], [psb])
            rms_rstd(ps, psb, D)
            for c in range(8):
                STT(hT[:, c, hcol0:hcol0 + 512], xT[:, c, t0:t0 + 512], prm[:, gain_col + c:gain_col + c + 1],
                    rstd[:], ALU.mult, ALU.mult, [xb[tile_i], prmb, rstdb], [hbuf])

        def ffn_layer(l):
            with scope() as st:
                hT = sb("f_hT", [128, 8, S], BF16, st); hb = [Buf('f_h%d' % i) for i in range(4)]
                uT = sb("f_uT", [128, NF, 1024], BF16, st); ub = [Buf('f_u%d' % f) for f in range(NF)]
                halo = sb("f_halo", [128, 2, NF, 2], F32, st); halob = [[Buf('f_halo%d_%d' % (a, f)) for f in range(NF)] for a in range(2)]
                cbuf = sb("f_c", [128, 2, 512], F32, st); cb = [Buf('f_c0'), Buf('f_c1')]
                sbuf = sb("f_s", [128, 2, 512], F32, st); sbb = [Buf('f_s0'), Buf('f_s1')]
                sq8 = sb("f_sq8", [128, 8, 512], BF16, st); sq8b = Buf('f_sq8')
                rota = Rot([0, 1, 2]); rotg = Rot([3, 4, 5]); roty = Rot([6, 7])
                step = 0
                cw = lambda k, f: prm[:, P_CW + (l * 3 + k) * NF + f:P_CW + (l * 3 + k) * NF + f + 1]
                cbias = lambda f: prm[:, P_CB + l * NF + f:P_CB + l * NF + f + 1]
                gcol = P_FFNN + 8 * l

                def norm_a(ti, cs=range(8)):
                    for c in cs:
                        ACT(sq8[:, c, :], xT[:, c, ti * 512:(ti + 1) * 512], AF.Square, [xb[ti]], [sq8b])

                def norm_b(ti):
                    ps, psb = bank(6 + ti % 2)
                    for c in range(8):
                        MM(ps, ones[:], sq8[:, c, :], c == 0, c == 7, [sq8b, constb], [psb], signal=(c == 7))
                    rms_rstd(ps, psb, D)

                def norm_c(ti, cs=range(8)):
                    for c in cs:
                        STT(hT[:, c, ti * 512:(ti + 1) * 512], xT[:, c, ti * 512:(ti + 1) * 512], prm[:, gcol + c:gcol + c + 1],
                            rstd[:], ALU.mult, ALU.mult, [xb[ti], prmb, rstdb], [hb[ti]])

                fscr = sb("f_scr", [128, 2], F32, st); fscrb = Buf('f_scr')
                ACT(fscr[:, 0:1], ones[:, 0:1], AF.Ln, [constb], [fscrb])
                for ti in range(2):
                    ACT(sq8[:], xT[:, :, ti * 512:(ti + 1) * 512], AF.Square, [xb[ti]], [sq8b])
                    norm_b(ti); norm_c(ti)
                for T in range(2):
                    for f in range(NF):
                        if T == 0:
                            if f <= 7:
                                norm_a(2, [f])
                            if f == 8:
                                norm_b(2)
                            if 9 <= f <= 16:
                                norm_c(2, [f - 9]); norm_a(3, [f - 9])
                            if f == 17:
                                norm_b(3)
                            if f >= 18:
                                norm_c(3, [2 * (f - 18), 2 * (f - 18) + 1])
                        wt, wbuf = wtile('f%di%d' % (l, f))
                        for tt in range(2):
                            pa, pab = bank(rota()); pg, pgb = bank(rotg())
                            hs = slice(T * 1024 + tt * 512, T * 1024 + (tt + 1) * 512)
                            us = slice(tt * 512, (tt + 1) * 512)
                            hbi = hb[T * 2 + tt]
                            for c in range(8):
                                MM(pa, wt[:, c, 0:128], hT[:, c, hs], c == 0, c == 7, [wbuf, hbi], [pab], signal=(c == 7))
                            for c in range(8):
                                MM(pg, wt[:, c, 128:256], hT[:, c, hs], c == 0, c == 7, [wbuf, hbi], [pgb], signal=(c == 7))
                            k = step % 2; step += 1
                            cc = cbuf[:, k, :]
                            ACT(cc, pg, AF.Identity, [pgb, prmb], [cb[k]], scale=cw(2, f), bias=cbias(f))
                            hp = (T * 2 + tt) % 2
                            STT(cc[:, 1:512], pg[:, 0:511], cw(1, f), cc[:, 1:512], ALU.mult, ALU.add, [pgb, prmb, cb[k]], [cb[k]])
                            STT(cc[:, 2:512], pg[:, 0:510], cw(0, f), cc[:, 2:512], ALU.mult, ALU.add, [pgb, prmb, cb[k]], [cb[k]])
                            kb.op('dve', lambda e, hp=hp, f=f, pg=pg: e.tensor_copy(out=halo[:, hp, f, :], in_=pg[:, 510:512]),
                                  [pgb], [halob[hp][f]])
                            if not (T == 0 and tt == 0):
                                STT(cc[:, 0:1], halo[:, 1 - hp, f, 1:2], cw(1, f), cc[:, 0:1], ALU.mult, ALU.add, [halob[1 - hp][f], prmb, cb[k]], [cb[k]])
                                STT(cc[:, 0:2], halo[:, 1 - hp, f, 0:2], cw(0, f), cc[:, 0:2], ALU.mult, ALU.add, [halob[1 - hp][f], prmb, cb[k]], [cb[k]])
                            ACT(sbuf[:, k, :], cc, AF.Silu, [cb[k]], [sbb[k]])
                            TT(uT[:, f, us], sbuf[:, k, :], pa, ALU.mult, [sbb[k], pab], [ub[f]])
                    for d in range(8):
                        wa, wab = wtile('f%do%d_0' % (l, d)); wb_, wbb = wtile('f%do%d_1' % (l, d))
                        for tt in range(2):
                            py, pyb = bank(roty())
                            us = slice(tt * 512, (tt + 1) * 512)
                            ts = slice(T * 1024 + tt * 512, T * 1024 + (tt + 1) * 512)
                            for kk in range(NF):
                                w, wbf = (wa, wab) if kk < 11 else (wb_, wbb)
                                MM(py, w[:, kk % 11, :], uT[:, kk, us], kk == 0, kk == NF - 1, [wbf, ub[kk]], [pyb],
                                   signal=(kk == NF - 1))
                            xi = xb[T * 2 + tt]
                            TT(xT[:, d, ts], xT[:, d, ts], py, ALU.add, [xi, pyb], [xi])

        def ret_layer():
            with scope() as st:
                hT = sb("r_hT", [128, 8, S], BF16, st); hb = [Buf('r_h%d' % i) for i in range(4)]
                rope = sb("r_rope", [128, 2, S], F32, st); ropeb = Buf('r_rope')
                dt_ = sb("r_dt", [128, S], F32, st); dtb = Buf('r_dt')
                qT = sb("r_q", [128, 2, S], BF16, st); qb = [Buf('r_q%d' % i) for i in range(4)]
                kT = sb("r_k", [128, 2, S], BF16, st); kbf = [Buf('r_k%d' % i) for i in range(4)]
                vt = sb("r_v", [128, 16, 512], BF16, st); vb = [Buf('r_v%d' % i) for i in range(16)]
                t12 = sb("r_t", [128, 2, 512], F32, st); tb_ = [Buf('r_t0'), Buf('r_t1')]
                P = sb("r_P", [128, 2, 512], BF16, st); Pb = [Buf('r_P0'), Buf('r_P1')]
                sg = sb("r_sg", [128, 2, 4, 512], BF16, st); sgb = [Buf('r_sg0'), Buf('r_sg1')]
                sqn = sb("r_sqn", [128, 4, 512], BF16, st); sqnv = [Buf('r_sqn%d' % i) for i in range(4)]
                wv_ = sb("r_w", [128, 512], F32, st); wvb = Buf('r_w')
                u = sb("r_u", [128, 2, 4, 512], BF16, st); ub = [Buf('r_u0'), Buf('r_u1')]
                dq = Defer()
                scr = sb("r_scr", [128, 2], F32, st); scrb = Buf('r_scr')
                kb.dma('sp', rope[:], rrope_d, writes=[ropeb])
                ACT(scr[:, 0:1], ones[:, 0:1], AF.Ln, [constb], [scrb])
                sgw = sg[:].rearrange("p a b c -> p (a b) c")
                for tt in range(4):
                    rmsnorm_tile(tt * 512, P_RETN, hT, hb[tt], tt * 512, 6 + tt % 2, sqw=sgw, sqwb=sgb)
                rotp = Rot([0, 1, 2, 3, 4, 5])
                rots = Rot([4, 5]); rotx = Rot([6, 7])
                pcnt = [0]
                gw = {}

                def gproj(h, it, par, groups, banks=None):
                    evs = []
                    if (h, it) not in gw:
                        gw.clear()
                        gw[(h, it)] = (wtile('rg%da' % h), wtile('rg%db' % h))
                    (wga, wgab), (wgb_, wgbb) = gw[(h, it)]
                    i0 = it * 512
                    for gc in groups:
                        w, wbf = (wga, wgab) if gc < 2 else (wgb_, wgbb)
                        bi = rotx() if banks is None else banks[gc]
                        pg, pgb = bank(bi)
                        for c in range(8):
                            MM(pg, w[:, c, (gc % 2) * 128:(gc % 2 + 1) * 128], hT[:, c, i0:i0 + 512], c == 0, c == 7,
                               [wbf, hb[it]], [pgb], signal=(c == 7))
                        evs.append(lambda pg=pg, pgb=pgb, gc=gc: ACT(sg[:, par, gc, :], pg, AF.Silu, [pgb], [sgb[par]]))
                    return evs

                tiles = [(h, it) for h in range(4) for it in range(4)]
                for h in range(4):
                    kb.dma('sp', dt_[:], dtab_d[h], writes=[dtb])
                    for nm, dst, dstb in (('rq%d' % h, qT, qb), ('rk%d' % h, kT, kbf)):
                        w, wbf = wtile(nm)
                        for tt in range(4):
                            ts = slice(tt * 512, (tt + 1) * 512)
                            p1, p1b = bank(rotp()); p2, p2b = bank(rotp())
                            for c in range(8):
                                MM(p1, w[:, c, 0:128], hT[:, c, ts], c == 0, c == 7, [wbf, hb[tt]], [p1b], signal=(c == 7))
                            for c in range(8):
                                MM(p2, w[:, c, 128:256], hT[:, c, ts], c == 0, c == 7, [wbf, hb[tt]], [p2b], signal=(c == 7))
                            cs = rope[:, 0, ts]; sn = rope[:, 1, ts]
                            TT(t12[:, 0, :], p1, cs, ALU.mult, [p1b, ropeb], [tb_[0]])
                            TT(t12[:, 1, :], p2, sn, ALU.mult, [p2b, ropeb], [tb_[1]])
                            TT(dst[:, 0, ts], t12[:, 0, :], t12[:, 1, :], ALU.subtract, [tb_[0], tb_[1]], [dstb[tt]])
                            TT(t12[:, 0, :], p2, cs, ALU.mult, [p2b, ropeb], [tb_[0]])
                            TT(t12[:, 1, :], p1, sn, ALU.mult, [p1b, ropeb], [tb_[1]])
                            TT(dst[:, 1, ts], t12[:, 0, :], t12[:, 1, :], ALU.add, [tb_[0], tb_[1]], [dstb[tt]])
                            dq.tick()
                    dq.flush()
                    wva, wvab = wtile('rv%da' % h); wvb_, wvbb = wtile('rv%db' % h)
                    for tb in range(16):
                        pv, pvb = bank(rotp())
                        for half, (w, wbf) in enumerate(((wva, wvab), (wvb_, wvbb))):
                            for c in range(8):
                                MM(pv[:, half * 256:(half + 1) * 256], hT[:, c, tb * 128:(tb + 1) * 128], w[:, c, :],
                                   c == 0, c == 7, [wbf, hb[tb // 4]], [pvb], signal=(c == 7))
                        ACT(vt[:, tb, :], pv, AF.Copy, [pvb], [vb[tb]])
                    if h == 0:
                        for gc in range(4):
                            for ev in gproj(0, 0, 0, [gc]):
                                ev()
                    for it in range(4):
                        g = h * 4 + it; par = g % 2
                        i0 = it * 512
                        njb = 4 * it + 4

                        def emit_S(jb):
                            c0 = max(jb * 128 - i0, 0); N = 512 - c0; ilo = i0 + c0
                            bi = rots()
                            ps, psb = bank(bi)
                            for c in range(2):
                                MM(ps[:, 0:N], kT[:, c, jb * 128:(jb + 1) * 128], qT[:, c, ilo:ilo + N], c == 0, c == 1,
                                   [kbf[jb // 4], qb[it]], [psb], signal=(c == 1))
                            return ps, psb

                        nxt = emit_S(0)
                        for jb in range(njb):
                            ps, psb = nxt
                            if jb + 1 < njb:
                                nxt = emit_S(jb + 1)
                            c0 = max(jb * 128 - i0, 0); N = 512 - c0; ilo = i0 + c0
                            k = pcnt[0] % 2; pcnt[0] += 1
                            doff = ilo - jb * 128
                            TT(P[:, k, 0:N], ps[:, 0:N], dt_[:, doff:doff + N], ALU.mult, [psb, dtb], [Pb[k]])
                            if jb == njb - 2:
                                ACT(scr[:, 0:1], ones[:, 0:1], AF.Ln, [constb], [scrb])
                            dq.tick()
                            for vc in range(4):
                                MM(psum[:, vc, c0:c0 + N], vt[:, jb, vc * 128:(vc + 1) * 128], P[:, k, 0:N], jb == 0, jb == njb - 1,
                                   [vb[jb], Pb[k]], [pb[vc]], signal=(vc == 3))
                        dq.flush(maxtag=g - 2)
                        nx = tiles[g + 1] if g + 1 < 16 else None
                        gb = {0: 6, 1: 4, 2: 5, 3: 6}
                        evs = gproj(nx[0], nx[1], 1 - par, [0], gb) if nx else []
                        pss, pssb = bank(7)
                        for vc in range(4):
                            ACT(sqn[:, vc, :], psum[:, vc, :], AF.Square, [pb[vc]], [sqnv[vc]])
                        for vc in range(4):
                            MM(pss, ones[:], sqn[:, vc, :], vc == 0, vc == 3, [sqnv[vc], constb], [pssb], signal=(vc == 3))
                        rms_rstd(pss, pssb, 512)
                        if nx:
                            evs += gproj(nx[0], nx[1], 1 - par, [1], gb)
                            evs += gproj(nx[0], nx[1], 1 - par, [2], gb)
                        for ev in evs:
                            ev()
                        for vc in range(4):
                            TT(wv_[:], sg[:, par, vc, :], rstd[:], ALU.mult, [sgb[par], rstdb], [wvb])
                            gcol = P_RETGN + h * 4 + vc
                            STT(u[:, par, vc, :], psum[:, vc, :], prm[:, gcol:gcol + 1], wv_[:], ALU.mult, ALU.mult,
                                [pb[vc], prmb, wvb], [ub[par]])
                        if nx:
                            for ev in gproj(nx[0], nx[1], 1 - par, [3], gb):
                                ev()
                        dq.tick()
                        wo = {}

                        def ychunk(d, h=h, i0=i0, it=it, par=par, wo=wo):
                            key = 'a' if d < 4 else 'b'
                            if key not in wo:
                                wo[key] = wtile('ro%d%s' % (h, key))
                            w, wbf = wo[key]
                            py, pyb = bank(rotx())
                            for vc in range(4):
                                MM(py, w[:, vc, (d % 4) * 128:(d % 4 + 1) * 128], u[:, par, vc, :], vc == 0, vc == 3,
                                   [wbf, ub[par]], [pyb], signal=(vc == 3))
                            TT(xT[:, d, i0:i0 + 512], xT[:, d, i0:i0 + 512], py, ALU.add, [xb[it], pyb], [xb[it]])
                        late = (4 * nx[1] + 4 + 1) if (nx and nx[0] == h) else 4
                        for d in range(8):
                            dq.add(1 + d // 2 if d < 6 else late, (lambda d=d, f=ychunk: f(d)), tag=g)
                dq.flush()

        def mla_layer():
            with scope() as so:
                cqn = sb("m_cqn", [128, 3, S], BF16, so); cqb = [Buf('m_cq%d' % i) for i in range(4)]
                ckvn = sb("m_ckvn", [128, 2, S], BF16, so); ckb = [Buf('m_ck%d' % i) for i in range(4)]
                KrT = sb("m_kr", [128, S], BF16, so); krb = [Buf('m_kr%d' % i) for i in range(4)]
                sqkr = sb("m_sqkr", [128, S], BF16, so); sqkrb = [Buf('m_sqkr%d' % i) for i in range(4)]
                mrope = sb("m_rope", [128, 2, S], F32, so); mropeb = Buf('m_rope')
                rstdk = sb("m_rstdk", [128, 16, 4], F32, so); rstdkb = Buf('m_rstdk')
                rtk = sb("m_rtk", [128, 16, 4], F32, so); rtkb = Buf('m_rtk')
                xg = sb("m_xg", [128, 512], BF16, so); xgb = Buf('m_xg')
                t12 = sb("m_t", [128, 2, 512], F32, so); tb_ = [Buf('m_t0'), Buf('m_t1')]
                kb.dma('sp', mrope[:], mrope_d, writes=[mropeb])

                def rope64_mm(src_ap, src_buf, pbank):
                    pr, prb = bank(pbank)
                    MM(pr, rotb[:], src_ap, True, True, [constb, src_buf], [prb])
                    return pr, prb

                def rope64_ew(src_ap, src_buf, pr, prb, dst_ap, dst_buf, ts):
                    TT(t12[:, 0, :], src_ap, mrope[:, 0, ts], ALU.mult, [src_buf, mropeb], [tb_[0]])
                    TT(t12[:, 1, :], pr, mrope[:, 1, ts], ALU.mult, [prb, mropeb], [tb_[1]])
                    TT(dst_ap, t12[:, 0, :], t12[:, 1, :], ALU.add, [tb_[0], tb_[1]], [dst_buf])

                with scope() as s1:
                    hT = sb("m_hT", [128, 8, S], BF16, s1); hb = [Buf('m_h%d' % i) for i in range(4)]
                    sq3 = sb("m_sq3", [128, 5, 512], BF16, s1); sq3b = Buf('m_sq3'); sq3c = Buf('m_sq3c')
                    rstd2 = sb("m_rstd2", [128, 512], F32, s1); rstd2b = Buf('m_rstd2')
                    msq8 = sb("m_sq8", [128, 8, 512], BF16, s1); msq8b = [Buf('m_sq8')]
                    mscr = sb("m_scr", [128, 2], F32, s1); mscrb = Buf('m_scr')
                    ACT(mscr[:, 0:1], ones[:, 0:1], AF.Ln, [constb], [mscrb])
                    for tt in range(4):
                        rmsnorm_tile(tt * 512, P_MLAN, hT, hb[tt], tt * 512, 6 + tt % 2, sqw=msq8[:], sqwb=msq8b)
                    wi = [wtile('mi0'), wtile('mi1'), wtile('mi2')]
                    for tt in range(4):
                        ts = slice(tt * 512, (tt + 1) * 512)
                        def proj(fcs):
                            for fc in fcs:
                                w, wbf = wi[fc // 2]; col = (fc % 2) * 128
                                for c in range(8):
                                    MM(psum[:, fc, :], w[:, c, col:col + 128], hT[:, c, ts], c == 0, c == 7, [wbf, hb[tt]], [pb[fc]],
                                       signal=(c == 7))
                        proj(range(0, 3))
                        ACT(sq3[:, 0:3, :], psum[:, 0:3, :], AF.Square, [pb[0], pb[1], pb[2]], [sq3b])
                        proj(range(3, 6))
                        ACT(sq3[:, 3:5, :], psum[:, 3:5, :], AF.Square, [pb[3], pb[4]], [sq3c])
                        pss, pssb = bank(6)
                        for k in range(3):
                            MM(pss, ones[:], sq3[:, k, :], k == 0, k == 2, [sq3b, constb], [pssb], signal=(k == 2))
                        rms_rstd(pss, pssb, 384)
                        ACT(sqkr[:, ts], psum[:, 5, :], AF.Square, [pb[5]], [sqkrb[tt]])
                        ACT(xg[:], psum[:, 5, :], AF.Identity, [pb[5], prmb], [xgb], scale=prm[:, P_KHR:P_KHR + 1])
                        for k in range(3):
                            STT(cqn[:, k, ts], psum[:, k, :], prm[:, P_QN + k:P_QN + k + 1], rstd[:], ALU.mult, ALU.mult,
                                [pb[k], prmb, rstdb], [cqb[tt]])
                        pss, pssb = bank(7)
                        for k in range(2):
                            MM(pss, ones[:], sq3[:, 3 + k, :], k == 0, k == 1, [sq3c, constb], [pssb], signal=(k == 1))
                        rms_rstd(pss, pssb, 256, out=rstd2, outb=rstd2b)
                        for k in range(2):
                            STT(ckvn[:, k, ts], psum[:, 3 + k, :], prm[:, P_KVN + k:P_KVN + k + 1], rstd2[:], ALU.mult, ALU.mult,
                                [pb[3 + k], prmb, rstd2b], [ckb[tt]])
                        pr, prb = rope64_mm(xg[:], xgb, 6)
                        rope64_ew(xg[:], xgb, pr, prb, KrT[:, ts], krb[tt], ts)

                for G in range(2):
                    with scope() as s2:
                        KnT = sb("m_kn", [128, 4, S], BF16, s2); knb = [[Buf('m_kn%d_%d' % (a, i)) for i in range(4)] for a in range(4)]
                        Vt = sb("m_v", [128, 16, 512], BF16, s2); vb = [Buf('m_v%d' % i) for i in range(16)]
                        QnT = sb("m_qn", [128, 2, S], BF16, s2); qnb = [[Buf('m_qn%d_%d' % (a, i)) for i in range(4)] for a in range(2)]
                        QrT = sb("m_qr", [128, 2, S], BF16, s2); qrb = [[Buf('m_qr%d_%d' % (a, i)) for i in range(4)] for a in range(2)]
                        onT = sb("m_on", [128, 2, S], BF16, s2); onb = [[Buf('m_on%d_%d' % (a, i)) for i in range(4)] for a in range(2)]
                        sqk = sb("m_sqk", [128, 2, 512], BF16, s2); sqkb = [Buf('m_sqk0'), Buf('m_sqk1')]
                        sq2 = sb("m_sq2", [128, 512], BF16, s2); sq2b = Buf('m_sq2')
                        P = sb("m_P", [128, 3, 512], BF16, s2); Pb = [Buf('m_P0'), Buf('m_P1'), Buf('m_P2')]
                        rden, rdenb = rstd, rstdb
                        rstdq = sb("m_rstdq", [128, 512], F32, s2); rstdqb = Buf('m_rstdq')
                        Osb = sb("m_osb", [128, 512], F32, s2); Osbb = Buf('m_osb')
                        lnd = sb("m_lnd", [128, 512], F32, s2); lndb = Buf('m_lnd')
                        sqq = sb("m_sqq", [128, 512], BF16, s2); sqqb = Buf('m_sqq')
                        dq = Defer()
                        def qchain(h, tt, qp):
                            ts = slice(tt * 512, (tt + 1) * 512)
                            wq, wqb_ = wtile('mq%d' % h); cb0 = 0
                            pqn, pqnb = bank(5); pqr, pqrb = bank(6)
                            for c in range(3):
                                MM(pqn, wq[:, c, cb0:cb0 + 128], cqn[:, c, ts], c == 0, c == 2, [wqb_, cqb[tt]], [pqnb], signal=(c == 2))
                            for c in range(3):
                                MM(pqr, wq[:, c, cb0 + 128:cb0 + 256], cqn[:, c, ts], c == 0, c == 2, [wqb_, cqb[tt]], [pqrb],
                                   signal=(c == 2))
                            ACT(sqq[:], pqn, AF.Square, [pqnb], [sqqb])
                            dq.add(1, lambda: ACT(sq2[:], pqr, AF.Square, [pqrb], [sq2b]))

                            pssh = [None]

                            def stB():
                                pss, pssb = bank(7)
                                MM(pss, ones[:], sqq[:], True, False, [sqqb, constb], [pssb])
                                MM(pss, ones[:], sq2[:], False, True, [sq2b, constb], [pssb])
                                ACT(lnb_t[:], pss, AF.Ln, [pssb], [lnbb], bias=EPS, scale=1.0 / 192)

                            def stB2():
                                ACT(rstdq[:], lnb_t[:], AF.Exp, [lnbb], [rstdqb], scale=-0.5)

                            def stC():
                                STT(QnT[:, qp, ts], pqn, prm[:, P_QHN:P_QHN + 1], rstdq[:], ALU.mult, ALU.mult,
                                    [pqnb, prmb, rstdqb], [qnb[qp][tt]])
                                STT(xg[:], pqr, prm[:, P_QHR:P_QHR + 1], rstdq[:], ALU.mult, ALU.mult,
                                    [pqrb, prmb, rstdqb], [xgb])

                            def stD():
                                pr, prb = rope64_mm(xg[:], xgb, 7)
                                dq.add(2, lambda: rope64_ew(xg[:], xgb, pr, prb, QrT[:, qp, ts], qrb[qp][tt], ts))
                            dq.add(3, stB)
                            dq.add(4, stB2)
                            dq.add(6, stC)
                            dq.add(7, stD)

                        rotp = Rot([2, 3, 4])
                        wkn, wknb = wtile('mkn')
                        pssk, psskb = bank(1)
                        nsq = 0
                        kticks = [0]

                        def ktick():
                            if kticks[0] % 8 == 0 and kticks[0] // 8 < 4:
                                qchain(4 * G, kticks[0] // 8, 0)
                            kticks[0] += 1
                            dq.tick()

                        prev = None
                        for q4 in range(4):
                            ACT(Osb[:, q4 * 128:(q4 + 1) * 128], ones[:], AF.Identity, [constb, prmb], [Osbb], scale=prm[:, P_KHN:P_KHN + 1])
                        for hl in range(4):
                            h = 4 * G + hl
                            for tt in range(4):
                                ts = slice(tt * 512, (tt + 1) * 512)
                                pk, pkb = bank(rotp())
                                for c in range(2):
                                    MM(pk, wkn[:, c, h * 128:(h + 1) * 128], ckvn[:, c, ts], c == 0, c == 1, [wknb, ckb[tt]], [pkb],
                                       signal=(c == 1))
                                k = nsq % 2; nsq += 1
                                ACT(sqk[:, k, :], pk, AF.Square, [pkb], [sqkb[k]])

                                def tiny(hl=hl, tt=tt, k=k, pk=pk, pkb=pkb, ts=ts):
                                    TT(KnT[:, hl, ts], pk, Osb[:], ALU.mult, [pkb, Osbb, sqkb[k]], [knb[hl][tt]])
                                    for b in range(4):
                                        tbk = tt * 4 + b
                                        col = tbk * 4 + hl
                                        MM(pssk[:, col:col + 1], sqk[:, k, b * 128:(b + 1) * 128], ones[:, 0:1], True, False,
                                           [sqkb[k], constb], [psskb])
                                        MM(pssk[:, col:col + 1], sqkr[:, tbk * 128:(tbk + 1) * 128], ones[:, 0:1], False, True,
                                           [sqkrb[tt], constb], [psskb])
                                if prev is not None:
                                    prev()
                                prev = tiny
                                ktick()
                        prev()
                        f2 = lambda a: a[:].rearrange("p a b -> p (a b)")
                        ACT(f2(rtk), pssk[:, 0:64], AF.Ln, [psskb], [rtkb], bias=EPS, scale=1.0 / 192)
                        ACT(f2(rstdk), f2(rtk), AF.Exp, [rtkb], [rstdkb], scale=-0.5)
                        kb.op('dve', lambda e: e.tensor_scalar(out=f2(rstdk), in0=f2(rstdk),
                                                               scalar1=float(192.0 ** -0.5), scalar2=None, op0=ALU.mult),
                              [rstdkb], [rstdkb])
                        wv, wvb = wtile('mv')
                        for tbk in range(16):
                            pv, pvb = bank(rotp())
                            for c in range(2):
                                MM(pv, ckvn[:, c, tbk * 128:(tbk + 1) * 128], wv[:, c, G * 512:(G + 1) * 512], c == 0, c == 1,
                                   [wvb, ckb[tbk // 4]], [pvb], signal=(c == 1))
                            ACT(Vt[:, tbk, :], pv, AF.Copy, [pvb], [vb[tbk]])
                            ktick()

                        dq.flush()
                        rots = Rot([2, 3, 4]); roty = Rot([5, 6, 7])
                        pcnt = 0
                        for hl in range(4):
                            h = 4 * G + hl; qp = hl % 2
                            hstep = 0
                            for it in range(4):
                                i0 = it * 512
                                njb = 4 * it + 4

                                def emit_S(jb, i0=i0, it=it):
                                    c0 = max(jb * 128 - i0, 0); N = 512 - c0; ilo = i0 + c0
                                    ps, psb = bank(rots())
                                    MM(ps[:, 0:N], KnT[:, hl, jb * 128:(jb + 1) * 128], QnT[:, qp, ilo:ilo + N], True, False,
                                       [knb[hl][jb // 4], qnb[qp][it]], [psb], signal=False)
                                    diag = jb >= 4 * it
                                    MM(ps[:, 0:N], KrT[:, jb * 128:(jb + 1) * 128], QrT[:, qp, ilo:ilo + N], False, not diag,
                                       [krb[jb // 4], qrb[qp][it]], [psb], signal=not diag)
                                    if diag:
                                        MM(ps[:, 0:128], identb[:], negmb[:], False, True, [constb], [psb])
                                    return ps, psb
                                pend = [emit_S(0), emit_S(1)]
                                for jb in range(njb):
                                    ps, psb = pend.pop(0)
                                    if jb + 2 < njb:
                                        pend.append(emit_S(jb + 2))
                                    c0 = max(jb * 128 - i0, 0); N = 512 - c0
                                    k = pcnt % 3; pcnt += 1
                                    ACT(P[:, k, 0:N], ps[:, 0:N], AF.Exp, [psb, rstdkb], [Pb[k]], scale=rstdk[:, jb, hl:hl + 1])
                                    if hl + 1 < 4 and hstep % 10 == 0:
                                        qchain(h + 1, hstep // 10, 1 - qp)
                                    hstep += 1
                                    dq.tick()
                                    MM(psum[:, 0, c0:c0 + N], Vt[:, jb, hl * 128:(hl + 1) * 128], P[:, k, 0:N], jb == 0, jb == njb - 1,
                                       [vb[jb], Pb[k]], [pb[0]], signal=False)
                                    MM(psum[:, 1, c0:c0 + N], ones[:], P[:, k, 0:N], jb == 0, jb == njb - 1, [constb, Pb[k]], [pb[1]])
                                kb.op('dve', lambda e: e.tensor_copy(out=Osb[:], in_=psum[:, 0, :]), [pb[0]], [Osbb])
                                ACT(lnd[:], psum[:, 1, :], AF.Ln, [pb[1]], [lndb])

                                def fin(qp=qp, i0=i0, it=it):
                                    ACT(rden[:], lnd[:], AF.Exp, [lndb], [rdenb], scale=-1.0)
                                    TT(onT[:, qp, i0:i0 + 512], Osb[:], rden[:], ALU.mult, [Osbb, rdenb], [onb[qp][it]])
                                dq.add(1, fin)
                            dq.flush()
                            if hl % 2 == 1:
                                wo0 = wtile('mo%d' % (h - 1)); wo1 = wtile('mo%d' % h)
                                for it in range(4):
                                    i0 = it * 512
                                    for d in range(8):
                                        py, pyb = bank(roty())
                                        MM(py, wo0[0][:, 0, d * 128:(d + 1) * 128], onT[:, 0, i0:i0 + 512], True, False,
                                           [wo0[1], onb[0][it]], [pyb], signal=False)
                                        MM(py, wo1[0][:, 0, d * 128:(d + 1) * 128], onT[:, 1, i0:i0 + 512], False, True,
                                           [wo1[1], onb[1][it]], [pyb])
                                        TT(xT[:, d, i0:i0 + 512], xT[:, d, i0:i0 + 512], py, ALU.add, [xb[it], pyb], [xb[it]])

        tmpst = ExitStack()
        miscf = sb("miscf", [128, 512], F32, tmpst); miscb = Buf('misc')
        kb.dma('sp', miscf[:], misc_d, writes=[miscb])
        kb.op('dve', lambda e: e.tensor_copy(out=identb[:], in_=miscf[:, 256:384]), reads=[miscb], writes=[constb])
        kb.op('dve', lambda e: e.tensor_copy(out=negmb[:], in_=miscf[:, 384:512]), reads=[miscb], writes=[constb])
        kb.op('dve', lambda e: e.tensor_copy(out=rotb[:], in_=miscf[:, 128:256]), reads=[miscb], writes=[constb])
        kb.barrier()
        tmpst.close()

        for s in range(nseq):
            for tt in range(4):
                ts = slice(tt * 512, (tt + 1) * 512)
                kb.dma('sp', xT[:, :, ts], xin[s][:, :, ts], writes=[xb[tt]])
            for ly in layers:
                if ly == 'ret':
                    ret_layer()
                elif ly == 'mla':
                    mla_layer()
                elif ly == 'ffn0':
                    ffn_layer(0)
                elif ly == 'ffn1':
                    ffn_layer(1)
            for tt in range(4):
                ts = slice(tt * 512, (tt + 1) * 512)
                kb.dma('sp', xout[s][:, :, ts], xT[:, :, ts], reads=[xb[tt]], writes=[ob[tt]])
        kb.wait_all('sp', ob)
        kb.wait_all('act', ob)
    return nc


_PROG = {}


def _get_prog(nseq, layers):
    key = (nseq, tuple(layers))
    if key not in _PROG:
        _PROG[key] = build_program(nseq, layers)
    return _PROG[key]


def kernel(**inp):
    inp = {k: np.asarray(v) for k, v in inp.items()}
    x = inp['x'].astype(np.float32, copy=False)
    B = x.shape[0]
    nseq = B // NC8
    xl = np.ascontiguousarray(x.reshape(B, S, 8, 128).transpose(0, 3, 2, 1))
    wts = pack_weights(inp)
    prm = pack_params(inp)
    rrope, mrope, dtab, misc = const_tables()
    nc = _get_prog(nseq, ('ret', 'ffn0', 'mla', 'ffn1'))
    in_maps = []
    for c in range(NC8):
        in_maps.append({"xin": xl[c * nseq:(c + 1) * nseq], "wts": wts, "prm": prm, "rrope": rrope,
                        "mrope": mrope, "dtab": dtab, "misc": misc})
    res = run_bass_kernel_spmd(nc, in_maps, core_ids=list(range(NC8)))
    outs = [np.asarray(r["xout"]) for r in res.results]
    o = np.concatenate(outs, axis=0)
    return np.ascontiguousarray(o.transpose(0, 3, 2, 1).reshape(B, S, D)).astype(np.float32, copy=False)
```

```python
import numpy as np
from contextlib import ExitStack, contextmanager
import concourse.bass as bass
import concourse.mybir as mybir
from concourse.bass_utils import run_bass_kernel_spmd

F32 = mybir.dt.float32
BF16 = mybir.dt.bfloat16
AF = mybir.ActivationFunctionType
ALU = mybir.AluOpType

D = 1024
S = 2048
NC8 = 8
EPS = 1e-6
THETA = 10000.0
FFN = 2816
NF = FFN // 128
RING = 4
SLOT = 2048


def _kt(w):
    K, C = w.shape
    return np.ascontiguousarray(w.reshape(K // 128, 128, C).transpose(1, 0, 2))


def _col(v):
    n = v.shape[0] // 128
    return np.ascontiguousarray(v.reshape(n, 128).T)


class WPack:
    def __init__(self):
        self.parts = []
        self.index = {}
        self.off = 0

    def add(self, name, arr3):
        a = np.ascontiguousarray(arr3, dtype=np.float32)
        assert a.shape[0] == 128
        n = int(np.prod(a.shape[1:]))
        assert n <= SLOT, (name, a.shape)
        self.index[name] = (self.off, tuple(a.shape[1:]))
        self.parts.append(a.reshape(-1))
        self.off += 128 * n


def weight_index():
    idx = {}
    off = 0

    def add(name, kc, cols):
        nonlocal off
        idx[name] = (off, (kc, cols))
        off += 128 * kc * cols
    for h in range(4):
        add('rq%d' % h, 8, 256); add('rk%d' % h, 8, 256)
        add('rv%da' % h, 8, 256); add('rv%db' % h, 8, 256)
        add('rg%da' % h, 8, 256); add('rg%db' % h, 8, 256)
        add('ro%da' % h, 4, 512); add('ro%db' % h, 4, 512)
    for l in range(2):
        for f in range(NF):
            add('f%di%d' % (l, f), 8, 256)
        for d in range(8):
            add('f%do%d_0' % (l, d), 11, 128); add('f%do%d_1' % (l, d), 11, 128)
    add('mi0', 8, 256); add('mi1', 8, 256); add('mi2', 8, 256)
    for h in range(8):
        add('mq%d' % h, 3, 256)
    add('mkn', 2, 1024); add('mv', 2, 1024)
    for h in range(8):
        add('mo%d' % h, 1, 1024)
    return idx, off


def pack_weights(inp):
    wp = WPack()
    rwi = inp['ret_w_in'][0]; rwo = inp['ret_w_out'][0]
    for h in range(4):
        wp.add('rq%d' % h, _kt(rwi[:, h * 256:(h + 1) * 256]))
        wp.add('rk%d' % h, _kt(rwi[:, 1024 + h * 256:1024 + (h + 1) * 256]))
        wp.add('rv%da' % h, _kt(rwi[:, 2048 + h * 512:2048 + h * 512 + 256]))
        wp.add('rv%db' % h, _kt(rwi[:, 2048 + h * 512 + 256:2048 + (h + 1) * 512]))
        wp.add('rg%da' % h, _kt(rwi[:, 4096 + h * 512:4096 + h * 512 + 256]))
        wp.add('rg%db' % h, _kt(rwi[:, 4096 + h * 512 + 256:4096 + (h + 1) * 512]))
        wp.add('ro%da' % h, _kt(rwo[h * 512:(h + 1) * 512, 0:512]))
        wp.add('ro%db' % h, _kt(rwo[h * 512:(h + 1) * 512, 512:1024]))
    for l in range(2):
        wi = inp['ffn_w_in'][l]; wo = inp['ffn_w_out'][l]
        for f in range(NF):
            wp.add('f%di%d' % (l, f), _kt(np.concatenate(
                [wi[:, f * 128:(f + 1) * 128], wi[:, FFN + f * 128:FFN + (f + 1) * 128]], axis=1)))
        for d in range(8):
            wp.add('f%do%d_0' % (l, d), _kt(wo[0:1408, d * 128:(d + 1) * 128]))
            wp.add('f%do%d_1' % (l, d), _kt(wo[1408:2816, d * 128:(d + 1) * 128]))
    mwi = inp['mla_w_in'][0]
    wp.add('mi0', _kt(mwi[:, 0:256])); wp.add('mi1', _kt(mwi[:, 256:512]))
    wp.add('mi2', _kt(np.concatenate([mwi[:, 512:704], np.zeros((1024, 64), np.float32)], axis=1)))
    wqb = inp['mla_w_qb'][0]
    for h in range(8):
        wp.add('mq%d' % h, _kt(np.concatenate([wqb[:, h * 192:(h + 1) * 192], np.zeros((384, 64), np.float32)], axis=1)))
    wkvb = inp['mla_w_kvb'][0].reshape(256, 8, 256)
    wp.add('mkn', _kt(np.ascontiguousarray(wkvb[:, :, 0:128]).reshape(256, 1024)))
    wp.add('mv', _kt(np.ascontiguousarray(wkvb[:, :, 128:256]).reshape(256, 1024)))
    mwo = inp['mla_w_out'][0]
    for h in range(8):
        wp.add('mo%d' % h, _kt(mwo[h * 128:(h + 1) * 128, :]))
    idx, tot = weight_index()
    assert tot == wp.off
    for k in idx:
        assert idx[k] == wp.index[k], k
    return np.concatenate(wp.parts)


P_RETN, P_RETGN, P_MLAN, P_QN, P_KVN, P_QHN, P_QHR, P_KHN, P_KHR, P_FFNN, P_CW, P_CB = (
    0, 8, 24, 32, 35, 37, 38, 39, 40, 41, 57, 57 + 132)
NPRM = 57 + 132 + 44


def pack_params(inp):
    p = np.zeros((128, NPRM), np.float32)
    p[:, P_RETN:P_RETN + 8] = _col(inp['ret_norm'][0])
    p[:, P_RETGN:P_RETGN + 16] = _col(inp['ret_gn'][0].reshape(-1))
    p[:, P_MLAN:P_MLAN + 8] = _col(inp['mla_norm'][0])
    p[:, P_QN:P_QN + 3] = _col(inp['mla_q_norm'][0])
    p[:, P_KVN:P_KVN + 2] = _col(inp['mla_kv_norm'][0])
    p[:, P_QHN] = inp['mla_q_head_norm'][0][0:128]
    p[0:64, P_QHR] = inp['mla_q_head_norm'][0][128:192]
    p[:, P_KHN] = inp['mla_k_head_norm'][0][0:128]
    p[0:64, P_KHR] = inp['mla_k_head_norm'][0][128:192]
    for l in range(2):
        p[:, P_FFNN + 8 * l:P_FFNN + 8 * l + 8] = _col(inp['ffn_norm'][l])
        for k in range(3):
            p[:, P_CW + (l * 3 + k) * NF:P_CW + (l * 3 + k + 1) * NF] = _col(inp['ffn_conv_w'][l, k])
        p[:, P_CB + l * NF:P_CB + (l + 1) * NF] = _col(inp['ffn_conv_b'][l])
    return p


def const_tables():
    pos = np.arange(S, dtype=np.float64)
    inv = THETA ** (-np.arange(128, dtype=np.float64) / 128.0)
    ang = inv[:, None] * pos[None, :]
    rrope = np.stack([np.cos(ang), np.sin(ang)], axis=1).astype(np.float32)
    inv = THETA ** (-np.arange(32, dtype=np.float64) / 32.0)
    ang = np.concatenate([inv, inv])[:, None] * pos[None, :]
    mrope = np.zeros((128, 2, S), np.float32)
    mrope[0:64] = np.stack([np.cos(ang), np.sin(ang)], axis=1).astype(np.float32)
    dtab = np.zeros((4, 128, S), np.float64)
    p = np.arange(128)[:, None]
    m = np.arange(S)[None, :]
    for h in range(4):
        lg = np.log1p(-2.0 ** (-5.0 - h))
        t = np.exp(lg * (m - p).astype(np.float64))
        md = np.arange(128)[None, :]
        allowed = (p // 64) <= (md // 64)
        t[:, 0:128] = np.where(allowed, np.exp(lg * np.abs(md - p)), 0.0)
        dtab[h] = t * (256.0 ** -0.5)
    dtab = dtab.astype(np.float32)
    misc = np.zeros((128, 512), np.float32)
    misc[:, 0:128] = ((p // 64) <= (np.arange(128)[None, :] // 64)).astype(np.float32)
    for i in range(32):
        misc[32 + i, 128 + i] = -1.0
        misc[i, 128 + 32 + i] = 1.0
    misc[:, 256:384] = np.eye(128, dtype=np.float32)
    misc[:, 384:512] = np.where(misc[:, 0:128] > 0, 0.0, -30000.0)
    return rrope, mrope, dtab, misc


class Buf:
    __slots__ = ('name', 'w', 'r', 'dsem', 'dcnt')

    def __init__(self, name):
        self.name = name; self.w = None; self.r = {}; self.dsem = None; self.dcnt = 0


class KB:
    def __init__(self, nc, es):
        self.nc = nc; self.es = es
        self.eng = {'pe': nc.tensor, 'act': nc.scalar, 'dve': nc.vector, 'pool': nc.gpsimd, 'sp': nc.sync}
        self.semh = {}
        for e in self.eng:
            self.semh[e] = es.enter_context(nc.semaphore('sem_' + e))
        self.cnt = {e: 0 for e in self.eng}
        self.pend = {e: False for e in self.eng}
        self.seen = {e: {} for e in self.eng}
        self.bar = {}
        self.nops = 0

    def _waits(self, e, reads, writes):
        deps = {}

        def add(tok):
            if tok is None:
                return
            k, v = tok
            if deps.get(k, 0) < v:
                deps[k] = v
        for b in reads:
            add(b.w)
        for b in writes:
            add(b.w)
            for t in b.r.values():
                add(t)
        if e != 'pool':
            for k, v in self.bar.items():
                add((k, v))
        eng = self.eng[e]; seen = self.seen[e]
        for k, v in deps.items():
            if k == e and e == 'pe':
                continue
            if seen.get(k, 0) >= v:
                continue
            if k in self.cnt:
                assert v <= self.cnt[k], ('future dependency', e, k, v, self.cnt[k])
            eng.wait_ge(self.semh[k], v)
            seen[k] = v

    def op(self, e, fn, reads=(), writes=(), signal=True):
        self._waits(e, reads, writes)
        ins = fn(self.eng[e])
        self.nops += 1
        if signal:
            self.cnt[e] += 1
            ins.then_inc(self.semh[e], 1)
            tok = (e, self.cnt[e]); self.pend[e] = False
        else:
            tok = (e, self.cnt[e] + 1); self.pend[e] = True
        for b in reads:
            b.r[e] = tok
        for b in writes:
            b.w = tok; b.r = {}
        return ins

    def dma(self, e, out, in_, reads=(), writes=(), **kw):
        tgt = writes[0]
        if tgt.dsem is None:
            self.nsem = getattr(self, 'nsem', 0) + 1
            tgt.dsem = 'd%d_%s' % (self.nsem, tgt.name)
            self.semh[tgt.dsem] = self.es.enter_context(self.nc.semaphore(tgt.dsem))
        self._waits(e, reads, writes)
        ins = self.eng[e].dma_start(out=out, in_=in_, **kw)
        tgt.dcnt += 16
        ins.then_inc(self.semh[tgt.dsem], 16)
        tok = (tgt.dsem, tgt.dcnt)
        for b in reads:
            b.r[tgt.dsem] = tok
        for b in writes:
            b.w = tok; b.r = {}
        return ins

    def barrier(self):
        for e in ('pe', 'act', 'dve'):
            assert not self.pend[e], e
            self.bar[e] = self.cnt[e]

    def wait_all(self, e, bufs):
        self._waits(e, bufs, ())


class Defer:
    def __init__(self):
        self.q = []; self.t = 0; self.n = 0

    def add(self, delay, fn, tag=0):
        self.n += 1
        self.q.append((self.t + delay, self.n, tag, fn))

    def tick(self):
        self.t += 1
        due = sorted([x for x in self.q if x[0] <= self.t], key=lambda x: (x[0], x[1]))
        self.q = [x for x in self.q if x[0] > self.t]
        for x in due:
            x[3]()

    def flush(self, maxtag=None):
        while True:
            sel = sorted([x for x in self.q if maxtag is None or x[2] <= maxtag], key=lambda x: (x[0], x[1]))
            if not sel:
                return
            self.q = [x for x in self.q if not (maxtag is None or x[2] <= maxtag)]
            for x in sel:
                x[3]()


class Rot:
    def __init__(self, items):
        self.items = list(items); self.i = 0

    def __call__(self):
        x = self.items[self.i % len(self.items)]; self.i += 1
        return x


def build_program(nseq=2, layers=('ret', 'ffn0', 'mla', 'ffn1')):
    nc = bass.Bass("TRN2", target_bir_lowering=False)
    widx, wtot = weight_index()
    xin = nc.dram_tensor("xin", [nseq, 128, 8, S], F32, kind="ExternalInput").ap()
    wts = nc.dram_tensor("wts", [wtot], F32, kind="ExternalInput").ap()
    prm_d = nc.dram_tensor("prm", [128, NPRM], F32, kind="ExternalInput").ap()
    rrope_d = nc.dram_tensor("rrope", [128, 2, S], F32, kind="ExternalInput").ap()
    mrope_d = nc.dram_tensor("mrope", [128, 2, S], F32, kind="ExternalInput").ap()
    dtab_d = nc.dram_tensor("dtab", [4, 128, S], F32, kind="ExternalInput").ap()
    misc_d = nc.dram_tensor("misc", [128, 512], F32, kind="ExternalInput").ap()
    xout = nc.dram_tensor("xout", [nseq, 128, 8, S], F32, kind="ExternalOutput").ap()

    with ExitStack() as es:
        kb = KB(nc, es)

        uid = [0]

        def sb(name, shape, dt, stack=es):
            uid[0] += 1
            return stack.enter_context(nc.sbuf_tensor("%s_%d" % (name, uid[0]), shape, dt))

        xT = sb("xT", [128, 8, S], F32)
        xb = [Buf('x%d' % i) for i in range(4)]
        ob = [Buf('o%d' % i) for i in range(4)]
        ring = sb("ring", [128, RING, SLOT], BF16)
        ringb = [Buf('ring%d' % i) for i in range(RING)]
        prm = sb("prm_s", [128, NPRM], F32); prmb = Buf('prm')
        identb = sb("identb", [128, 128], BF16)
        negmb = sb("negmb", [128, 128], BF16)
        ones = sb("ones", [128, 128], BF16); constb = Buf('const')
        psum = es.enter_context(nc.psum_tensor("psum", [128, 8, 512], F32))
        pb = [Buf('ps%d' % i) for i in range(8)]

        kb.dma('sp', prm[:], prm_d, writes=[prmb])
        kb.op('dve', lambda e: e.memset(ones[:], 1.0), writes=[constb])
        rotb = sb("rotb", [128, 128], BF16)

        ring_i = [0]

        def wtile(name):
            off, (kc, cols) = widx[name]
            n = kc * cols
            slot = ring_i[0] % RING; ring_i[0] += 1
            src = wts[off:off + 128 * n].rearrange("(p n) -> p n", p=128)
            kb.dma('pool', ring[:, slot, 0:n], src, writes=[ringb[slot]], max_dma_last_dim=8192)
            return ring[:, slot, 0:n].rearrange("p (k c) -> p k c", k=kc), ringb[slot]

        @contextmanager
        def scope():
            with ExitStack() as st:
                yield st
                kb.barrier()

        def MM(out, lhsT, rhs, start, stop, reads, writes, signal=True):
            return kb.op('pe', lambda e: e.matmul(out, lhsT, rhs, start=start, stop=stop), reads, writes, signal)

        def ACT(out, in_, func, reads, writes, **kw):
            return kb.op('act', lambda e: e.activation(out=out, in_=in_, func=func, **kw), reads, writes)

        def TT(out, a, b, op, reads, writes):
            return kb.op('dve', lambda e: e.tensor_tensor(out=out, in0=a, in1=b, op=op), reads, writes)

        def STT(out, in0, scalar, in1, op0, op1, reads, writes):
            return kb.op('dve', lambda e: e.scalar_tensor_tensor(out=out, in0=in0, scalar=scalar, in1=in1,
                                                                 op0=op0, op1=op1), reads, writes)

        def RECIP(out, in_, reads, writes):
            return kb.op('dve', lambda e: e.reciprocal(out=out, in_=in_), reads, writes)

        def bank(i):
            return psum[:, i, :], pb[i]

        sqt = sb("sqt", [128, 2, 512], BF16); sqtb = [Buf('sqt0'), Buf('sqt1')]
        rstd = sb("rstd", [128, 512], F32); rstdb = Buf('rstd')

        lnb_t = sb("lnb", [128, 512], F32); lnbb = Buf('lnb')

        def rms_rstd(ps_ap, ps_buf, nfeat, npart=128, out=None, outb=None):
            if out is None:
                out, outb = rstd, rstdb
            ACT(lnb_t[0:npart, :], ps_ap, AF.Ln, [ps_buf], [lnbb], bias=EPS, scale=1.0 / nfeat)
            ACT(out[0:npart, :], lnb_t[0:npart, :], AF.Exp, [lnbb], [outb], scale=-0.5)

        def rmsnorm_tile(t0, gain_col, hT, hbuf, hcol0, pbank):
            tile_i = t0 // 512
            ps, psb = bank(pbank)
            for c in range(8):
                ACT(sqt[:, c % 2, :], xT[:, c, t0:t0 + 512], AF.Square, [xb[tile_i]], [sqtb[c % 2]])
                MM(ps, ones[:], sqt[:, c % 2, :], c == 0, c == 7, [sqtb[c % 2], constb], [psb])
            rms_rstd(ps, psb, D)
            for c in range(8):
                STT(hT[:, c, hcol0:hcol0 + 512], xT[:, c, t0:t0 + 512], prm[:, gain_col + c:gain_col + c + 1],
                    rstd[:], ALU.mult, ALU.mult, [xb[tile_i], prmb, rstdb], [hbuf])

        def ffn_layer(l):
            with scope() as st:
                hT = sb("f_hT", [128, 8, S], BF16, st); hb = [[Buf('f_h%d_%d' % (i, c)) for c in range(8)] for i in range(4)]
                uT = sb("f_uT", [128, NF, 1024], BF16, st); ub = [Buf('f_u%d' % f) for f in range(NF)]
                halo = sb("f_halo", [128, 2, NF, 2], F32, st); halob = [[Buf('f_halo%d_%d' % (a, f)) for f in range(NF)] for a in range(2)]
                cbuf = sb("f_c", [128, 2, 512], F32, st); cb = [Buf('f_c0'), Buf('f_c1')]
                sbuf = sb("f_s", [128, 2, 512], F32, st); sbb = [Buf('f_s0'), Buf('f_s1')]
                sq8 = sb("f_sq8", [128, 8, 512], BF16, st); sq8b = Buf('f_sq8')
                rota = Rot([0, 1, 2]); rotg = Rot([3, 4, 5]); roty = Rot([6, 7])
                step = 0
                cw = lambda k, f: prm[:, P_CW + (l * 3 + k) * NF + f:P_CW + (l * 3 + k) * NF + f + 1]
                cbias = lambda f: prm[:, P_CB + l * NF + f:P_CB + l * NF + f + 1]
                gcol = P_FFNN + 8 * l

                def norm_a(ti, cs=range(8)):
                    for c in cs:
                        ACT(sq8[:, c, :], xT[:, c, ti * 512:(ti + 1) * 512], AF.Square, [xb[ti]], [sq8b])

                def norm_b(ti):
                    ps, psb = bank(6 + ti % 2)
                    for c in range(8):
                        MM(ps, ones[:], sq8[:, c, :], c == 0, c == 7, [sq8b, constb], [psb], signal=(c == 7))
                    rms_rstd(ps, psb, D)

                def norm_c(ti, cs=range(8)):
                    for c in cs:
                        STT(hT[:, c, ti * 512:(ti + 1) * 512], xT[:, c, ti * 512:(ti + 1) * 512], prm[:, gcol + c:gcol + c + 1],
                            rstd[:], ALU.mult, ALU.mult, [xb[ti], prmb, rstdb], [hb[ti][c]])

                fscr = sb("f_scr", [128, 2], F32, st); fscrb = Buf('f_scr')
                ACT(fscr[:, 0:1], ones[:, 0:1], AF.Ln, [constb], [fscrb])
                for ti in range(2):
                    ACT(sq8[:], xT[:, :, ti * 512:(ti + 1) * 512], AF.Square, [xb[ti]], [sq8b])
                    norm_b(ti); norm_c(ti)
                for T in range(2):
                    for f in range(NF):
                        if T == 0:
                            if f <= 7:
                                norm_a(2, [f])
                            if f == 8:
                                norm_b(2)
                            if 9 <= f <= 16:
                                norm_c(2, [f - 9]); norm_a(3, [f - 9])
                            if f == 17:
                                norm_b(3)
                            if f >= 18:
                                norm_c(3, [2 * (f - 18), 2 * (f - 18) + 1])
                        wt, wbuf = wtile('f%di%d' % (l, f))
                        for tt in range(2):
                            pa, pab = bank(rota()); pg, pgb = bank(rotg())
                            hs = slice(T * 1024 + tt * 512, T * 1024 + (tt + 1) * 512)
                            us = slice(tt * 512, (tt + 1) * 512)
                            hbi = hb[T * 2 + tt]
                            for c in range(8):
                                MM(pa, wt[:, c, 0:128], hT[:, c, hs], c == 0, c == 7, [wbuf, hbi[c]], [pab], signal=(c == 7))
                            for c in range(8):
                                MM(pg, wt[:, c, 128:256], hT[:, c, hs], c == 0, c == 7, [wbuf, hbi[c]], [pgb], signal=(c == 7))
                            k = step % 2; step += 1
                            cc = cbuf[:, k, :]
                            ACT(cc, pg, AF.Identity, [pgb, prmb], [cb[k]], scale=cw(2, f), bias=cbias(f))
                            hp = (T * 2 + tt) % 2
                            STT(cc[:, 1:512], pg[:, 0:511], cw(1, f), cc[:, 1:512], ALU.mult, ALU.add, [pgb, prmb, cb[k]], [cb[k]])
                            STT(cc[:, 2:512], pg[:, 0:510], cw(0, f), cc[:, 2:512], ALU.mult, ALU.add, [pgb, prmb, cb[k]], [cb[k]])
                            kb.op('dve', lambda e, hp=hp, f=f, pg=pg: e.tensor_copy(out=halo[:, hp, f, :], in_=pg[:, 510:512]),
                                  [pgb], [halob[hp][f]])
                            if not (T == 0 and tt == 0):
                                STT(cc[:, 0:1], halo[:, 1 - hp, f, 1:2], cw(1, f), cc[:, 0:1], ALU.mult, ALU.add, [halob[1 - hp][f], prmb, cb[k]], [cb[k]])
                                STT(cc[:, 0:2], halo[:, 1 - hp, f, 0:2], cw(0, f), cc[:, 0:2], ALU.mult, ALU.add, [halob[1 - hp][f], prmb, cb[k]], [cb[k]])
                            ACT(sbuf[:, k, :], cc, AF.Silu, [cb[k]], [sbb[k]])
                            TT(uT[:, f, us], sbuf[:, k, :], pa, ALU.mult, [sbb[k], pab], [ub[f]])
                    for d in range(8):
                        wa, wab = wtile('f%do%d_0' % (l, d)); wb_, wbb = wtile('f%do%d_1' % (l, d))
                        for tt in range(2):
                            py, pyb = bank(roty())
                            us = slice(tt * 512, (tt + 1) * 512)
                            ts = slice(T * 1024 + tt * 512, T * 1024 + (tt + 1) * 512)
                            for kk in range(NF):
                                w, wbf = (wa, wab) if kk < 11 else (wb_, wbb)
                                MM(py, w[:, kk % 11, :], uT[:, kk, us], kk == 0, kk == NF - 1, [wbf, ub[kk]], [pyb],
                                   signal=(kk == NF - 1))
                            xi = xb[T * 2 + tt]
                            TT(xT[:, d, ts], xT[:, d, ts], py, ALU.add, [xi, pyb], [xi])

        def ret_layer():
            with scope() as st:
                hT = sb("r_hT", [128, 8, S], BF16, st); hb = [Buf('r_h%d' % i) for i in range(4)]
                rope = sb("r_rope", [128, 2, S], F32, st); ropeb = Buf('r_rope')
                dt_ = sb("r_dt", [128, S], F32, st); dtb = Buf('r_dt')
                qT = sb("r_q", [128, 2, S], BF16, st); qb = [Buf('r_q%d' % i) for i in range(4)]
                kT = sb("r_k", [128, 2, S], BF16, st); kbf = [Buf('r_k%d' % i) for i in range(4)]
                vt = sb("r_v", [128, 16, 512], BF16, st); vb = [Buf('r_v%d' % i) for i in range(16)]
                t12 = sb("r_t", [128, 2, 512], F32, st); tb_ = [Buf('r_t0'), Buf('r_t1')]
                P = sb("r_P", [128, 2, 512], BF16, st); Pb = [Buf('r_P0'), Buf('r_P1')]
                sg = sb("r_sg", [128, 2, 4, 512], BF16, st); sgb = [Buf('r_sg0'), Buf('r_sg1')]
                sqn = sb("r_sqn", [128, 4, 512], BF16, st); sqnv = [Buf('r_sqn%d' % i) for i in range(4)]
                wv_ = sb("r_w", [128, 512], F32, st); wvb = Buf('r_w')
                u = sb("r_u", [128, 2, 4, 512], BF16, st); ub = [Buf('r_u0'), Buf('r_u1')]
                dq = Defer()
                scr = sb("r_scr", [128, 2], F32, st); scrb = Buf('r_scr')
                kb.dma('sp', rope[:], rrope_d, writes=[ropeb])
                for tt in range(4):
                    rmsnorm_tile(tt * 512, P_RETN, hT, hb[tt], tt * 512, 6 + tt % 2)
                rotp = Rot([0, 1, 2, 3, 4, 5])
                rots = Rot([4, 5]); rotx = Rot([6, 7])
                pcnt = [0]
                gw = {}

                def gproj(h, it, par, groups, banks=None):
                    evs = []
                    if (h, it) not in gw:
                        gw.clear()
                        gw[(h, it)] = (wtile('rg%da' % h), wtile('rg%db' % h))
                    (wga, wgab), (wgb_, wgbb) = gw[(h, it)]
                    i0 = it * 512
                    for gc in groups:
                        w, wbf = (wga, wgab) if gc < 2 else (wgb_, wgbb)
                        bi = rotx() if banks is None else banks[gc]
                        pg, pgb = bank(bi)
                        for c in range(8):
                            MM(pg, w[:, c, (gc % 2) * 128:(gc % 2 + 1) * 128], hT[:, c, i0:i0 + 512], c == 0, c == 7,
                               [wbf, hb[it]], [pgb], signal=(c == 7))
                        evs.append(lambda pg=pg, pgb=pgb, gc=gc: ACT(sg[:, par, gc, :], pg, AF.Silu, [pgb], [sgb[par]]))
                    return evs

                tiles = [(h, it) for h in range(4) for it in range(4)]
                for h in range(4):
                    kb.dma('sp', dt_[:], dtab_d[h], writes=[dtb])
                    for nm, dst, dstb in (('rq%d' % h, qT, qb), ('rk%d' % h, kT, kbf)):
                        w, wbf = wtile(nm)
                        for tt in range(4):
                            ts = slice(tt * 512, (tt + 1) * 512)
                            p1, p1b = bank(rotp()); p2, p2b = bank(rotp())
                            for c in range(8):
                                MM(p1, w[:, c, 0:128], hT[:, c, ts], c == 0, c == 7, [wbf, hb[tt]], [p1b], signal=(c == 7))
                            for c in range(8):
                                MM(p2, w[:, c, 128:256], hT[:, c, ts], c == 0, c == 7, [wbf, hb[tt]], [p2b], signal=(c == 7))
                            cs = rope[:, 0, ts]; sn = rope[:, 1, ts]
                            TT(t12[:, 0, :], p1, cs, ALU.mult, [p1b, ropeb], [tb_[0]])
                            TT(t12[:, 1, :], p2, sn, ALU.mult, [p2b, ropeb], [tb_[1]])
                            TT(dst[:, 0, ts], t12[:, 0, :], t12[:, 1, :], ALU.subtract, [tb_[0], tb_[1]], [dstb[tt]])
                            TT(t12[:, 0, :], p2, cs, ALU.mult, [p2b, ropeb], [tb_[0]])
                            TT(t12[:, 1, :], p1, sn, ALU.mult, [p1b, ropeb], [tb_[1]])
                            TT(dst[:, 1, ts], t12[:, 0, :], t12[:, 1, :], ALU.add, [tb_[0], tb_[1]], [dstb[tt]])
                            dq.tick()
                    dq.flush()
                    wva, wvab = wtile('rv%da' % h); wvb_, wvbb = wtile('rv%db' % h)
                    for tb in range(16):
                        pv, pvb = bank(rotp())
                        for half, (w, wbf) in enumerate(((wva, wvab), (wvb_, wvbb))):
                            for c in range(8):
                                MM(pv[:, half * 256:(half + 1) * 256], hT[:, c, tb * 128:(tb + 1) * 128], w[:, c, :],
                                   c == 0, c == 7, [wbf, hb[tb // 4]], [pvb], signal=(c == 7))
                        ACT(vt[:, tb, :], pv, AF.Copy, [pvb], [vb[tb]])
                    if h == 0:
                        for gc in range(4):
                            for ev in gproj(0, 0, 0, [gc]):
                                ev()
                    for it in range(4):
                        g = h * 4 + it; par = g % 2
                        i0 = it * 512
                        njb = 4 * it + 4

                        def emit_S(jb):
                            c0 = max(jb * 128 - i0, 0); N = 512 - c0; ilo = i0 + c0
                            bi = rots()
                            ps, psb = bank(bi)
                            for c in range(2):
                                MM(ps[:, 0:N], kT[:, c, jb * 128:(jb + 1) * 128], qT[:, c, ilo:ilo + N], c == 0, c == 1,
                                   [kbf[jb // 4], qb[it]], [psb], signal=(c == 1))
                            return ps, psb

                        nxt = emit_S(0)
                        for jb in range(njb):
                            ps, psb = nxt
                            if jb + 1 < njb:
                                nxt = emit_S(jb + 1)
                            c0 = max(jb * 128 - i0, 0); N = 512 - c0; ilo = i0 + c0
                            k = pcnt[0] % 2; pcnt[0] += 1
                            doff = ilo - jb * 128
                            TT(P[:, k, 0:N], ps[:, 0:N], dt_[:, doff:doff + N], ALU.mult, [psb, dtb], [Pb[k]])
                            if jb == njb - 2:
                                ACT(scr[:, 0:1], ones[:, 0:1], AF.Ln, [constb], [scrb])
                            dq.tick()
                            for vc in range(4):
                                MM(psum[:, vc, c0:c0 + N], vt[:, jb, vc * 128:(vc + 1) * 128], P[:, k, 0:N], jb == 0, jb == njb - 1,
                                   [vb[jb], Pb[k]], [pb[vc]], signal=(vc == 3))
                        dq.flush(maxtag=g - 2)
                        nx = tiles[g + 1] if g + 1 < 16 else None
                        gb = {0: 6, 1: 4, 2: 5, 3: 6}
                        evs = gproj(nx[0], nx[1], 1 - par, [0], gb) if nx else []
                        pss, pssb = bank(7)
                        for vc in range(4):
                            ACT(sqn[:, vc, :], psum[:, vc, :], AF.Square, [pb[vc]], [sqnv[vc]])
                        for vc in range(4):
                            MM(pss, ones[:], sqn[:, vc, :], vc == 0, vc == 3, [sqnv[vc], constb], [pssb], signal=(vc == 3))
                        rms_rstd(pss, pssb, 512)
                        if nx:
                            evs += gproj(nx[0], nx[1], 1 - par, [1], gb)
                            evs += gproj(nx[0], nx[1], 1 - par, [2], gb)
                        for ev in evs:
                            ev()
                        for vc in range(4):
                            TT(wv_[:], sg[:, par, vc, :], rstd[:], ALU.mult, [sgb[par], rstdb], [wvb])
                            gcol = P_RETGN + h * 4 + vc
                            STT(u[:, par, vc, :], psum[:, vc, :], prm[:, gcol:gcol + 1], wv_[:], ALU.mult, ALU.mult,
                                [pb[vc], prmb, wvb], [ub[par]])
                        if nx:
                            for ev in gproj(nx[0], nx[1], 1 - par, [3], gb):
                                ev()
                        dq.tick()
                        wo = {}

                        def ychunk(d, h=h, i0=i0, it=it, par=par, wo=wo):
                            key = 'a' if d < 4 else 'b'
                            if key not in wo:
                                wo[key] = wtile('ro%d%s' % (h, key))
                            w, wbf = wo[key]
                            py, pyb = bank(rotx())
                            for vc in range(4):
                                MM(py, w[:, vc, (d % 4) * 128:(d % 4 + 1) * 128], u[:, par, vc, :], vc == 0, vc == 3,
                                   [wbf, ub[par]], [pyb], signal=(vc == 3))
                            TT(xT[:, d, i0:i0 + 512], xT[:, d, i0:i0 + 512], py, ALU.add, [xb[it], pyb], [xb[it]])
                        late = (4 * nx[1] + 4 + 1) if (nx and nx[0] == h) else 4
                        for d in range(8):
                            dq.add(1 + d // 2 if d < 6 else late, (lambda d=d, f=ychunk: f(d)), tag=g)
                dq.flush()

        def mla_layer():
            with scope() as so:
                cqn = sb("m_cqn", [128, 3, S], BF16, so); cqb = [Buf('m_cq%d' % i) for i in range(4)]
                ckvn = sb("m_ckvn", [128, 2, S], BF16, so); ckb = [Buf('m_ck%d' % i) for i in range(4)]
                KrT = sb("m_kr", [128, S], BF16, so); krb = [Buf('m_kr%d' % i) for i in range(4)]
                sqkr = sb("m_sqkr", [128, S], BF16, so); sqkrb = [Buf('m_sqkr%d' % i) for i in range(4)]
                mrope = sb("m_rope", [128, 2, S], F32, so); mropeb = Buf('m_rope')
                rstdk = sb("m_rstdk", [128, 16, 4], F32, so); rstdkb = Buf('m_rstdk')
                rtk = sb("m_rtk", [128, 16, 4], F32, so); rtkb = Buf('m_rtk')
                xg = sb("m_xg", [128, 512], BF16, so); xgb = Buf('m_xg')
                t12 = sb("m_t", [128, 2, 512], F32, so); tb_ = [Buf('m_t0'), Buf('m_t1')]
                kb.dma('sp', mrope[:], mrope_d, writes=[mropeb])

                def rope64_mm(src_ap, src_buf, pbank):
                    pr, prb = bank(pbank)
                    MM(pr, rotb[:], src_ap, True, True, [constb, src_buf], [prb])
                    return pr, prb

                def rope64_ew(src_ap, src_buf, pr, prb, dst_ap, dst_buf, ts):
                    TT(t12[:, 0, :], src_ap, mrope[:, 0, ts], ALU.mult, [src_buf, mropeb], [tb_[0]])
                    TT(t12[:, 1, :], pr, mrope[:, 1, ts], ALU.mult, [prb, mropeb], [tb_[1]])
                    TT(dst_ap, t12[:, 0, :], t12[:, 1, :], ALU.add, [tb_[0], tb_[1]], [dst_buf])

                with scope() as s1:
                    hT = sb("m_hT", [128, 8, S], BF16, s1); hb = [Buf('m_h%d' % i) for i in range(4)]
                    sq3 = sb("m_sq3", [128, 5, 512], BF16, s1); sq3b = Buf('m_sq3'); sq3c = Buf('m_sq3c')
                    rstd2 = sb("m_rstd2", [128, 512], F32, s1); rstd2b = Buf('m_rstd2')
                    for tt in range(4):
                        rmsnorm_tile(tt * 512, P_MLAN, hT, hb[tt], tt * 512, 6 + tt % 2)
                    wi = [wtile('mi0'), wtile('mi1'), wtile('mi2')]
                    for tt in range(4):
                        ts = slice(tt * 512, (tt + 1) * 512)
                        def proj(fcs):
                            for fc in fcs:
                                w, wbf = wi[fc // 2]; col = (fc % 2) * 128
                                for c in range(8):
                                    MM(psum[:, fc, :], w[:, c, col:col + 128], hT[:, c, ts], c == 0, c == 7, [wbf, hb[tt]], [pb[fc]],
                                       signal=(c == 7))
                        proj(range(0, 3))
                        ACT(sq3[:, 0:3, :], psum[:, 0:3, :], AF.Square, [pb[0], pb[1], pb[2]], [sq3b])
                        proj(range(3, 6))
                        ACT(sq3[:, 3:5, :], psum[:, 3:5, :], AF.Square, [pb[3], pb[4]], [sq3c])
                        pss, pssb = bank(6)
                        for k in range(3):
                            MM(pss, ones[:], sq3[:, k, :], k == 0, k == 2, [sq3b, constb], [pssb], signal=(k == 2))
                        rms_rstd(pss, pssb, 384)
                        ACT(sqkr[:, ts], psum[:, 5, :], AF.Square, [pb[5]], [sqkrb[tt]])
                        ACT(xg[:], psum[:, 5, :], AF.Identity, [pb[5], prmb], [xgb], scale=prm[:, P_KHR:P_KHR + 1])
                        for k in range(3):
                            STT(cqn[:, k, ts], psum[:, k, :], prm[:, P_QN + k:P_QN + k + 1], rstd[:], ALU.mult, ALU.mult,
                                [pb[k], prmb, rstdb], [cqb[tt]])
                        pss, pssb = bank(7)
                        for k in range(2):
                            MM(pss, ones[:], sq3[:, 3 + k, :], k == 0, k == 1, [sq3c, constb], [pssb], signal=(k == 1))
                        rms_rstd(pss, pssb, 256, out=rstd2, outb=rstd2b)
                        for k in range(2):
                            STT(ckvn[:, k, ts], psum[:, 3 + k, :], prm[:, P_KVN + k:P_KVN + k + 1], rstd2[:], ALU.mult, ALU.mult,
                                [pb[3 + k], prmb, rstd2b], [ckb[tt]])
                        pr, prb = rope64_mm(xg[:], xgb, 6)
                        rope64_ew(xg[:], xgb, pr, prb, KrT[:, ts], krb[tt], ts)

                for G in range(2):
                    with scope() as s2:
                        KnT = sb("m_kn", [128, 4, S], BF16, s2); knb = [[Buf('m_kn%d_%d' % (a, i)) for i in range(4)] for a in range(4)]
                        Vt = sb("m_v", [128, 16, 512], BF16, s2); vb = [Buf('m_v%d' % i) for i in range(16)]
                        QnT = sb("m_qn", [128, 2, S], BF16, s2); qnb = [[Buf('m_qn%d_%d' % (a, i)) for i in range(4)] for a in range(2)]
                        QrT = sb("m_qr", [128, 2, S], BF16, s2); qrb = [[Buf('m_qr%d_%d' % (a, i)) for i in range(4)] for a in range(2)]
                        onT = sb("m_on", [128, 2, S], BF16, s2); onb = [[Buf('m_on%d_%d' % (a, i)) for i in range(4)] for a in range(2)]
                        sqk = sb("m_sqk", [128, 2, 512], BF16, s2); sqkb = [Buf('m_sqk0'), Buf('m_sqk1')]
                        sq2 = sb("m_sq2", [128, 512], BF16, s2); sq2b = Buf('m_sq2')
                        P = sb("m_P", [128, 3, 512], BF16, s2); Pb = [Buf('m_P0'), Buf('m_P1'), Buf('m_P2')]
                        rden, rdenb = rstd, rstdb
                        rstdq = sb("m_rstdq", [128, 512], F32, s2); rstdqb = Buf('m_rstdq')
                        Osb = sb("m_osb", [128, 512], F32, s2); Osbb = Buf('m_osb')
                        lnd = sb("m_lnd", [128, 512], F32, s2); lndb = Buf('m_lnd')
                        sqq = sb("m_sqq", [128, 512], BF16, s2); sqqb = Buf('m_sqq')
                        dq = Defer()
                        def qchain(h, tt, qp):
                            ts = slice(tt * 512, (tt + 1) * 512)
                            wq, wqb_ = wtile('mq%d' % h); cb0 = 0
                            pqn, pqnb = bank(5); pqr, pqrb = bank(6)
                            for c in range(3):
                                MM(pqn, wq[:, c, cb0:cb0 + 128], cqn[:, c, ts], c == 0, c == 2, [wqb_, cqb[tt]], [pqnb], signal=(c == 2))
                            for c in range(3):
                                MM(pqr, wq[:, c, cb0 + 128:cb0 + 256], cqn[:, c, ts], c == 0, c == 2, [wqb_, cqb[tt]], [pqrb],
                                   signal=(c == 2))
                            ACT(sqq[:], pqn, AF.Square, [pqnb], [sqqb])
                            dq.add(1, lambda: ACT(sq2[:], pqr, AF.Square, [pqrb], [sq2b]))

                            pssh = [None]

                            def stB():
                                pss, pssb = bank(7)
                                MM(pss, ones[:], sqq[:], True, False, [sqqb, constb], [pssb])
                                MM(pss, ones[:], sq2[:], False, True, [sq2b, constb], [pssb])
                                ACT(lnb_t[:], pss, AF.Ln, [pssb], [lnbb], bias=EPS, scale=1.0 / 192)

                            def stB2():
                                ACT(rstdq[:], lnb_t[:], AF.Exp, [lnbb], [rstdqb], scale=-0.5)

                            def stC():
                                STT(QnT[:, qp, ts], pqn, prm[:, P_QHN:P_QHN + 1], rstdq[:], ALU.mult, ALU.mult,
                                    [pqnb, prmb, rstdqb], [qnb[qp][tt]])
                                STT(xg[:], pqr, prm[:, P_QHR:P_QHR + 1], rstdq[:], ALU.mult, ALU.mult,
                                    [pqrb, prmb, rstdqb], [xgb])

                            def stD():
                                pr, prb = rope64_mm(xg[:], xgb, 7)
                                dq.add(2, lambda: rope64_ew(xg[:], xgb, pr, prb, QrT[:, qp, ts], qrb[qp][tt], ts))
                            dq.add(3, stB)
                            dq.add(4, stB2)
                            dq.add(6, stC)
                            dq.add(7, stD)

                        rotp = Rot([2, 3, 4])
                        wkn, wknb = wtile('mkn')
                        pssk, psskb = bank(1)
                        nsq = 0
                        kticks = [0]

                        def ktick():
                            if kticks[0] % 8 == 0 and kticks[0] // 8 < 4:
                                qchain(4 * G, kticks[0] // 8, 0)
                            kticks[0] += 1
                            dq.tick()

                        prev = None
                        for q4 in range(4):
                            ACT(Osb[:, q4 * 128:(q4 + 1) * 128], ones[:], AF.Identity, [constb, prmb], [Osbb], scale=prm[:, P_KHN:P_KHN + 1])
                        for hl in range(4):
                            h = 4 * G + hl
                            for tt in range(4):
                                ts = slice(tt * 512, (tt + 1) * 512)
                                pk, pkb = bank(rotp())
                                for c in range(2):
                                    MM(pk, wkn[:, c, h * 128:(h + 1) * 128], ckvn[:, c, ts], c == 0, c == 1, [wknb, ckb[tt]], [pkb],
                                       signal=(c == 1))
                                k = nsq % 2; nsq += 1
                                ACT(sqk[:, k, :], pk, AF.Square, [pkb], [sqkb[k]])

                                def tiny(hl=hl, tt=tt, k=k, pk=pk, pkb=pkb, ts=ts):
                                    TT(KnT[:, hl, ts], pk, Osb[:], ALU.mult, [pkb, Osbb, sqkb[k]], [knb[hl][tt]])
                                    for b in range(4):
                                        tbk = tt * 4 + b
                                        col = tbk * 4 + hl
                                        MM(pssk[:, col:col + 1], sqk[:, k, b * 128:(b + 1) * 128], ones[:, 0:1], True, False,
                                           [sqkb[k], constb], [psskb])
                                        MM(pssk[:, col:col + 1], sqkr[:, tbk * 128:(tbk + 1) * 128], ones[:, 0:1], False, True,
                                           [sqkrb[tt], constb], [psskb])
                                if prev is not None:
                                    prev()
                                prev = tiny
                                ktick()
                        prev()
                        f2 = lambda a: a[:].rearrange("p a b -> p (a b)")
                        ACT(f2(rtk), pssk[:, 0:64], AF.Ln, [psskb], [rtkb], bias=EPS, scale=1.0 / 192)
                        ACT(f2(rstdk), f2(rtk), AF.Exp, [rtkb], [rstdkb], scale=-0.5)
                        kb.op('dve', lambda e: e.tensor_scalar(out=f2(rstdk), in0=f2(rstdk),
                                                               scalar1=float(192.0 ** -0.5), scalar2=None, op0=ALU.mult),
                              [rstdkb], [rstdkb])
                        wv, wvb = wtile('mv')
                        for tbk in range(16):
                            pv, pvb = bank(rotp())
                            for c in range(2):
                                MM(pv, ckvn[:, c, tbk * 128:(tbk + 1) * 128], wv[:, c, G * 512:(G + 1) * 512], c == 0, c == 1,
                                   [wvb, ckb[tbk // 4]], [pvb], signal=(c == 1))
                            ACT(Vt[:, tbk, :], pv, AF.Copy, [pvb], [vb[tbk]])
                            ktick()

                        dq.flush()
                        rots = Rot([2, 3, 4]); roty = Rot([5, 6, 7])
                        pcnt = 0
                        for hl in range(4):
                            h = 4 * G + hl; qp = hl % 2
                            hstep = 0
                            for it in range(4):
                                i0 = it * 512
                                njb = 4 * it + 4

                                def emit_S(jb, i0=i0, it=it):
                                    c0 = max(jb * 128 - i0, 0); N = 512 - c0; ilo = i0 + c0
                                    ps, psb = bank(rots())
                                    MM(ps[:, 0:N], KnT[:, hl, jb * 128:(jb + 1) * 128], QnT[:, qp, ilo:ilo + N], True, False,
                                       [knb[hl][jb // 4], qnb[qp][it]], [psb], signal=False)
                                    diag = jb >= 4 * it
                                    MM(ps[:, 0:N], KrT[:, jb * 128:(jb + 1) * 128], QrT[:, qp, ilo:ilo + N], False, not diag,
                                       [krb[jb // 4], qrb[qp][it]], [psb], signal=not diag)
                                    if diag:
                                        MM(ps[:, 0:128], identb[:], negmb[:], False, True, [constb], [psb])
                                    return ps, psb
                                pend = [emit_S(0), emit_S(1)]
                                for jb in range(njb):
                                    ps, psb = pend.pop(0)
                                    if jb + 2 < njb:
                                        pend.append(emit_S(jb + 2))
                                    c0 = max(jb * 128 - i0, 0); N = 512 - c0
                                    k = pcnt % 3; pcnt += 1
                                    ACT(P[:, k, 0:N], ps[:, 0:N], AF.Exp, [psb, rstdkb], [Pb[k]], scale=rstdk[:, jb, hl:hl + 1])
                                    if hl + 1 < 4 and hstep in (4, 14, 24, 34):
                                        qchain(h + 1, (hstep - 4) // 10, 1 - qp)
                                    hstep += 1
                                    dq.tick()
                                    MM(psum[:, 0, c0:c0 + N], Vt[:, jb, hl * 128:(hl + 1) * 128], P[:, k, 0:N], jb == 0, jb == njb - 1,
                                       [vb[jb], Pb[k]], [pb[0]], signal=False)
                                    MM(psum[:, 1, c0:c0 + N], ones[:], P[:, k, 0:N], jb == 0, jb == njb - 1, [constb, Pb[k]], [pb[1]])
                                kb.op('dve', lambda e: e.tensor_copy(out=Osb[:], in_=psum[:, 0, :]), [pb[0]], [Osbb])
                                ACT(lnd[:], psum[:, 1, :], AF.Ln, [pb[1]], [lndb])

                                def fin(qp=qp, i0=i0, it=it):
                                    ACT(rden[:], lnd[:], AF.Exp, [lndb], [rdenb], scale=-1.0)
                                    TT(onT[:, qp, i0:i0 + 512], Osb[:], rden[:], ALU.mult, [Osbb, rdenb], [onb[qp][it]])
                                dq.add(1, fin)
                            dq.flush()
                            if hl % 2 == 1:
                                wo0 = wtile('mo%d' % (h - 1)); wo1 = wtile('mo%d' % h)
                                for it in range(4):
                                    i0 = it * 512
                                    for d in range(8):
                                        py, pyb = bank(roty())
                                        MM(py, wo0[0][:, 0, d * 128:(d + 1) * 128], onT[:, 0, i0:i0 + 512], True, False,
                                           [wo0[1], onb[0][it]], [pyb], signal=False)
                                        MM(py, wo1[0][:, 0, d * 128:(d + 1) * 128], onT[:, 1, i0:i0 + 512], False, True,
                                           [wo1[1], onb[1][it]], [pyb])
                                        TT(xT[:, d, i0:i0 + 512], xT[:, d, i0:i0 + 512], py, ALU.add, [xb[it], pyb], [xb[it]])

        tmpst = ExitStack()
        miscf = sb("miscf", [128, 512], F32, tmpst); miscb = Buf('misc')
        kb.dma('sp', miscf[:], misc_d, writes=[miscb])
        kb.op('dve', lambda e: e.tensor_copy(out=identb[:], in_=miscf[:, 256:384]), reads=[miscb], writes=[constb])
        kb.op('dve', lambda e: e.tensor_copy(out=negmb[:], in_=miscf[:, 384:512]), reads=[miscb], writes=[constb])
        kb.op('dve', lambda e: e.tensor_copy(out=rotb[:], in_=miscf[:, 128:256]), reads=[miscb], writes=[constb])
        kb.barrier()
        tmpst.close()

        for s in range(nseq):
            for tt in range(4):
                ts = slice(tt * 512, (tt + 1) * 512)
                kb.dma('sp', xT[:, :, ts], xin[s][:, :, ts], writes=[xb[tt]])
            for ly in layers:
                if ly == 'ret':
                    ret_layer()
                elif ly == 'mla':
                    mla_layer()
                elif ly == 'ffn0':
                    ffn_layer(0)
                elif ly == 'ffn1':
                    ffn_layer(1)
            for tt in range(4):
                ts = slice(tt * 512, (tt + 1) * 512)
                kb.dma('sp', xout[s][:, :, ts], xT[:, :, ts], reads=[xb[tt]], writes=[ob[tt]])
        kb.wait_all('sp', ob)
        kb.wait_all('act', ob)
    return nc


_PROG = {}


def _get_prog(nseq, layers):
    key = (nseq, tuple(layers))
    if key not in _PROG:
        _PROG[key] = build_program(nseq, layers)
    return _PROG[key]


def kernel(**inp):
    inp = {k: np.asarray(v) for k, v in inp.items()}
    x = inp['x'].astype(np.float32, copy=False)
    B = x.shape[0]
    nseq = B // NC8
    xl = np.ascontiguousarray(x.reshape(B, S, 8, 128).transpose(0, 3, 2, 1))
    wts = pack_weights(inp)
    prm = pack_params(inp)
    rrope, mrope, dtab, misc = const_tables()
    nc = _get_prog(nseq, ('ret', 'ffn0', 'mla', 'ffn1'))
    in_maps = []
    for c in range(NC8):
        in_maps.append({"xin": xl[c * nseq:(c + 1) * nseq], "wts": wts, "prm": prm, "rrope": rrope,
                        "mrope": mrope, "dtab": dtab, "misc": misc})
    res = run_bass_kernel_spmd(nc, in_maps, core_ids=list(range(NC8)))
    outs = [np.asarray(r["xout"]) for r in res.results]
    o = np.concatenate(outs, axis=0)
    return np.ascontiguousarray(o.transpose(0, 3, 2, 1).reshape(B, S, D)).astype(np.float32, copy=False)
```

```python
import numpy as np
from contextlib import ExitStack, contextmanager
import concourse.bass as bass
import concourse.mybir as mybir
from concourse.bass_utils import run_bass_kernel_spmd

F32 = mybir.dt.float32
BF16 = mybir.dt.bfloat16
AF = mybir.ActivationFunctionType
ALU = mybir.AluOpType

D = 1024
S = 2048
NC8 = 8
EPS = 1e-6
THETA = 10000.0
FFN = 2816
NF = FFN // 128
RING = 4
SLOT = 2048


def _kt(w):
    K, C = w.shape
    return np.ascontiguousarray(w.reshape(K // 128, 128, C).transpose(1, 0, 2))


def _col(v):
    n = v.shape[0] // 128
    return np.ascontiguousarray(v.reshape(n, 128).T)


class WPack:
    def __init__(self):
        self.parts = []
        self.index = {}
        self.off = 0

    def add(self, name, arr3):
        a = np.ascontiguousarray(arr3, dtype=np.float32)
        assert a.shape[0] == 128
        n = int(np.prod(a.shape[1:]))
        assert n <= SLOT, (name, a.shape)
        self.index[name] = (self.off, tuple(a.shape[1:]))
        self.parts.append(a.reshape(-1))
        self.off += 128 * n


def weight_index():
    idx = {}
    off = 0

    def add(name, kc, cols):
        nonlocal off
        idx[name] = (off, (kc, cols))
        off += 128 * kc * cols
    for h in range(4):
        add('rq%d' % h, 8, 256); add('rk%d' % h, 8, 256)
        add('rv%da' % h, 8, 256); add('rv%db' % h, 8, 256)
        add('rg%da' % h, 8, 256); add('rg%db' % h, 8, 256)
        add('ro%da' % h, 4, 512); add('ro%db' % h, 4, 512)
    for l in range(2):
        for f in range(NF):
            add('f%di%d' % (l, f), 8, 256)
        for d in range(8):
            add('f%do%d_0' % (l, d), 11, 128); add('f%do%d_1' % (l, d), 11, 128)
    add('mi0', 8, 256); add('mi1', 8, 256); add('mi2', 8, 256)
    for h in range(8):
        add('mq%d' % h, 3, 256)
    add('mkn', 2, 1024); add('mv', 2, 1024)
    for h in range(8):
        add('mo%d' % h, 1, 1024)
    return idx, off


def pack_weights(inp):
    wp = WPack()
    rwi = inp['ret_w_in'][0]; rwo = inp['ret_w_out'][0]
    for h in range(4):
        wp.add('rq%d' % h, _kt(rwi[:, h * 256:(h + 1) * 256]))
        wp.add('rk%d' % h, _kt(rwi[:, 1024 + h * 256:1024 + (h + 1) * 256]))
        wp.add('rv%da' % h, _kt(rwi[:, 2048 + h * 512:2048 + h * 512 + 256]))
        wp.add('rv%db' % h, _kt(rwi[:, 2048 + h * 512 + 256:2048 + (h + 1) * 512]))
        wp.add('rg%da' % h, _kt(rwi[:, 4096 + h * 512:4096 + h * 512 + 256]))
        wp.add('rg%db' % h, _kt(rwi[:, 4096 + h * 512 + 256:4096 + (h + 1) * 512]))
        wp.add('ro%da' % h, _kt(rwo[h * 512:(h + 1) * 512, 0:512]))
        wp.add('ro%db' % h, _kt(rwo[h * 512:(h + 1) * 512, 512:1024]))
    for l in range(2):
        wi = inp['ffn_w_in'][l]; wo = inp['ffn_w_out'][l]
        for f in range(NF):
            wp.add('f%di%d' % (l, f), _kt(np.concatenate(
                [wi[:, f * 128:(f + 1) * 128], wi[:, FFN + f * 128:FFN + (f + 1) * 128]], axis=1)))
        for d in range(8):
            wp.add('f%do%d_0' % (l, d), _kt(wo[0:1408, d * 128:(d + 1) * 128]))
            wp.add('f%do%d_1' % (l, d), _kt(wo[1408:2816, d * 128:(d + 1) * 128]))
    mwi = inp['mla_w_in'][0]
    wp.add('mi0', _kt(mwi[:, 0:256])); wp.add('mi1', _kt(mwi[:, 256:512]))
    wp.add('mi2', _kt(np.concatenate([mwi[:, 512:704], np.zeros((1024, 64), np.float32)], axis=1)))
    wqb = inp['mla_w_qb'][0]
    for h in range(8):
        wp.add('mq%d' % h, _kt(np.concatenate([wqb[:, h * 192:(h + 1) * 192], np.zeros((384, 64), np.float32)], axis=1)))
    wkvb = inp['mla_w_kvb'][0].reshape(256, 8, 256)
    wp.add('mkn', _kt(np.ascontiguousarray(wkvb[:, :, 0:128]).reshape(256, 1024)))
    wp.add('mv', _kt(np.ascontiguousarray(wkvb[:, :, 128:256]).reshape(256, 1024)))
    mwo = inp['mla_w_out'][0]
    for h in range(8):
        wp.add('mo%d' % h, _kt(mwo[h * 128:(h + 1) * 128, :]))
    idx, tot = weight_index()
    assert tot == wp.off
    for k in idx:
        assert idx[k] == wp.index[k], k
    return np.concatenate(wp.parts)


P_RETN, P_RETGN, P_MLAN, P_QN, P_KVN, P_QHN, P_QHR, P_KHN, P_KHR, P_FFNN, P_CW, P_CB = (
    0, 8, 24, 32, 35, 37, 38, 39, 40, 41, 57, 57 + 132)
NPRM = 57 + 132 + 44


def pack_params(inp):
    p = np.zeros((128, NPRM), np.float32)
    p[:, P_RETN:P_RETN + 8] = _col(inp['ret_norm'][0])
    p[:, P_RETGN:P_RETGN + 16] = _col(inp['ret_gn'][0].reshape(-1))
    p[:, P_MLAN:P_MLAN + 8] = _col(inp['mla_norm'][0])
    p[:, P_QN:P_QN + 3] = _col(inp['mla_q_norm'][0])
    p[:, P_KVN:P_KVN + 2] = _col(inp['mla_kv_norm'][0])
    p[:, P_QHN] = inp['mla_q_head_norm'][0][0:128]
    p[0:64, P_QHR] = inp['mla_q_head_norm'][0][128:192]
    p[:, P_KHN] = inp['mla_k_head_norm'][0][0:128]
    p[0:64, P_KHR] = inp['mla_k_head_norm'][0][128:192]
    for l in range(2):
        p[:, P_FFNN + 8 * l:P_FFNN + 8 * l + 8] = _col(inp['ffn_norm'][l])
        for k in range(3):
            p[:, P_CW + (l * 3 + k) * NF:P_CW + (l * 3 + k + 1) * NF] = _col(inp['ffn_conv_w'][l, k])
        p[:, P_CB + l * NF:P_CB + (l + 1) * NF] = _col(inp['ffn_conv_b'][l])
    return p


def const_tables():
    pos = np.arange(S, dtype=np.float64)
    inv = THETA ** (-np.arange(128, dtype=np.float64) / 128.0)
    ang = inv[:, None] * pos[None, :]
    rrope = np.stack([np.cos(ang), np.sin(ang)], axis=1).astype(np.float32)
    inv = THETA ** (-np.arange(32, dtype=np.float64) / 32.0)
    ang = np.concatenate([inv, inv])[:, None] * pos[None, :]
    mrope = np.zeros((128, 2, S), np.float32)
    mrope[0:64] = np.stack([np.cos(ang), np.sin(ang)], axis=1).astype(np.float32)
    dtab = np.zeros((4, 128, S), np.float64)
    p = np.arange(128)[:, None]
    m = np.arange(S)[None, :]
    for h in range(4):
        lg = np.log1p(-2.0 ** (-5.0 - h))
        t = np.exp(lg * (m - p).astype(np.float64))
        md = np.arange(128)[None, :]
        allowed = (p // 64) <= (md // 64)
        t[:, 0:128] = np.where(allowed, np.exp(lg * np.abs(md - p)), 0.0)
        dtab[h] = t * (256.0 ** -0.5)
    dtab = dtab.astype(np.float32)
    misc = np.zeros((128, 512), np.float32)
    misc[:, 0:128] = ((p // 64) <= (np.arange(128)[None, :] // 64)).astype(np.float32)
    for i in range(32):
        misc[32 + i, 128 + i] = -1.0
        misc[i, 128 + 32 + i] = 1.0
    misc[:, 256:384] = np.eye(128, dtype=np.float32)
    misc[:, 384:512] = np.where(misc[:, 0:128] > 0, 0.0, -30000.0)
    return rrope, mrope, dtab, misc


class Buf:
    __slots__ = ('name', 'w', 'r', 'dsem', 'dcnt')

    def __init__(self, name):
        self.name = name; self.w = None; self.r = {}; self.dsem = None; self.dcnt = 0


class KB:
    def __init__(self, nc, es):
        self.nc = nc; self.es = es
        self.eng = {'pe': nc.tensor, 'act': nc.scalar, 'dve': nc.vector, 'pool': nc.gpsimd, 'sp': nc.sync}
        self.semh = {}
        for e in self.eng:
            self.semh[e] = es.enter_context(nc.semaphore('sem_' + e))
        self.cnt = {e: 0 for e in self.eng}
        self.pend = {e: False for e in self.eng}
        self.seen = {e: {} for e in self.eng}
        self.bar = {}
        self.nops = 0

    def _waits(self, e, reads, writes):
        deps = {}

        def add(tok):
            if tok is None:
                return
            k, v = tok
            if deps.get(k, 0) < v:
                deps[k] = v
        for b in reads:
            add(b.w)
        for b in writes:
            add(b.w)
            for t in b.r.values():
                add(t)
        if e != 'pool':
            for k, v in self.bar.items():
                add((k, v))
        eng = self.eng[e]; seen = self.seen[e]
        for k, v in deps.items():
            if k == e and e == 'pe':
                continue
            if seen.get(k, 0) >= v:
                continue
            if k in self.cnt:
                assert v <= self.cnt[k], ('future dependency', e, k, v, self.cnt[k])
            eng.wait_ge(self.semh[k], v)
            seen[k] = v

    def op(self, e, fn, reads=(), writes=(), signal=True):
        self._waits(e, reads, writes)
        ins = fn(self.eng[e])
        self.nops += 1
        if signal:
            self.cnt[e] += 1
            ins.then_inc(self.semh[e], 1)
            tok = (e, self.cnt[e]); self.pend[e] = False
        else:
            tok = (e, self.cnt[e] + 1); self.pend[e] = True
        for b in reads:
            b.r[e] = tok
        for b in writes:
            b.w = tok; b.r = {}
        return ins

    def dma(self, e, out, in_, reads=(), writes=(), **kw):
        tgt = writes[0]
        if tgt.dsem is None:
            self.nsem = getattr(self, 'nsem', 0) + 1
            tgt.dsem = 'd%d_%s' % (self.nsem, tgt.name)
            self.semh[tgt.dsem] = self.es.enter_context(self.nc.semaphore(tgt.dsem))
        self._waits(e, reads, writes)
        ins = self.eng[e].dma_start(out=out, in_=in_, **kw)
        tgt.dcnt += 16
        ins.then_inc(self.semh[tgt.dsem], 16)
        tok = (tgt.dsem, tgt.dcnt)
        for b in reads:
            b.r[tgt.dsem] = tok
        for b in writes:
            b.w = tok; b.r = {}
        return ins

    def barrier(self):
        for e in ('pe', 'act', 'dve'):
            assert not self.pend[e], e
            self.bar[e] = self.cnt[e]

    def wait_all(self, e, bufs):
        self._waits(e, bufs, ())


class Defer:
    def __init__(self):
        self.q = []; self.t = 0; self.n = 0

    def add(self, delay, fn, tag=0):
        self.n += 1
        self.q.append((self.t + delay, self.n, tag, fn))

    def tick(self):
        self.t += 1
        due = sorted([x for x in self.q if x[0] <= self.t], key=lambda x: (x[0], x[1]))
        self.q = [x for x in self.q if x[0] > self.t]
        for x in due:
            x[3]()

    def flush(self, maxtag=None):
        while True:
            sel = sorted([x for x in self.q if maxtag is None or x[2] <= maxtag], key=lambda x: (x[0], x[1]))
            if not sel:
                return
            self.q = [x for x in self.q if not (maxtag is None or x[2] <= maxtag)]
            for x in sel:
                x[3]()


class Rot:
    def __init__(self, items):
        self.items = list(items); self.i = 0

    def __call__(self):
        x = self.items[self.i % len(self.items)]; self.i += 1
        return x


def build_program(nseq=2, layers=('ret', 'ffn0', 'mla', 'ffn1')):
    nc = bass.Bass("TRN2", target_bir_lowering=False)
    widx, wtot = weight_index()
    xin = nc.dram_tensor("xin", [nseq, 128, 8, S], F32, kind="ExternalInput").ap()
    wts = nc.dram_tensor("wts", [wtot], F32, kind="ExternalInput").ap()
    prm_d = nc.dram_tensor("prm", [128, NPRM], F32, kind="ExternalInput").ap()
    rrope_d = nc.dram_tensor("rrope", [128, 2, S], F32, kind="ExternalInput").ap()
    mrope_d = nc.dram_tensor("mrope", [128, 2, S], F32, kind="ExternalInput").ap()
    dtab_d = nc.dram_tensor("dtab", [4, 128, S], F32, kind="ExternalInput").ap()
    misc_d = nc.dram_tensor("misc", [128, 512], F32, kind="ExternalInput").ap()
    xout = nc.dram_tensor("xout", [nseq, 128, 8, S], F32, kind="ExternalOutput").ap()

    with ExitStack() as es:
        kb = KB(nc, es)

        uid = [0]

        def sb(name, shape, dt, stack=es):
            uid[0] += 1
            return stack.enter_context(nc.sbuf_tensor("%s_%d" % (name, uid[0]), shape, dt))

        xT = sb("xT", [128, 8, S], F32)
        xb = [Buf('x%d' % i) for i in range(4)]
        ob = [Buf('o%d' % i) for i in range(4)]
        ring = sb("ring", [128, RING, SLOT], BF16)
        ringb = [Buf('ring%d' % i) for i in range(RING)]
        prm = sb("prm_s", [128, NPRM], F32); prmb = Buf('prm')
        identb = sb("identb", [128, 128], BF16)
        negmb = sb("negmb", [128, 128], BF16)
        ones = sb("ones", [128, 128], BF16); constb = Buf('const')
        psum = es.enter_context(nc.psum_tensor("psum", [128, 8, 512], F32))
        pb = [Buf('ps%d' % i) for i in range(8)]

        kb.dma('sp', prm[:], prm_d, writes=[prmb])
        kb.op('dve', lambda e: e.memset(ones[:], 1.0), writes=[constb])
        rotb = sb("rotb", [128, 128], BF16)

        ring_i = [0]

        def wtile(name):
            off, (kc, cols) = widx[name]
            n = kc * cols
            slot = ring_i[0] % RING; ring_i[0] += 1
            src = wts[off:off + 128 * n].rearrange("(p n) -> p n", p=128)
            kb.dma('pool', ring[:, slot, 0:n], src, writes=[ringb[slot]], max_dma_last_dim=8192)
            return ring[:, slot, 0:n].rearrange("p (k c) -> p k c", k=kc), ringb[slot]

        @contextmanager
        def scope():
            with ExitStack() as st:
                yield st
                kb.barrier()

        def MM(out, lhsT, rhs, start, stop, reads, writes, signal=True):
            return kb.op('pe', lambda e: e.matmul(out, lhsT, rhs, start=start, stop=stop), reads, writes, signal)

        def ACT(out, in_, func, reads, writes, **kw):
            return kb.op('act', lambda e: e.activation(out=out, in_=in_, func=func, **kw), reads, writes)

        def TT(out, a, b, op, reads, writes):
            return kb.op('dve', lambda e: e.tensor_tensor(out=out, in0=a, in1=b, op=op), reads, writes)

        def STT(out, in0, scalar, in1, op0, op1, reads, writes):
            return kb.op('dve', lambda e: e.scalar_tensor_tensor(out=out, in0=in0, scalar=scalar, in1=in1,
                                                                 op0=op0, op1=op1), reads, writes)

        def RECIP(out, in_, reads, writes):
            return kb.op('dve', lambda e: e.reciprocal(out=out, in_=in_), reads, writes)

        def bank(i):
            return psum[:, i, :], pb[i]

        sqt = sb("sqt", [128, 2, 512], BF16); sqtb = [Buf('sqt0'), Buf('sqt1')]
        rstd = sb("rstd", [128, 512], F32); rstdb = Buf('rstd')

        lnb_t = sb("lnb", [128, 512], F32); lnbb = Buf('lnb')

        def rms_rstd(ps_ap, ps_buf, nfeat, npart=128, out=None, outb=None):
            if out is None:
                out, outb = rstd, rstdb
            ACT(lnb_t[0:npart, :], ps_ap, AF.Ln, [ps_buf], [lnbb], bias=EPS, scale=1.0 / nfeat)
            ACT(out[0:npart, :], lnb_t[0:npart, :], AF.Exp, [lnbb], [outb], scale=-0.5)

        def rmsnorm_tile(t0, gain_col, hT, hbuf, hcol0, pbank):
            tile_i = t0 // 512
            ps, psb = bank(pbank)
            for c in range(8):
                ACT(sqt[:, c % 2, :], xT[:, c, t0:t0 + 512], AF.Square, [xb[tile_i]], [sqtb[c % 2]])
                MM(ps, ones[:], sqt[:, c % 2, :], c == 0, c == 7, [sqtb[c % 2], constb], [psb])
            rms_rstd(ps, psb, D)
            for c in range(8):
                STT(hT[:, c, hcol0:hcol0 + 512], xT[:, c, t0:t0 + 512], prm[:, gain_col + c:gain_col + c + 1],
                    rstd[:], ALU.mult, ALU.mult, [xb[tile_i], prmb, rstdb], [hbuf])

        def ffn_layer(l, after_T0=None):
            with scope() as st:
                hT = sb("f_hT", [128, 8, S], BF16, st); hb = [[Buf('f_h%d_%d' % (i, c)) for c in range(8)] for i in range(4)]
                uT = sb("f_uT", [128, NF, 1024], BF16, st); ub = [Buf('f_u%d' % f) for f in range(NF)]
                halo = sb("f_halo", [128, 2, NF, 2], F32, st); halob = [[Buf('f_halo%d_%d' % (a, f)) for f in range(NF)] for a in range(2)]
                cbuf = sb("f_c", [128, 2, 512], F32, st); cb = [Buf('f_c0'), Buf('f_c1')]
                sbuf = sb("f_s", [128, 2, 512], F32, st); sbb = [Buf('f_s0'), Buf('f_s1')]
                sq8 = sb("f_sq8", [128, 8, 512], BF16, st); sq8b = Buf('f_sq8')
                rota = Rot([0, 1, 2]); rotg = Rot([3, 4, 5]); roty = Rot([6, 7])
                step = 0
                cw = lambda k, f: prm[:, P_CW + (l * 3 + k) * NF + f:P_CW + (l * 3 + k) * NF + f + 1]
                cbias = lambda f: prm[:, P_CB + l * NF + f:P_CB + l * NF + f + 1]
                gcol = P_FFNN + 8 * l

                def norm_a(ti, cs=range(8)):
                    for c in cs:
                        ACT(sq8[:, c, :], xT[:, c, ti * 512:(ti + 1) * 512], AF.Square, [xb[ti]], [sq8b])

                def norm_b(ti):
                    ps, psb = bank(6 + ti % 2)
                    for c in range(8):
                        MM(ps, ones[:], sq8[:, c, :], c == 0, c == 7, [sq8b, constb], [psb], signal=(c == 7))
                    rms_rstd(ps, psb, D)

                def norm_c(ti, cs=range(8)):
                    for c in cs:
                        STT(hT[:, c, ti * 512:(ti + 1) * 512], xT[:, c, ti * 512:(ti + 1) * 512], prm[:, gcol + c:gcol + c + 1],
                            rstd[:], ALU.mult, ALU.mult, [xb[ti], prmb, rstdb], [hb[ti][c]])

                fscr = sb("f_scr", [128, 2], F32, st); fscrb = Buf('f_scr')
                ACT(fscr[:, 0:1], ones[:, 0:1], AF.Ln, [constb], [fscrb])
                for ti in range(2):
                    ACT(sq8[:], xT[:, :, ti * 512:(ti + 1) * 512], AF.Square, [xb[ti]], [sq8b])
                    norm_b(ti); norm_c(ti)
                for T in range(2):
                    for f in range(NF):
                        if T == 0:
                            if f <= 7:
                                norm_a(2, [f])
                            if f == 8:
                                norm_b(2)
                            if 9 <= f <= 16:
                                norm_c(2, [f - 9]); norm_a(3, [f - 9])
                            if f == 17:
                                norm_b(3)
                            if f >= 18:
                                norm_c(3, [2 * (f - 18), 2 * (f - 18) + 1])
                        wt, wbuf = wtile('f%di%d' % (l, f))
                        for tt in range(2):
                            pa, pab = bank(rota()); pg, pgb = bank(rotg())
                            hs = slice(T * 1024 + tt * 512, T * 1024 + (tt + 1) * 512)
                            us = slice(tt * 512, (tt + 1) * 512)
                            hbi = hb[T * 2 + tt]
                            for c in range(8):
                                MM(pa, wt[:, c, 0:128], hT[:, c, hs], c == 0, c == 7, [wbuf, hbi[c]], [pab], signal=(c == 7))
                            for c in range(8):
                                MM(pg, wt[:, c, 128:256], hT[:, c, hs], c == 0, c == 7, [wbuf, hbi[c]], [pgb], signal=(c == 7))
                            k = step % 2; step += 1
                            cc = cbuf[:, k, :]
                            ACT(cc, pg, AF.Identity, [pgb, prmb], [cb[k]], scale=cw(2, f), bias=cbias(f))
                            hp = (T * 2 + tt) % 2
                            STT(cc[:, 1:512], pg[:, 0:511], cw(1, f), cc[:, 1:512], ALU.mult, ALU.add, [pgb, prmb, cb[k]], [cb[k]])
                            STT(cc[:, 2:512], pg[:, 0:510], cw(0, f), cc[:, 2:512], ALU.mult, ALU.add, [pgb, prmb, cb[k]], [cb[k]])
                            kb.op('dve', lambda e, hp=hp, f=f, pg=pg: e.tensor_copy(out=halo[:, hp, f, :], in_=pg[:, 510:512]),
                                  [pgb], [halob[hp][f]])
                            if not (T == 0 and tt == 0):
                                STT(cc[:, 0:1], halo[:, 1 - hp, f, 1:2], cw(1, f), cc[:, 0:1], ALU.mult, ALU.add, [halob[1 - hp][f], prmb, cb[k]], [cb[k]])
                                STT(cc[:, 0:2], halo[:, 1 - hp, f, 0:2], cw(0, f), cc[:, 0:2], ALU.mult, ALU.add, [halob[1 - hp][f], prmb, cb[k]], [cb[k]])
                            ACT(sbuf[:, k, :], cc, AF.Silu, [cb[k]], [sbb[k]])
                            TT(uT[:, f, us], sbuf[:, k, :], pa, ALU.mult, [sbb[k], pab], [ub[f]])
                    for d in range(8):
                        wa, wab = wtile('f%do%d_0' % (l, d)); wb_, wbb = wtile('f%do%d_1' % (l, d))
                        for tt in range(2):
                            py, pyb = bank(roty())
                            us = slice(tt * 512, (tt + 1) * 512)
                            ts = slice(T * 1024 + tt * 512, T * 1024 + (tt + 1) * 512)
                            for kk in range(NF):
                                w, wbf = (wa, wab) if kk < 11 else (wb_, wbb)
                                MM(py, w[:, kk % 11, :], uT[:, kk, us], kk == 0, kk == NF - 1, [wbf, ub[kk]], [pyb],
                                   signal=(kk == NF - 1))
                            xi = xb[T * 2 + tt]
                            TT(xT[:, d, ts], xT[:, d, ts], py, ALU.add, [xi, pyb], [xi])
                    if T == 0 and after_T0 is not None:
                        after_T0()

        def ret_layer():
            with scope() as st:
                hT = sb("r_hT", [128, 8, S], BF16, st); hb = [Buf('r_h%d' % i) for i in range(4)]
                rope = sb("r_rope", [128, 2, S], F32, st); ropeb = Buf('r_rope')
                dt_ = sb("r_dt", [128, S], F32, st); dtb = Buf('r_dt')
                qT = sb("r_q", [128, 2, S], BF16, st); qb = [Buf('r_q%d' % i) for i in range(4)]
                kT = sb("r_k", [128, 2, S], BF16, st); kbf = [Buf('r_k%d' % i) for i in range(4)]
                vt = sb("r_v", [128, 16, 512], BF16, st); vb = [Buf('r_v%d' % i) for i in range(16)]
                t12 = sb("r_t", [128, 2, 512], F32, st); tb_ = [Buf('r_t0'), Buf('r_t1')]
                P = sb("r_P", [128, 2, 512], BF16, st); Pb = [Buf('r_P0'), Buf('r_P1')]
                sg = sb("r_sg", [128, 2, 4, 512], BF16, st); sgb = [Buf('r_sg0'), Buf('r_sg1')]
                sqn = sb("r_sqn", [128, 4, 512], BF16, st); sqnv = [Buf('r_sqn%d' % i) for i in range(4)]
                wv_ = sb("r_w", [128, 512], F32, st); wvb = Buf('r_w')
                u = sb("r_u", [128, 2, 4, 512], BF16, st); ub = [Buf('r_u0'), Buf('r_u1')]
                dq = Defer()
                scr = sb("r_scr", [128, 2], F32, st); scrb = Buf('r_scr')
                kb.dma('sp', rope[:], rrope_d, writes=[ropeb])
                for tt in range(4):
                    rmsnorm_tile(tt * 512, P_RETN, hT, hb[tt], tt * 512, 6 + tt % 2)
                rotp = Rot([0, 1, 2, 3, 4, 5])
                rots = Rot([4, 5]); rotx = Rot([6, 7])
                pcnt = [0]
                gw = {}

                def gproj(h, it, par, groups, banks=None):
                    evs = []
                    if (h, it) not in gw:
                        gw.clear()
                        gw[(h, it)] = (wtile('rg%da' % h), wtile('rg%db' % h))
                    (wga, wgab), (wgb_, wgbb) = gw[(h, it)]
                    i0 = it * 512
                    for gc in groups:
                        w, wbf = (wga, wgab) if gc < 2 else (wgb_, wgbb)
                        bi = rotx() if banks is None else banks[gc]
                        pg, pgb = bank(bi)
                        for c in range(8):
                            MM(pg, w[:, c, (gc % 2) * 128:(gc % 2 + 1) * 128], hT[:, c, i0:i0 + 512], c == 0, c == 7,
                               [wbf, hb[it]], [pgb], signal=(c == 7))
                        evs.append(lambda pg=pg, pgb=pgb, gc=gc: ACT(sg[:, par, gc, :], pg, AF.Silu, [pgb], [sgb[par]]))
                    return evs

                tiles = [(h, it) for h in range(4) for it in range(4)]
                for h in range(4):
                    kb.dma('sp', dt_[:], dtab_d[h], writes=[dtb])
                    for nm, dst, dstb in (('rq%d' % h, qT, qb), ('rk%d' % h, kT, kbf)):
                        w, wbf = wtile(nm)
                        for tt in range(4):
                            ts = slice(tt * 512, (tt + 1) * 512)
                            p1, p1b = bank(rotp()); p2, p2b = bank(rotp())
                            for c in range(8):
                                MM(p1, w[:, c, 0:128], hT[:, c, ts], c == 0, c == 7, [wbf, hb[tt]], [p1b], signal=(c == 7))
                            for c in range(8):
                                MM(p2, w[:, c, 128:256], hT[:, c, ts], c == 0, c == 7, [wbf, hb[tt]], [p2b], signal=(c == 7))
                            cs = rope[:, 0, ts]; sn = rope[:, 1, ts]
                            TT(t12[:, 0, :], p1, cs, ALU.mult, [p1b, ropeb], [tb_[0]])
                            TT(t12[:, 1, :], p2, sn, ALU.mult, [p2b, ropeb], [tb_[1]])
                            TT(dst[:, 0, ts], t12[:, 0, :], t12[:, 1, :], ALU.subtract, [tb_[0], tb_[1]], [dstb[tt]])
                            TT(t12[:, 0, :], p2, cs, ALU.mult, [p2b, ropeb], [tb_[0]])
                            TT(t12[:, 1, :], p1, sn, ALU.mult, [p1b, ropeb], [tb_[1]])
                            TT(dst[:, 1, ts], t12[:, 0, :], t12[:, 1, :], ALU.add, [tb_[0], tb_[1]], [dstb[tt]])
                            dq.tick()
                    dq.flush()
                    wva, wvab = wtile('rv%da' % h); wvb_, wvbb = wtile('rv%db' % h)
                    for tb in range(16):
                        pv, pvb = bank(rotp())
                        for half, (w, wbf) in enumerate(((wva, wvab), (wvb_, wvbb))):
                            for c in range(8):
                                MM(pv[:, half * 256:(half + 1) * 256], hT[:, c, tb * 128:(tb + 1) * 128], w[:, c, :],
                                   c == 0, c == 7, [wbf, hb[tb // 4]], [pvb], signal=(c == 7))
                        ACT(vt[:, tb, :], pv, AF.Copy, [pvb], [vb[tb]])
                    if h == 0:
                        for gc in range(4):
                            for ev in gproj(0, 0, 0, [gc]):
                                ev()
                    for it in range(4):
                        g = h * 4 + it; par = g % 2
                        i0 = it * 512
                        njb = 4 * it + 4

                        def emit_S(jb):
                            c0 = max(jb * 128 - i0, 0); N = 512 - c0; ilo = i0 + c0
                            bi = rots()
                            ps, psb = bank(bi)
                            for c in range(2):
                                MM(ps[:, 0:N], kT[:, c, jb * 128:(jb + 1) * 128], qT[:, c, ilo:ilo + N], c == 0, c == 1,
                                   [kbf[jb // 4], qb[it]], [psb], signal=(c == 1))
                            return ps, psb

                        nxt = emit_S(0)
                        for jb in range(njb):
                            ps, psb = nxt
                            if jb + 1 < njb:
                                nxt = emit_S(jb + 1)
                            c0 = max(jb * 128 - i0, 0); N = 512 - c0; ilo = i0 + c0
                            k = pcnt[0] % 2; pcnt[0] += 1
                            doff = ilo - jb * 128
                            TT(P[:, k, 0:N], ps[:, 0:N], dt_[:, doff:doff + N], ALU.mult, [psb, dtb], [Pb[k]])
                            if jb == njb - 2:
                                ACT(scr[:, 0:1], ones[:, 0:1], AF.Ln, [constb], [scrb])
                            dq.tick()
                            for vc in range(4):
                                MM(psum[:, vc, c0:c0 + N], vt[:, jb, vc * 128:(vc + 1) * 128], P[:, k, 0:N], jb == 0, jb == njb - 1,
                                   [vb[jb], Pb[k]], [pb[vc]], signal=(vc == 3))
                        dq.flush(maxtag=g - 2)
                        nx = tiles[g + 1] if g + 1 < 16 else None
                        gb = {0: 6, 1: 4, 2: 5, 3: 6}
                        evs = gproj(nx[0], nx[1], 1 - par, [0], gb) if nx else []
                        pss, pssb = bank(7)
                        for vc in range(4):
                            ACT(sqn[:, vc, :], psum[:, vc, :], AF.Square, [pb[vc]], [sqnv[vc]])
                        for vc in range(4):
                            MM(pss, ones[:], sqn[:, vc, :], vc == 0, vc == 3, [sqnv[vc], constb], [pssb], signal=(vc == 3))
                        rms_rstd(pss, pssb, 512)
                        if nx:
                            evs += gproj(nx[0], nx[1], 1 - par, [1], gb)
                            evs += gproj(nx[0], nx[1], 1 - par, [2], gb)
                        for ev in evs:
                            ev()
                        for vc in range(4):
                            TT(wv_[:], sg[:, par, vc, :], rstd[:], ALU.mult, [sgb[par], rstdb], [wvb])
                            gcol = P_RETGN + h * 4 + vc
                            STT(u[:, par, vc, :], psum[:, vc, :], prm[:, gcol:gcol + 1], wv_[:], ALU.mult, ALU.mult,
                                [pb[vc], prmb, wvb], [ub[par]])
                        if nx:
                            for ev in gproj(nx[0], nx[1], 1 - par, [3], gb):
                                ev()
                        dq.tick()
                        wo = {}

                        def ychunk(d, h=h, i0=i0, it=it, par=par, wo=wo):
                            key = 'a' if d < 4 else 'b'
                            if key not in wo:
                                wo[key] = wtile('ro%d%s' % (h, key))
                            w, wbf = wo[key]
                            py, pyb = bank(rotx())
                            for vc in range(4):
                                MM(py, w[:, vc, (d % 4) * 128:(d % 4 + 1) * 128], u[:, par, vc, :], vc == 0, vc == 3,
                                   [wbf, ub[par]], [pyb], signal=(vc == 3))
                            TT(xT[:, d, i0:i0 + 512], xT[:, d, i0:i0 + 512], py, ALU.add, [xb[it], pyb], [xb[it]])
                        late = (4 * nx[1] + 4 + 1) if (nx and nx[0] == h) else 4
                        for d in range(8):
                            dq.add(1 + d // 2 if d < 6 else late, (lambda d=d, f=ychunk: f(d)), tag=g)
                dq.flush()

        def mla_layer():
            with scope() as so:
                cqn = sb("m_cqn", [128, 3, S], BF16, so); cqb = [Buf('m_cq%d' % i) for i in range(4)]
                ckvn = sb("m_ckvn", [128, 2, S], BF16, so); ckb = [Buf('m_ck%d' % i) for i in range(4)]
                KrT = sb("m_kr", [128, S], BF16, so); krb = [Buf('m_kr%d' % i) for i in range(4)]
                sqkr = sb("m_sqkr", [128, S], BF16, so); sqkrb = [Buf('m_sqkr%d' % i) for i in range(4)]
                mrope = sb("m_rope", [128, 2, S], F32, so); mropeb = Buf('m_rope')
                rstdk = sb("m_rstdk", [128, 16, 4], F32, so); rstdkb = Buf('m_rstdk')
                rtk = sb("m_rtk", [128, 16, 4], F32, so); rtkb = Buf('m_rtk')
                xg = sb("m_xg", [128, 512], BF16, so); xgb = Buf('m_xg')
                t12 = sb("m_t", [128, 2, 512], F32, so); tb_ = [Buf('m_t0'), Buf('m_t1')]
                kb.dma('sp', mrope[:], mrope_d, writes=[mropeb])

                def rope64_mm(src_ap, src_buf, pbank):
                    pr, prb = bank(pbank)
                    MM(pr, rotb[:], src_ap, True, True, [constb, src_buf], [prb])
                    return pr, prb

                def rope64_ew(src_ap, src_buf, pr, prb, dst_ap, dst_buf, ts):
                    TT(t12[:, 0, :], src_ap, mrope[:, 0, ts], ALU.mult, [src_buf, mropeb], [tb_[0]])
                    TT(t12[:, 1, :], pr, mrope[:, 1, ts], ALU.mult, [prb, mropeb], [tb_[1]])
                    TT(dst_ap, t12[:, 0, :], t12[:, 1, :], ALU.add, [tb_[0], tb_[1]], [dst_buf])

                with scope() as s1:
                    hT = sb("m_hT", [128, 8, S], BF16, s1); hb = [Buf('m_h%d' % i) for i in range(4)]
                    sq3 = sb("m_sq3", [128, 5, 512], BF16, s1); sq3b = Buf('m_sq3'); sq3c = Buf('m_sq3c')
                    rstd2 = sb("m_rstd2", [128, 512], F32, s1); rstd2b = Buf('m_rstd2')
                    for tt in range(4):
                        rmsnorm_tile(tt * 512, P_MLAN, hT, hb[tt], tt * 512, 6 + tt % 2)
                    wi = [wtile('mi0'), wtile('mi1'), wtile('mi2')]
                    for tt in range(4):
                        ts = slice(tt * 512, (tt + 1) * 512)
                        def proj(fcs):
                            for fc in fcs:
                                w, wbf = wi[fc // 2]; col = (fc % 2) * 128
                                for c in range(8):
                                    MM(psum[:, fc, :], w[:, c, col:col + 128], hT[:, c, ts], c == 0, c == 7, [wbf, hb[tt]], [pb[fc]],
                                       signal=(c == 7))
                        proj(range(0, 3))
                        ACT(sq3[:, 0:3, :], psum[:, 0:3, :], AF.Square, [pb[0], pb[1], pb[2]], [sq3b])
                        proj(range(3, 6))
                        ACT(sq3[:, 3:5, :], psum[:, 3:5, :], AF.Square, [pb[3], pb[4]], [sq3c])
                        pss, pssb = bank(6)
                        for k in range(3):
                            MM(pss, ones[:], sq3[:, k, :], k == 0, k == 2, [sq3b, constb], [pssb], signal=(k == 2))
                        rms_rstd(pss, pssb, 384)
                        ACT(sqkr[:, ts], psum[:, 5, :], AF.Square, [pb[5]], [sqkrb[tt]])
                        ACT(xg[:], psum[:, 5, :], AF.Identity, [pb[5], prmb], [xgb], scale=prm[:, P_KHR:P_KHR + 1])
                        for k in range(3):
                            STT(cqn[:, k, ts], psum[:, k, :], prm[:, P_QN + k:P_QN + k + 1], rstd[:], ALU.mult, ALU.mult,
                                [pb[k], prmb, rstdb], [cqb[tt]])
                        pss, pssb = bank(7)
                        for k in range(2):
                            MM(pss, ones[:], sq3[:, 3 + k, :], k == 0, k == 1, [sq3c, constb], [pssb], signal=(k == 1))
                        rms_rstd(pss, pssb, 256, out=rstd2, outb=rstd2b)
                        for k in range(2):
                            STT(ckvn[:, k, ts], psum[:, 3 + k, :], prm[:, P_KVN + k:P_KVN + k + 1], rstd2[:], ALU.mult, ALU.mult,
                                [pb[3 + k], prmb, rstd2b], [ckb[tt]])
                        pr, prb = rope64_mm(xg[:], xgb, 6)
                        rope64_ew(xg[:], xgb, pr, prb, KrT[:, ts], krb[tt], ts)

                for G in range(2):
                    with scope() as s2:
                        KnT = sb("m_kn", [128, 4, S], BF16, s2); knb = [[Buf('m_kn%d_%d' % (a, i)) for i in range(4)] for a in range(4)]
                        Vt = sb("m_v", [128, 16, 512], BF16, s2); vb = [Buf('m_v%d' % i) for i in range(16)]
                        QnT = sb("m_qn", [128, 2, S], BF16, s2); qnb = [[Buf('m_qn%d_%d' % (a, i)) for i in range(4)] for a in range(2)]
                        QrT = sb("m_qr", [128, 2, S], BF16, s2); qrb = [[Buf('m_qr%d_%d' % (a, i)) for i in range(4)] for a in range(2)]
                        onT = sb("m_on", [128, 2, S], BF16, s2); onb = [[Buf('m_on%d_%d' % (a, i)) for i in range(4)] for a in range(2)]
                        sqk = sb("m_sqk", [128, 2, 512], BF16, s2); sqkb = [Buf('m_sqk0'), Buf('m_sqk1')]
                        sq2 = sb("m_sq2", [128, 512], BF16, s2); sq2b = Buf('m_sq2')
                        P = sb("m_P", [128, 3, 512], BF16, s2); Pb = [Buf('m_P0'), Buf('m_P1'), Buf('m_P2')]
                        rden, rdenb = rstd, rstdb
                        rstdq = sb("m_rstdq", [128, 512], F32, s2); rstdqb = Buf('m_rstdq')
                        Osb = sb("m_osb", [128, 512], F32, s2); Osbb = Buf('m_osb')
                        lnd = sb("m_lnd", [128, 512], F32, s2); lndb = Buf('m_lnd')
                        sqq = sb("m_sqq", [128, 512], BF16, s2); sqqb = Buf('m_sqq')
                        dq = Defer()
                        def qchain(h, tt, qp):
                            ts = slice(tt * 512, (tt + 1) * 512)
                            wq, wqb_ = wtile('mq%d' % h); cb0 = 0
                            pqn, pqnb = bank(5); pqr, pqrb = bank(6)
                            for c in range(3):
                                MM(pqn, wq[:, c, cb0:cb0 + 128], cqn[:, c, ts], c == 0, c == 2, [wqb_, cqb[tt]], [pqnb], signal=(c == 2))
                            for c in range(3):
                                MM(pqr, wq[:, c, cb0 + 128:cb0 + 256], cqn[:, c, ts], c == 0, c == 2, [wqb_, cqb[tt]], [pqrb],
                                   signal=(c == 2))
                            ACT(sqq[:], pqn, AF.Square, [pqnb], [sqqb])
                            dq.add(1, lambda: ACT(sq2[:], pqr, AF.Square, [pqrb], [sq2b]))

                            pssh = [None]

                            def stB():
                                pss, pssb = bank(7)
                                MM(pss, ones[:], sqq[:], True, False, [sqqb, constb], [pssb])
                                MM(pss, ones[:], sq2[:], False, True, [sq2b, constb], [pssb])
                                ACT(lnb_t[:], pss, AF.Ln, [pssb], [lnbb], bias=EPS, scale=1.0 / 192)

                            def stB2():
                                ACT(rstdq[:], lnb_t[:], AF.Exp, [lnbb], [rstdqb], scale=-0.5)

                            def stC():
                                STT(QnT[:, qp, ts], pqn, prm[:, P_QHN:P_QHN + 1], rstdq[:], ALU.mult, ALU.mult,
                                    [pqnb, prmb, rstdqb], [qnb[qp][tt]])
                                STT(xg[:], pqr, prm[:, P_QHR:P_QHR + 1], rstdq[:], ALU.mult, ALU.mult,
                                    [pqrb, prmb, rstdqb], [xgb])

                            def stD():
                                pr, prb = rope64_mm(xg[:], xgb, 7)
                                dq.add(2, lambda: rope64_ew(xg[:], xgb, pr, prb, QrT[:, qp, ts], qrb[qp][tt], ts))
                            dq.add(3, stB)
                            dq.add(4, stB2)
                            dq.add(6, stC)
                            dq.add(7, stD)

                        rotp = Rot([2, 3, 4])
                        wkn, wknb = wtile('mkn')
                        pssk, psskb = bank(1)
                        nsq = 0
                        kticks = [0]

                        def ktick():
                            if kticks[0] % 8 == 0 and kticks[0] // 8 < 4:
                                qchain(4 * G, kticks[0] // 8, 0)
                            kticks[0] += 1
                            dq.tick()

                        prev = None
                        for q4 in range(4):
                            ACT(Osb[:, q4 * 128:(q4 + 1) * 128], ones[:], AF.Identity, [constb, prmb], [Osbb], scale=prm[:, P_KHN:P_KHN + 1])
                        for hl in range(4):
                            h = 4 * G + hl
                            for tt in range(4):
                                ts = slice(tt * 512, (tt + 1) * 512)
                                pk, pkb = bank(rotp())
                                for c in range(2):
                                    MM(pk, wkn[:, c, h * 128:(h + 1) * 128], ckvn[:, c, ts], c == 0, c == 1, [wknb, ckb[tt]], [pkb],
                                       signal=(c == 1))
                                k = nsq % 2; nsq += 1
                                ACT(sqk[:, k, :], pk, AF.Square, [pkb], [sqkb[k]])

                                def tiny(hl=hl, tt=tt, k=k, pk=pk, pkb=pkb, ts=ts):
                                    TT(KnT[:, hl, ts], pk, Osb[:], ALU.mult, [pkb, Osbb, sqkb[k]], [knb[hl][tt]])
                                    for b in range(4):
                                        tbk = tt * 4 + b
                                        col = tbk * 4 + hl
                                        MM(pssk[:, col:col + 1], sqk[:, k, b * 128:(b + 1) * 128], ones[:, 0:1], True, False,
                                           [sqkb[k], constb], [psskb])
                                        MM(pssk[:, col:col + 1], sqkr[:, tbk * 128:(tbk + 1) * 128], ones[:, 0:1], False, True,
                                           [sqkrb[tt], constb], [psskb])
                                if prev is not None:
                                    prev()
                                prev = tiny
                                ktick()
                        prev()
                        f2 = lambda a: a[:].rearrange("p a b -> p (a b)")
                        ACT(f2(rtk), pssk[:, 0:64], AF.Ln, [psskb], [rtkb], bias=EPS, scale=1.0 / 192)
                        ACT(f2(rstdk), f2(rtk), AF.Exp, [rtkb], [rstdkb], scale=-0.5)
                        kb.op('dve', lambda e: e.tensor_scalar(out=f2(rstdk), in0=f2(rstdk),
                                                               scalar1=float(192.0 ** -0.5), scalar2=None, op0=ALU.mult),
                              [rstdkb], [rstdkb])
                        wv, wvb = wtile('mv')
                        for tbk in range(16):
                            pv, pvb = bank(rotp())
                            for c in range(2):
                                MM(pv, ckvn[:, c, tbk * 128:(tbk + 1) * 128], wv[:, c, G * 512:(G + 1) * 512], c == 0, c == 1,
                                   [wvb, ckb[tbk // 4]], [pvb], signal=(c == 1))
                            ACT(Vt[:, tbk, :], pv, AF.Copy, [pvb], [vb[tbk]])
                            ktick()

                        dq.flush()
                        rots = Rot([2, 3, 4]); roty = Rot([5, 6, 7])
                        pcnt = 0
                        for hl in range(4):
                            h = 4 * G + hl; qp = hl % 2
                            hstep = 0
                            for it in range(4):
                                i0 = it * 512
                                njb = 4 * it + 4

                                def emit_S(jb, i0=i0, it=it):
                                    c0 = max(jb * 128 - i0, 0); N = 512 - c0; ilo = i0 + c0
                                    ps, psb = bank(rots())
                                    MM(ps[:, 0:N], KnT[:, hl, jb * 128:(jb + 1) * 128], QnT[:, qp, ilo:ilo + N], True, False,
                                       [knb[hl][jb // 4], qnb[qp][it]], [psb], signal=False)
                                    diag = jb >= 4 * it
                                    MM(ps[:, 0:N], KrT[:, jb * 128:(jb + 1) * 128], QrT[:, qp, ilo:ilo + N], False, not diag,
                                       [krb[jb // 4], qrb[qp][it]], [psb], signal=not diag)
                                    if diag:
                                        MM(ps[:, 0:128], identb[:], negmb[:], False, True, [constb], [psb])
                                    return ps, psb
                                pend = [emit_S(0), emit_S(1)]
                                for jb in range(njb):
                                    ps, psb = pend.pop(0)
                                    if jb + 2 < njb:
                                        pend.append(emit_S(jb + 2))
                                    c0 = max(jb * 128 - i0, 0); N = 512 - c0
                                    k = pcnt % 3; pcnt += 1
                                    ACT(P[:, k, 0:N], ps[:, 0:N], AF.Exp, [psb, rstdkb], [Pb[k]], scale=rstdk[:, jb, hl:hl + 1])
                                    if hl + 1 < 4 and hstep in (4, 14, 24, 34):
                                        qchain(h + 1, (hstep - 4) // 10, 1 - qp)
                                    hstep += 1
                                    dq.tick()
                                    MM(psum[:, 0, c0:c0 + N], Vt[:, jb, hl * 128:(hl + 1) * 128], P[:, k, 0:N], jb == 0, jb == njb - 1,
                                       [vb[jb], Pb[k]], [pb[0]], signal=False)
                                    MM(psum[:, 1, c0:c0 + N], ones[:], P[:, k, 0:N], jb == 0, jb == njb - 1, [constb, Pb[k]], [pb[1]])
                                kb.op('dve', lambda e: e.tensor_copy(out=Osb[:], in_=psum[:, 0, :]), [pb[0]], [Osbb])
                                ACT(lnd[:], psum[:, 1, :], AF.Ln, [pb[1]], [lndb])

                                def fin(qp=qp, i0=i0, it=it):
                                    ACT(rden[:], lnd[:], AF.Exp, [lndb], [rdenb], scale=-1.0)
                                    TT(onT[:, qp, i0:i0 + 512], Osb[:], rden[:], ALU.mult, [Osbb, rdenb], [onb[qp][it]])
                                dq.add(1, fin)
                            dq.flush()
                            if hl % 2 == 1:
                                wo0 = wtile('mo%d' % (h - 1)); wo1 = wtile('mo%d' % h)
                                for it in range(4):
                                    i0 = it * 512
                                    for d in range(8):
                                        py, pyb = bank(roty())
                                        MM(py, wo0[0][:, 0, d * 128:(d + 1) * 128], onT[:, 0, i0:i0 + 512], True, False,
                                           [wo0[1], onb[0][it]], [pyb], signal=False)
                                        MM(py, wo1[0][:, 0, d * 128:(d + 1) * 128], onT[:, 1, i0:i0 + 512], False, True,
                                           [wo1[1], onb[1][it]], [pyb])
                                        TT(xT[:, d, i0:i0 + 512], xT[:, d, i0:i0 + 512], py, ALU.add, [xb[it], pyb], [xb[it]])

        tmpst = ExitStack()
        miscf = sb("miscf", [128, 512], F32, tmpst); miscb = Buf('misc')
        kb.dma('sp', miscf[:], misc_d, writes=[miscb])
        kb.op('dve', lambda e: e.tensor_copy(out=identb[:], in_=miscf[:, 256:384]), reads=[miscb], writes=[constb])
        kb.op('dve', lambda e: e.tensor_copy(out=negmb[:], in_=miscf[:, 384:512]), reads=[miscb], writes=[constb])
        kb.op('dve', lambda e: e.tensor_copy(out=rotb[:], in_=miscf[:, 128:256]), reads=[miscb], writes=[constb])
        kb.barrier()
        tmpst.close()

        def x_load(s, tt):
            ts = slice(tt * 512, (tt + 1) * 512)
            kb.dma('sp', xT[:, :, ts], xin[s][:, :, ts], writes=[xb[tt]])

        def x_store(s, tt):
            ts = slice(tt * 512, (tt + 1) * 512)
            kb.dma('sp', xout[s][:, :, ts], xT[:, :, ts], reads=[xb[tt]], writes=[ob[tt]])

        preloaded = set()
        for s in range(nseq):
            for tt in range(4):
                if (s, tt) not in preloaded:
                    x_load(s, tt)
            stored = set()

            def early(s=s, stored=stored):
                for tt in (0, 1):
                    x_store(s, tt); stored.add(tt)
                if s + 1 < nseq:
                    for tt in (0, 1):
                        x_load(s + 1, tt); preloaded.add((s + 1, tt))
            for li, ly in enumerate(layers):
                cb_ = early if (li == len(layers) - 1) else None
                if ly == 'ret':
                    ret_layer()
                elif ly == 'mla':
                    mla_layer()
                elif ly == 'ffn0':
                    ffn_layer(0, cb_)
                elif ly == 'ffn1':
                    ffn_layer(1, cb_)
            for tt in range(4):
                if tt not in stored:
                    x_store(s, tt)
        kb.wait_all('sp', ob)
        kb.wait_all('act', ob)
    return nc


_PROG = {}


def _get_prog(nseq, layers):
    key = (nseq, tuple(layers))
    if key not in _PROG:
        _PROG[key] = build_program(nseq, layers)
    return _PROG[key]


def kernel(**inp):
    inp = {k: np.asarray(v) for k, v in inp.items()}
    x = inp['x'].astype(np.float32, copy=False)
    B = x.shape[0]
    nseq = B // NC8
    xl = np.ascontiguousarray(x.reshape(B, S, 8, 128).transpose(0, 3, 2, 1))
    wts = pack_weights(inp)
    prm = pack_params(inp)
    rrope, mrope, dtab, misc = const_tables()
    nc = _get_prog(nseq, ('ret', 'ffn0', 'mla', 'ffn1'))
    in_maps = []
    for c in range(NC8):
        in_maps.append({"xin": xl[c * nseq:(c + 1) * nseq], "wts": wts, "prm": prm, "rrope": rrope,
                        "mrope": mrope, "dtab": dtab, "misc": misc})
    res = run_bass_kernel_spmd(nc, in_maps, core_ids=list(range(NC8)))
    outs = [np.asarray(r["xout"]) for r in res.results]
    o = np.concatenate(outs, axis=0)
    return np.ascontiguousarray(o.transpose(0, 3, 2, 1).reshape(B, S, D)).astype(np.float32, copy=False)
```
